# Optimizing a Trainium2 kernel written in Bass

```python
import jax, jax.numpy as jnp
from jax import lax
import numpy as np

D_MODEL = 1024
BATCH = 8
SEQ = 4096
DEPTH = 2

HEAD_DIM = 64
BLOCK_Q = 128
ROPE_THETA = 10000.0
EPS = 1e-6
N_BRANCH = 3
FOX_HEADS = 8
FOX_WIDTH = FOX_HEADS * HEAD_DIM
DSA_HEADS = 8
DSA_WIDTH = DSA_HEADS * HEAD_DIM
DSA_KV_RANK = 128
IDX_HEADS = 8
IDX_DIM = 64
DSA_TOPK_MAX = 256
SB_HEADS = 8
SB_WIDTH = SB_HEADS * HEAD_DIM

IN_SIZES = (FOX_WIDTH, FOX_WIDTH, FOX_WIDTH, FOX_HEADS, FOX_WIDTH,
            DSA_WIDTH, DSA_KV_RANK, IDX_HEADS * IDX_DIM, IDX_DIM, IDX_HEADS, DSA_WIDTH,
            SB_WIDTH, SB_WIDTH, SB_WIDTH, SB_WIDTH,
            N_BRANCH * D_MODEL)
N_IN = sum(IN_SIZES)

kernel_name = "hybrid_fox_dsa_stickbreak_gated_block"


def rms_norm(x, g):
    x32 = x.astype(jnp.float32)
    y = x32 * lax.rsqrt(jnp.mean(x32 * x32, axis=-1, keepdims=True) + EPS)
    return (y * g.astype(jnp.float32)).astype(x.dtype)


def rope(x, positions):
    half = x.shape[-1] // 2
    inv_freq = ROPE_THETA ** (-jnp.arange(half, dtype=jnp.float32) / half)
    ang = positions.astype(jnp.float32)[..., None] * inv_freq
    if x.ndim == 4:
        ang = ang[:, :, None, :]
    cos, sin = jnp.cos(ang), jnp.sin(ang)
    x32 = x.astype(jnp.float32)
    x1, x2 = x32[..., :half], x32[..., half:]
    return jnp.concatenate([x1 * cos - x2 * sin, x2 * cos + x1 * sin], axis=-1).astype(x.dtype)


def fox_attention(q, k, v, log_f):
    S, d = q.shape[1], q.shape[3]
    scale = d ** -0.5
    F = jnp.cumsum(log_f, axis=1).transpose(0, 2, 1)
    outs = []
    for i in range(S // BLOCK_Q):
        q0, q1 = i * BLOCK_Q, (i + 1) * BLOCK_Q
        s = jnp.einsum('bqhd,bkhd->bhqk', q[:, q0:q1], k[:, :q1]).astype(jnp.float32) * scale
        bias = F[:, :, q0:q1, None] - F[:, :, None, :q1]
        mask = jnp.arange(q0, q1)[:, None] >= jnp.arange(q1)[None, :]
        p = jax.nn.softmax(jnp.where(mask, s + bias, -jnp.inf), axis=-1)
        outs.append(jnp.einsum('bhqk,bkhd->bqhd', p.astype(v.dtype), v[:, :q1]))
    return jnp.concatenate(outs, axis=1)


def dsa_attention(q, k, v, iq, ik, iw, k_top):
    S, d = q.shape[1], q.shape[3]
    scale = d ** -0.5
    idx_scale = IDX_DIM ** -0.5
    gather = jax.vmap(lambda a, i: a[i])
    outs = []
    for i in range(S // BLOCK_Q):
        q0, q1 = i * BLOCK_Q, (i + 1) * BLOCK_Q
        kend = max(q1, k_top)
        qpos = jnp.arange(q0, q1)
        si = jnp.einsum('bqhd,bkd->bqhk', iq[:, q0:q1], ik[:, :kend]).astype(jnp.float32) * idx_scale
        score = jnp.einsum('bqhk,bqh->bqk', jax.nn.relu(si), iw[:, q0:q1].astype(jnp.float32))
        score = jnp.where(jnp.arange(kend)[None, None, :] <= qpos[None, :, None], score, -jnp.inf)
        _, sel = lax.top_k(score, k_top)
        valid = sel <= qpos[None, :, None]
        k_sel = gather(k, sel)
        v_sel = gather(v, sel)
        s = jnp.einsum('bqhd,bqkd->bqhk', q[:, q0:q1], k_sel).astype(jnp.float32) * scale
        p = jax.nn.softmax(jnp.where(valid[:, :, None, :], s, -jnp.inf), axis=-1)
        outs.append(jnp.einsum('bqhk,bqkd->bqhd', p.astype(v.dtype), v_sel))
    return jnp.concatenate(outs, axis=1)


def stick_breaking_attention(q, k, v):
    S, d = q.shape[1], q.shape[3]
    scale = d ** -0.5
    outs = []
    for i in range(S // BLOCK_Q):
        q0, q1 = i * BLOCK_Q, (i + 1) * BLOCK_Q
        z = jnp.einsum('bqhd,bkhd->bhqk', q[:, q0:q1], k[:, :q1]).astype(jnp.float32) * scale
        mask = jnp.arange(q0, q1)[:, None] > jnp.arange(q1)[None, :]
        log_1m = jnp.where(mask, jax.nn.log_sigmoid(-z), 0.0)
        between = lax.cumsum(log_1m, axis=3, reverse=True) - log_1m
        a = jnp.where(mask, jnp.exp(jax.nn.log_sigmoid(z) + between), 0.0)
        outs.append(jnp.einsum('bhqk,bkhd->bqhd', a.astype(v.dtype), v[:, :q1]))
    return jnp.concatenate(outs, axis=1)


def mixer_layer(x, c, positions, w_ada, b_ada, g_norm, w_in, b_fgt, g_kv, w_kv_up,
                w_br_fox, w_br_dsa, w_br_sb, w_out):
    B, S, _ = x.shape
    k_top = min(DSA_TOPK_MAX, S // 4)
    mod = jax.nn.silu(c) @ w_ada + b_ada
    shift, scale, gate = jnp.split(mod, 3, axis=-1)
    h = rms_norm(x, g_norm) * (1.0 + scale[:, None, :]) + shift[:, None, :]
    z = h @ w_in
    points = [int(p) for p in np.cumsum(IN_SIZES)[:-1]]
    (fq, fk, fv, ff, fg,
     dq, dckv, diq, dik, diw, dg,
     sq, sk, sv, sg, merge) = jnp.split(z, points, axis=-1)
    heads = lambda t, n: t.reshape(B, S, n, HEAD_DIM)

    log_f = jax.nn.log_sigmoid((ff + b_fgt).astype(jnp.float32))
    y_fox = fox_attention(heads(fq, FOX_HEADS), heads(fk, FOX_HEADS), heads(fv, FOX_HEADS), log_f)
    y_fox = y_fox.reshape(B, S, FOX_WIDTH) * jax.nn.silu(fg)

    kv = rms_norm(dckv, g_kv) @ w_kv_up
    dk, dv = kv[..., :HEAD_DIM], kv[..., HEAD_DIM:]
    q_d = rope(heads(dq, DSA_HEADS), positions)
    dk = rope(dk, positions)
    iq = rope(diq.reshape(B, S, IDX_HEADS, IDX_DIM), positions)
    ik = rope(dik, positions)
    iw = diw * (IDX_HEADS ** -0.5)
    y_dsa = dsa_attention(q_d, dk, dv, iq, ik, iw, k_top)
    y_dsa = y_dsa.reshape(B, S, DSA_WIDTH) * jax.nn.silu(dg)

    y_sb = stick_breaking_attention(heads(sq, SB_HEADS), heads(sk, SB_HEADS), heads(sv, SB_HEADS))
    y_sb = y_sb.reshape(B, S, SB_WIDTH) * jax.nn.silu(sg)

    m_fox, m_dsa, m_sb = jnp.split(jax.nn.sigmoid(merge), N_BRANCH, axis=-1)
    mixed = m_fox * (y_fox @ w_br_fox) + m_dsa * (y_dsa @ w_br_dsa) + m_sb * (y_sb @ w_br_sb)
    return x + gate[:, None, :] * (mixed @ w_out)


def setup_inputs(seed: int = 0) -> dict:
    key = jax.random.key(seed)
    ks = jax.random.split(key, 16)
    nrm = lambda k, shape, s: jax.random.normal(k, shape, jnp.float32) * s
    D = D_MODEL
    x = nrm(ks[0], (BATCH, SEQ, D), 1.0)
    c = nrm(ks[1], (BATCH, D), 1.0)
    offsets = jax.random.randint(ks[2], (BATCH, 1), 0, SEQ, dtype=jnp.int32)
    positions = jnp.arange(SEQ, dtype=jnp.int32)[None, :] + offsets
    return {
        "x": x,
        "c": c,
        "positions": positions,
        "w_ada": nrm(ks[3], (DEPTH, D, 3 * D), 0.5 * D ** -0.5),
        "b_ada": nrm(ks[4], (DEPTH, 3 * D), 0.02),
        "g_norm": 1.0 + nrm(ks[5], (DEPTH, D), 0.02),
        "w_in": nrm(ks[6], (DEPTH, D, N_IN), D ** -0.5),
        "b_fgt": 2.0 + nrm(ks[7], (DEPTH, FOX_HEADS), 0.5),
        "g_kv": 1.0 + nrm(ks[8], (DEPTH, DSA_KV_RANK), 0.02),
        "w_kv_up": nrm(ks[9], (DEPTH, DSA_KV_RANK, 2 * HEAD_DIM), DSA_KV_RANK ** -0.5),
        "w_br_fox": nrm(ks[10], (DEPTH, FOX_WIDTH, D), FOX_WIDTH ** -0.5),
        "w_br_dsa": nrm(ks[11], (DEPTH, DSA_WIDTH, D), DSA_WIDTH ** -0.5),
        "w_br_sb": nrm(ks[12], (DEPTH, SB_WIDTH, D), SB_WIDTH ** -0.5),
        "w_out": nrm(ks[13], (DEPTH, D, D), D ** -0.5),
        "g_final": 1.0 + nrm(ks[14], (D,), 0.02),
    }


def reference(x, c, positions, w_ada, b_ada, g_norm, w_in, b_fgt, g_kv, w_kv_up,
              w_br_fox, w_br_dsa, w_br_sb, w_out, g_final):
    for l in range(DEPTH):
        x = mixer_layer(x, c, positions, w_ada[l], b_ada[l], g_norm[l], w_in[l], b_fgt[l],
                        g_kv[l], w_kv_up[l], w_br_fox[l], w_br_dsa[l], w_br_sb[l], w_out[l])
    return rms_norm(x, g_final)
```

```python
import numpy as np
from contextlib import ExitStack
import concourse.bass as bass
import concourse.mybir as mybir
from concourse.bass_utils import run_bass_kernel_spmd

F32 = mybir.dt.float32
BF16 = mybir.dt.bfloat16
FP16 = mybir.dt.float16
I32 = mybir.dt.int32
AF = mybir.ActivationFunctionType
ALU = mybir.AluOpType

S = 4096
D = 1024
NT = 32
P = 128
DEPTH = 2
N_IN = 8912
OFF = dict(fq=0, fk=512, fv=1024, ff=1536, fg=1544, dq=2056, dckv=2568, diq=2696, dik=3208,
           diw=3272, dg=3280, sq=3792, sk=4304, sv=4816, sg=5328, merge=5840)
NEG = -30000.0
NIT = 14
import os
DSA_STAGES = int(os.environ.get('DSA_STAGES', '3'))
EPS = 1e-6
TWO_PI = 2.0 * np.pi
C1 = 6.28125
C2 = float(TWO_PI - 6.28125)


class Buf:
    __slots__ = ("name", "w", "r", "sem", "cnt", "loose", "q")

    def __init__(self, name, loose=False):
        self.name = name
        self.loose = loose
        self.w = {}
        self.r = {}
        self.sem = None
        self.cnt = 0


class Eng:
    def __init__(self, name, eng, sem, self_sync):
        self.name = name
        self.eng = eng
        self.sem = sem
        self.cnt = 0
        self.waited = {}
        self.self_sync = self_sync


class KB:
    def __init__(self, nc, es):
        self.nc = nc
        self.es = es
        self.nsem = 0
        self.E = {}
        for n, e, ss in [("pe", nc.tensor, False), ("act", nc.scalar, True), ("dve", nc.vector, True),
                         ("pool", nc.gpsimd, True), ("sp", nc.sync, True)]:
            self.E[n] = Eng(n, e, self.new_sem("e_" + n), ss)
        self.slots = {}
        self.all_slots = []
        self.free_sems = {"sp": [], "pool": [], "act": []}

    def new_sem(self, name):
        self.nsem += 1
        return self.es.enter_context(self.nc.semaphore("%s_%d" % (name, self.nsem)))

    def _wait(self, E, deps):
        for sid, (sem, val) in deps.items():
            slot = self.slots.get(sid)
            if slot is not None:
                val = slot.cnt
            elif sem is E.sem and not E.self_sync:
                continue
            if E.waited.get(sid, 0) >= val:
                continue
            E.eng.wait_ge(sem, val)
            E.waited[sid] = val

    @staticmethod
    def _merge(d, src):
        for sid, ev in src.items():
            o = d.get(sid)
            if o is None or o[1] < ev[1]:
                d[sid] = ev

    def _deps(self, reads, writes, add, own_sid=None):
        deps = {}
        for b in reads:
            self._merge(deps, b.w)
        for b in writes:
            self._merge(deps, b.w)
            self._merge(deps, b.r)
        for b in add:
            self._merge(deps, b.r)
            if not b.loose:
                for sid, ev in b.w.items():
                    if sid != own_sid:
                        self._merge(deps, {sid: ev})
        return deps

    def _commit(self, ev, reads, writes, add):
        sid = id(ev[0])
        for b in writes:
            b.w = {sid: ev}
            b.r = {}
        for b in add:
            b.w[sid] = ev
        for b in reads:
            b.r[sid] = ev

    def op(self, e, fn, reads=(), writes=(), add=()):
        E = self.E[e]
        self._wait(E, self._deps(reads, writes, add, id(E.sem)))
        inst = fn(E.eng)
        E.cnt += 1
        inst.then_inc(E.sem, 1)
        self._commit((E.sem, E.cnt), reads, writes, add)

    def dma(self, q, out, in_, slot, reads=(), writes=(), add=()):
        E = self.E[q]
        if slot.sem is None:
            if self.free_sems[q]:
                slot.sem, slot.cnt = self.free_sems[q].pop()
            else:
                slot.sem, slot.cnt = self.new_sem("d" + q), 0
            slot.q = q
            self.slots[id(slot.sem)] = slot
            self.all_slots.append(slot)
        assert slot.q == q, "DMA slot %s used from two queues" % slot.name
        self._wait(E, self._deps(reads, writes, add, id(slot.sem)))
        inst = E.eng.dma_start(out=out, in_=in_)
        slot.cnt += 16
        inst.then_inc(slot.sem, 16)
        self._commit((slot.sem, slot.cnt), reads, writes, add)

    def barrier(self, engines=("pe", "act", "dve", "pool", "sp")):
        deps = {}
        for n, X in self.E.items():
            if X.cnt > 0:
                deps[id(X.sem)] = (X.sem, X.cnt)
        for s in self.all_slots:
            if s.cnt > 0:
                deps[id(s.sem)] = (s.sem, s.cnt)
        for n in engines:
            E = self.E[n]
            ss = E.self_sync
            E.self_sync = True
            self._wait(E, deps)
            E.self_sync = ss
        if len(engines) == 5:
            for sl in self.all_slots:
                self.free_sems[sl.q].append((sl.sem, sl.cnt))
                del self.slots[id(sl.sem)]
                sl.sem = None
            self.all_slots = []


def act_kwargs(**kw):
    return {k: v for k, v in kw.items() if v is not None}


def build_program(nlayers=DEPTH, debug=False, stop_after=None):
    nc = bass.Bass("TRN2", target_bir_lowering=False)
    dt_in = lambda name, shape, dt=F32: nc.dram_tensor(name, list(shape), dt, kind="ExternalInput").ap()
    skind = "ExternalOutput" if debug else "Internal"
    dt_sc = lambda name, shape, dt=BF16: nc.dram_tensor(name, list(shape), dt, kind=skind).ap()

    x_in = dt_in("x", [S, D])
    crep_in = dt_in("crep", [P, 8, P])
    pos_in = dt_in("pos", [1, S], I32)
    invf_in = dt_in("invf", [P, 2])
    w_ada = dt_in("w_ada", [DEPTH, D, 3 * D])
    b_ada = dt_in("b_ada", [DEPTH, 3 * D])
    g_norm = dt_in("g_norm", [DEPTH, D])
    w_in = dt_in("w_in", [DEPTH, D, N_IN])
    w_rot = dt_in("w_rot", [DEPTH, D, 1088])
    b_fgt = dt_in("b_fgt", [DEPTH, 8, 1])
    g_kv = dt_in("g_kv", [DEPTH, P, 1])
    w_kv = dt_in("w_kv", [DEPTH, P, P])
    w_kvr = dt_in("w_kvr", [DEPTH, P, 64])
    w_brs = [dt_in("w_br_fox", [DEPTH, 512, D]), dt_in("w_br_dsa", [DEPTH, 512, D]), dt_in("w_br_sb", [DEPTH, 512, D])]
    w_out = dt_in("w_out", [DEPTH, D, D])
    g_final = dt_in("g_final", [1, D])
    out_d = nc.dram_tensor("out", [S, D], F32, kind="ExternalOutput").ap()

    XR = dt_sc("XR", [S, D], F32)
    QF = dt_sc("QF", [8, 68, S]); KF = dt_sc("KF", [8, 68, S]); VF = dt_sc("VF", [S, 8, 65]); GF = dt_sc("GF", [S, 512])
    QD = dt_sc("QD", [64, 8, S]); KD = dt_sc("KD", [64, S]); VD = dt_sc("VD", [S, 65]); GD = dt_sc("GD", [S, 512])
    IQ = dt_sc("IQ", [64, 8, S], FP16); IK = dt_sc("IK", [64, S], FP16)
    QS = dt_sc("QS", [8, 64, S]); KS = dt_sc("KS", [8, 64, S]); VS = dt_sc("VS", [S, 512]); GS = dt_sc("GS", [S, 512])
    MG = dt_sc("MG", [S, 3 * D])
    YF = dt_sc("YF", [S, 512]); YD = dt_sc("YD", [S, 512]); YS = dt_sc("YS", [S, 512])
    dbg = {}
    if debug:
        dbg["WAB"] = dt_sc("WAB", [P, NT, 8], F32)
        dbg["WSG"] = dt_sc("WSG", [P, NT, 8], F32)
        dbg["HT"] = dt_sc("HT", [P, 8, S], BF16)
        dbg["TAB"] = dt_sc("TAB", [P, 2, S], F32)
        dbg["SC"] = dt_sc("SC", [S, S], F32)
        dbg["TH"] = dt_sc("TH", [S, 1], F32)

    es = ExitStack()
    with es:
        K = KB(nc, es)
        uid = [0]

        def sbt(ctx, n, s, d):
            uid[0] += 1
            return ctx.enter_context(nc.sbuf_tensor("%s_%d" % (n, uid[0]), list(s), d))

        psum = es.enter_context(nc.psum_tensor("psum", [P, 8, 512], F32))
        PB = [Buf("pb%d" % i) for i in range(8)]
        ident = sbt(es, "ident", [P, P], BF16); b_ident = Buf("ident")
        tri_le = sbt(es, "tri_le", [P, P], BF16)
        tri_lt = sbt(es, "tri_lt", [P, P], BF16)
        negU = sbt(es, "negU", [P, P], BF16)
        negones = sbt(es, "negones", [P, P], BF16)
        ones_bf = sbt(es, "ones_bf", [P, P], BF16)
        caus = sbt(es, "caus", [P, P], F32)
        pow2 = sbt(es, "pow2", [P, NIT], F32)
        invf = sbt(es, "invf_sb", [P, 2], F32)
        CT = sbt(es, "CT", [P, S], F32)
        SS = sbt(es, "SS", [P, S], F32)
        b_const = Buf("const")
        b_tab = Buf("tab")
        A_rep = sbt(es, "A_rep", [P, D], F32); shift_rep = sbt(es, "shift_rep", [P, D], F32)
        gate_rep = sbt(es, "gate_rep", [P, D], F32)
        b_mod = Buf("mod")
        wab = sbt(es, "wab", [P, NT, 8], F32); wsg = sbt(es, "wsg", [P, NT, 8], F32)
        b_wab = Buf("wab")

        def pool_op(fn, writes):
            K.op("pool", fn, (), writes)
        pool_op(lambda e: e.memset(ident[:], 1.0), [b_const])
        pool_op(lambda e: e.affine_select(out=ident[:], in_=ident[:], pattern=[[1, P]], compare_op=ALU.is_equal,
                                          fill=0.0, base=0, channel_multiplier=-1), [b_const])
        pool_op(lambda e: e.memset(tri_le[:], 0.0), [b_const])
        pool_op(lambda e: e.affine_select(out=tri_le[:], in_=tri_le[:], pattern=[[1, P]], compare_op=ALU.is_ge,
                                          fill=NEG, base=0, channel_multiplier=-1), [b_const])
        pool_op(lambda e: e.memset(tri_lt[:], 0.0), [b_const])
        pool_op(lambda e: e.affine_select(out=tri_lt[:], in_=tri_lt[:], pattern=[[1, P]], compare_op=ALU.is_ge,
                                          fill=NEG, base=-1, channel_multiplier=-1), [b_const])
        pool_op(lambda e: e.memset(negU[:], -1.0), [b_const])
        pool_op(lambda e: e.affine_select(out=negU[:], in_=negU[:], pattern=[[-1, P]], compare_op=ALU.is_ge,
                                          fill=0.0, base=0, channel_multiplier=1), [b_const])
        pool_op(lambda e: e.memset(negones[:], -1.0), [b_const])
        pool_op(lambda e: e.memset(ones_bf[:], 1.0), [b_const])
        pool_op(lambda e: e.memset(caus[:], 0.0), [b_const])
        pool_op(lambda e: e.affine_select(out=caus[:], in_=caus[:], pattern=[[-1, P]], compare_op=ALU.is_ge,
                                          fill=-1e30, base=0, channel_multiplier=1), [b_const])
        for k in range(NIT):
            pool_op(lambda e, k=k: e.memset(pow2[:, k:k + 1], float(2.0 ** -(k + 1))), [b_const])
        s_misc = Buf("misc")
        K.dma("sp", invf[:], invf_in[:, :], s_misc, (), [b_const])

        with ExitStack() as cs:
            pi_t = sbt(cs, "pi_t", [P, 512], I32); pf_t = sbt(cs, "pf_t", [P, 512], F32)
            a_t = sbt(cs, "a_t", [P, 512], F32); k_t = sbt(cs, "k_t", [P, 512], F32)
            ki_t = sbt(cs, "ki_t", [P, 512], I32); r_t = sbt(cs, "r_t", [P, 512], F32)
            b_pi = Buf("pi"); b_pf = Buf("pf"); b_a = Buf("a"); b_k = Buf("k"); b_ki = Buf("ki"); b_r = Buf("r")
            for tcn in range(8):
                sl = slice(tcn * 512, (tcn + 1) * 512)
                K.dma("sp", pi_t[:], pos_in[0:1, sl].to_broadcast([P, 512]), b_pi, (), [b_pi])
                K.op("dve", lambda e: e.tensor_copy(out=pf_t[:], in_=pi_t[:]), [b_pi], [b_pf])
                for which in range(2):
                    tab = CT if which == 0 else SS
                    if which == 0:
                        K.op("dve", lambda e: e.tensor_scalar(out=a_t[:], in0=pf_t[:], scalar1=invf[:, 0:1], scalar2=float(np.pi / 2),
                                                              op0=ALU.mult, op1=ALU.add), [b_pf, b_const], [b_a])
                    else:
                        K.op("dve", lambda e: e.tensor_scalar(out=a_t[:], in0=pf_t[:], scalar1=invf[:, 1:2], scalar2=None,
                                                              op0=ALU.mult), [b_pf, b_const], [b_a])
                    K.op("dve", lambda e: e.tensor_scalar(out=k_t[:], in0=a_t[:], scalar1=float(1.0 / TWO_PI), scalar2=None,
                                                          op0=ALU.mult), [b_a], [b_k])
                    K.op("dve", lambda e: e.tensor_copy(out=ki_t[:], in_=k_t[:]), [b_k], [b_ki])
                    K.op("dve", lambda e: e.tensor_copy(out=k_t[:], in_=ki_t[:]), [b_ki], [b_k])
                    K.op("dve", lambda e: e.scalar_tensor_tensor(out=r_t[:], in0=k_t[:], scalar=-C1, in1=a_t[:],
                                                                 op0=ALU.mult, op1=ALU.add), [b_k, b_a], [b_r])
                    K.op("dve", lambda e: e.scalar_tensor_tensor(out=a_t[:], in0=k_t[:], scalar=-C2, in1=r_t[:],
                                                                 op0=ALU.mult, op1=ALU.add), [b_k, b_r], [b_a])
                    K.op("dve", lambda e: e.tensor_scalar(out=r_t[:], in0=a_t[:], scalar1=float(-np.pi), scalar2=float(np.pi),
                                                          op0=ALU.max, op1=ALU.min), [b_a], [b_r])
                    K.op("act", lambda e, tab=tab, sl=sl: e.activation(out=tab[:, sl], in_=r_t[:], func=AF.Sin), [b_r], (), [b_tab])
            if debug:
                K.dma("sp", dbg["TAB"][:, 0, :], CT[:], s_misc, [b_tab], ())
                K.dma("sp", dbg["TAB"][:, 1, :], SS[:], s_misc, [b_tab], ())
            K.barrier()

        x_cur = x_in
        b_xsrc = Buf("xsrc")
        for l in range(nlayers):
            last = (l == DEPTH - 1)
            x_dst = out_d if last else XR
            dB = {n: Buf("d_" + n, loose=True) for n in ["QF", "KF", "VF", "GF", "QD", "KD", "VD", "GD", "IQ", "IK", "QS", "KS", "VS", "GS",
                                             "MG", "YF", "YD", "YS"]}
            with ExitStack() as cs:
                hT = sbt(cs, "hT", [P, 8, S], BF16); b_hT = Buf("hT")
                NW = 2
                wbuf = [sbt(cs, "wbuf%d" % i, [P, 8, 512], BF16) for i in range(NW)]
                b_w = [Buf("wbuf%d" % i) for i in range(NW)]
                wrb = [sbt(cs, "wrb%d" % i, [P, 8, P], BF16) for i in range(NW)]
                b_wr = [Buf("wrb%d" % i) for i in range(NW)]
                wctr = [0]

                def load_w(src_ap, ncols, rot_ap=None):
                    i = wctr[0] % NW
                    wctr[0] += 1
                    K.dma("pool", wbuf[i][:, :, 0:ncols], src_ap.rearrange("(kc p) n -> p kc n", p=P), b_w[i], (), [b_w[i]])
                    if rot_ap is not None:
                        K.dma("pool", wrb[i][:, :, 0:ncols], rot_ap.rearrange("(kc p) n -> p kc n", p=P), b_wr[i], (), [b_wr[i]])
                    return i

                with ExitStack() as ms:
                    csb = sbt(ms, "csb", [P, 8, P], F32); scb = sbt(ms, "scb", [P, 8, P], BF16)
                    brep = [sbt(ms, "brep%d" % i, [P, 512], F32) for i in range(2)]
                    gnrep = sbt(ms, "gnrep", [P, D], F32)
                    mtmp = sbt(ms, "mtmp", [P, 512], F32)
                    b_c = Buf("csb"); b_sc = Buf("scb"); b_br = [Buf("brep0"), Buf("brep1")]; b_gn = Buf("gnrep"); b_mt = Buf("mtmp")
                    K.dma("sp", csb[:], crep_in[:, :, :], b_c, (), [b_c])
                    K.dma("sp", gnrep[:], g_norm[l:l + 1, :].to_broadcast([P, D]), b_gn, (), [b_gn])
                    K.op("act", lambda e: e.activation(out=scb[:], in_=csb[:], func=AF.Silu), [b_c], [b_sc])
                    for n in range(6):
                        wi = load_w(w_ada[l, :, n * 512:(n + 1) * 512], 512)
                        bi = n % 2
                        K.dma("sp", brep[bi][:], b_ada[l:l + 1, n * 512:(n + 1) * 512].to_broadcast([P, 512]), b_br[bi], (), [b_br[bi]])
                        bank = n % 2
                        for kc in range(8):
                            K.op("pe", lambda e, kc=kc, wi=wi, bank=bank: e.matmul(psum[:, bank, :], lhsT=scb[:, kc, :], rhs=wbuf[wi][:, kc, :],
                                                                                   start=(kc == 0), stop=(kc == 7), skip_group_check=True),
                                 [b_sc, b_w[wi]], [PB[bank]])
                        csl = slice((n % 2) * 512, (n % 2) * 512 + 512)
                        if n < 2:
                            K.op("dve", lambda e, bank=bank, bi=bi, csl=csl: e.tensor_tensor(out=shift_rep[:, csl], in0=psum[:, bank, :], in1=brep[bi][:], op=ALU.add),
                                 [PB[bank], b_br[bi]], (), [b_mod])
                        elif n < 4:
                            K.op("dve", lambda e, bank=bank, bi=bi: e.tensor_tensor(out=mtmp[:], in0=psum[:, bank, :], in1=brep[bi][:], op=ALU.add),
                                 [PB[bank], b_br[bi]], [b_mt])
                            K.op("dve", lambda e, csl=csl: e.scalar_tensor_tensor(out=A_rep[:, csl], in0=mtmp[:], scalar=1.0, in1=gnrep[:, csl],
                                                                                  op0=ALU.add, op1=ALU.mult), [b_mt, b_gn], (), [b_mod])
                        else:
                            K.op("dve", lambda e, bank=bank, bi=bi, csl=csl: e.tensor_tensor(out=gate_rep[:, csl], in0=psum[:, bank, :], in1=brep[bi][:], op=ALU.add),
                                 [PB[bank], b_br[bi]], (), [b_mod])
                    K.barrier()

                with ExitStack() as hs:
                    xt = [sbt(hs, "xt%d" % i, [P, D], F32) for i in range(2)]; b_xt = [Buf("xt0"), Buf("xt1")]
                    ht = [sbt(hs, "ht%d" % i, [P, D], BF16) for i in range(2)]; b_ht = [Buf("ht0"), Buf("ht1")]
                    junk = sbt(hs, "junkA", [P, D], F32); b_junk = Buf("junkA")
                    htmp = sbt(hs, "htmp", [P, D], F32); b_htmp = Buf("htmp")
                    st = sbt(hs, "statA", [P, 4], F32); b_st = Buf("statA")

                    def h_load(i):
                        K.dma("sp", xt[i % 2][:], x_cur[i * P:(i + 1) * P, :], b_xt[i % 2], [b_xsrc], [b_xt[i % 2]])

                    def hs1(i):
                        s = i % 2
                        K.op("act", lambda e: e.activation(out=junk[:], in_=xt[s][:], func=AF.Square, accum_out=st[:, 0:1]),
                             [b_xt[s]], [b_junk, b_st])
                        K.op("dve", lambda e: e.tensor_scalar(out=st[:, 1:2], in0=st[:, 0:1], scalar1=float(1.0 / D), scalar2=EPS,
                                                              op0=ALU.mult, op1=ALU.add), [b_st], [b_st])
                        K.op("act", lambda e: e.activation(out=st[:, 2:3], in_=st[:, 1:2], func=AF.Sqrt), [b_st], [b_st])
                        K.op("dve", lambda e: e.reciprocal(out=st[:, 3:4], in_=st[:, 2:3]), [b_st], [b_st])
                        K.op("dve", lambda e: e.scalar_tensor_tensor(out=htmp[:], in0=xt[s][:], scalar=st[:, 3:4], in1=A_rep[:],
                                                                     op0=ALU.mult, op1=ALU.mult), [b_xt[s], b_st, b_mod], [b_htmp])
                        K.op("pool", lambda e: e.tensor_tensor(out=ht[s][:], in0=htmp[:], in1=shift_rep[:], op=ALU.add),
                             [b_htmp, b_mod], [b_ht[s]])

                    def hs2(i):
                        s = i % 2
                        bank = 2 + (i % 2)
                        pbf = psum[:, bank, :].bitcast(BF16)
                        for kc in range(8):
                            K.op("pe", lambda e, kc=kc: e.transpose(out=pbf[:, kc * P:(kc + 1) * P], in_=ht[s][:, kc * P:(kc + 1) * P],
                                                                    identity=ident[:]), [b_ht[s], b_const], [PB[bank]])
                        K.op("act", lambda e: e.activation(out=hT[:, :, i * P:(i + 1) * P], in_=pbf.rearrange("p (a b) -> p a b", a=8),
                                                           func=AF.Copy), [PB[bank]], (), [b_hT])

                    h_load(0)
                    for i in range(NT + 1):
                        if i < NT:
                            if i + 1 < NT:
                                h_load(i + 1)
                            hs1(i)
                        if i >= 1:
                            hs2(i - 1)
                    if debug:
                        K.dma("sp", dbg["HT"][:, :, :], hT[:], s_misc, [b_hT], ())
                    K.barrier()

                with ExitStack() as ps_:
                    stg = [sbt(ps_, "stg%d" % i, [P, S], BF16) for i in range(2)]; b_stg = [Buf("stg0"), Buf("stg1")]
                    tstg = [sbt(ps_, "tstg%d" % i, [P, 520], BF16) for i in range(3)]; b_tstg = [Buf("tstg%d" % i) for i in range(3)]
                    t1 = [sbt(ps_, "t1_%d" % i, [P, 512], F32) for i in range(2)]; b_t1 = [Buf("t1_0"), Buf("t1_1")]
                    t2 = [sbt(ps_, "t2_%d" % i, [P, 512], F32) for i in range(2)]; b_t2 = [Buf("t2_0"), Buf("t2_1")]
                    ckvT = sbt(ps_, "ckvT", [P, 512], BF16); b_ckv = Buf("ckvT")
                    sqb = sbt(ps_, "sqb", [P, 512], BF16); b_sqb = Buf("sqb")
                    rstd = sbt(ps_, "rstd", [P, 512], F32); b_rstd = Buf("rstd")
                    dvT = sbt(ps_, "dvT", [P, 512], BF16); b_dvT = Buf("dvT")
                    wkv_f = sbt(ps_, "wkv_f", [P, P], F32); wkvr_f = sbt(ps_, "wkvr_f", [P, 64], F32)
                    wkv_b = sbt(ps_, "wkv_b", [P, P], BF16); wkvr_b = sbt(ps_, "wkvr_b", [P, 64], BF16)
                    gkv = sbt(ps_, "gkv", [P, 1], F32); b_wkv = Buf("wkv")
                    bf = sbt(ps_, "bfg", [8, 1], F32); b_bf = Buf("bfg")
                    etmp = sbt(ps_, "etmpB", [8, 512], F32); b_etmp = Buf("etmpB")
                    spc = sbt(ps_, "spc", [8, 512], F32); b_spc = Buf("spc")
                    fnc = [sbt(ps_, "fnc%d" % i, [8, 512], F32) for i in range(2)]; b_fnc = [Buf("fnc0"), Buf("fnc1")]
                    fr1 = sbt(ps_, "fr1", [8, 512], F32); b_fr1 = Buf("fr1")
                    fr2 = sbt(ps_, "fr2", [8, 512], F32); b_fr2 = Buf("fr2")
                    fbs = [sbt(ps_, "fbs%d" % i, [8, 512], BF16) for i in range(4)]; b_fbs = [Buf("fbs%d" % i) for i in range(4)]
                    for i in range(3):
                        K.op("pool", lambda e, i=i: e.memset(tstg[i][:], 1.0), (), [b_tstg[i]])
                    K.op("pool", lambda e: e.memset(stg[0][0:8, :], 1.0), (), [b_stg[0]])
                    for r_ in range(3):
                        K.dma("sp", QF[:, 65 + r_, :], stg[0][0:8, :], b_stg[0], [b_stg[0]], (), [dB["QF"]])
                    K.dma("sp", KF[:, 64, :], stg[0][0:8, :], b_stg[0], [b_stg[0]], (), [dB["KF"]])
                    K.dma("sp", bf[:], b_fgt[l, :, :], b_bf, (), [b_bf])
                    K.op("dve", lambda e: e.tensor_scalar(out=bf[:], in0=bf[:], scalar1=-1.0, scalar2=None, op0=ALU.mult), [b_bf], [b_bf])
                    K.dma("sp", wkv_f[:], w_kv[l, :, :], b_wkv, (), [b_wkv])
                    K.dma("sp", wkvr_f[:], w_kvr[l, :, :], b_wkv, (), [b_wkv])
                    K.dma("sp", gkv[:], g_kv[l, :, :], b_wkv, (), [b_wkv])
                    K.op("dve", lambda e: e.tensor_scalar(out=wkv_b[:], in0=wkv_f[:], scalar1=gkv[:, 0:1], scalar2=None, op0=ALU.mult), [b_wkv], [b_wkv])
                    K.op("dve", lambda e: e.tensor_scalar(out=wkvr_b[:], in0=wkvr_f[:], scalar1=gkv[:, 0:1], scalar2=None, op0=ALU.mult), [b_wkv], [b_wkv])
                    bctr = [0]

                    def next_bank():
                        b = bctr[0] % 4
                        bctr[0] += 1
                        return b

                    sctr = [0]
                    jobs = []

                    def fm_job(col0, M, rotcol0, post, dst_fn):
                        rot_ap = w_rot[l, :, rotcol0:rotcol0 + M] if rotcol0 is not None else None
                        box = {}

                        def load():
                            box["wi"] = load_w(w_in[l, :, col0:col0 + M], M, rot_ap)

                        def run():
                            wi = box["wi"]
                            si = sctr[0] % 2
                            sctr[0] += 1
                            for tcn in range(8):
                                sl = slice(tcn * 512, (tcn + 1) * 512)
                                bank = next_bank()
                                for kc in range(8):
                                    K.op("pe", lambda e, kc=kc, bank=bank, sl=sl: e.matmul(psum[0:M, bank, :], lhsT=wbuf[wi][:, kc, 0:M], rhs=hT[:, kc, sl],
                                                                                         start=(kc == 0), stop=(kc == 7), skip_group_check=True),
                                         [b_w[wi], b_hT], [PB[bank]])
                                bank2 = None
                                if rot_ap is not None:
                                    bank2 = next_bank()
                                    for kc in range(8):
                                        K.op("pe", lambda e, kc=kc, bank2=bank2, sl=sl: e.matmul(psum[0:M, bank2, :], lhsT=wrb[wi][:, kc, 0:M], rhs=hT[:, kc, sl],
                                                                                               start=(kc == 0), stop=(kc == 7), skip_group_check=True),
                                             [b_wr[wi], b_hT], [PB[bank2]])
                                post(tcn, sl, bank, bank2, si)
                            if dst_fn is not None:
                                dst_fn(si)
                        jobs.append((load, run))

                    def post_scale(scale):
                        def f(tcn, sl, bank, bank2, si, M=P):
                            K.op("act", lambda e: e.activation(out=stg[si][0:M, sl], in_=psum[0:M, bank, :], func=AF.Copy, scale=float(scale)),
                                 [PB[bank]], (), [b_stg[si]])
                        return f

                    def post_rope(scale, M, odt=None):
                        def f(tcn, sl, bank, bank2, si):
                            j = tcn % 2
                            so = stg[si][:, :] if odt is None else stg[si][:, :].bitcast(odt)
                            K.op("dve", lambda e: e.scalar_tensor_tensor(out=t1[j][0:M, :], in0=psum[0:M, bank, :], scalar=float(scale), in1=CT[0:M, sl],
                                                                         op0=ALU.mult, op1=ALU.mult), [PB[bank], b_tab], [b_t1[j]])
                            K.op("dve", lambda e: e.scalar_tensor_tensor(out=t2[j][0:M, :], in0=psum[0:M, bank2, :], scalar=float(scale), in1=SS[0:M, sl],
                                                                         op0=ALU.mult, op1=ALU.mult), [PB[bank2], b_tab], [b_t2[j]])
                            K.op("pool", lambda e: e.tensor_tensor(out=so[0:M, sl], in0=t1[j][0:M, :], in1=t2[j][0:M, :], op=ALU.add),
                                 [b_t1[j], b_t2[j]], (), [b_stg[si]])
                        return f

                    def store_heads(dst, pair, dbuf):
                        def f(si):
                            for hh in range(2):
                                K.dma("sp", dst[2 * pair + hh, 0:64, :], stg[si][hh * 64:(hh + 1) * 64, :], b_stg[si], [b_stg[si]], (), [dbuf])
                        return f

                    def store_fm_heads(dst, pair, dbuf, odt=None):
                        def f(si):
                            so = stg[si][:, :] if odt is None else stg[si][:, :].bitcast(odt)
                            for hh in range(2):
                                K.dma("sp", dst[:, 2 * pair + hh, :], so[hh * 64:(hh + 1) * 64, :], b_stg[si], [b_stg[si]], (), [dbuf])
                        return f

                    def post_ff(tcn, sl, bank, bank2, si):
                        K.op("act", lambda e: e.activation(out=etmp[:], in_=psum[0:8, bank, :], func=AF.Exp, bias=bf[:, 0:1], scale=-1.0),
                             [PB[bank], b_bf], [b_etmp])
                        K.op("act", lambda e: e.activation(out=spc[:], in_=etmp[:], func=AF.Ln, bias=1.0), [b_etmp], [b_spc])
                        j = tcn % 2
                        init = 0.0 if tcn == 0 else fnc[1 - j][:, 511:512]
                        rd = [b_spc] if tcn == 0 else [b_spc, b_fnc[1 - j]]
                        K.op("dve", lambda e: e.tensor_tensor_scan(out=fnc[j][:], data0=spc[:], data1=spc[:], initial=init, op0=ALU.add, op1=ALU.max),
                             rd, [b_fnc[j]])
                        K.op("dve", lambda e: e.tensor_scalar(out=fbs[0][:], in0=fnc[j][:], scalar1=-1.0, scalar2=None, op0=ALU.mult), [b_fnc[j]], [b_fbs[0]])
                        K.op("dve", lambda e: e.tensor_copy(out=fbs[1][:], in_=fnc[j][:]), [b_fnc[j]], [b_fbs[1]])
                        K.op("dve", lambda e: e.tensor_tensor(out=fr1[:], in0=fnc[j][:], in1=fbs[1][:], op=ALU.subtract), [b_fnc[j], b_fbs[1]], [b_fr1])
                        K.op("dve", lambda e: e.tensor_copy(out=fbs[2][:], in_=fr1[:]), [b_fr1], [b_fbs[2]])
                        K.op("dve", lambda e: e.tensor_tensor(out=fr2[:], in0=fr1[:], in1=fbs[2][:], op=ALU.subtract), [b_fr1, b_fbs[2]], [b_fr2])
                        K.op("dve", lambda e: e.tensor_copy(out=fbs[3][:], in_=fr2[:]), [b_fr2], [b_fbs[3]])
                        K.dma("sp", QF[:, 64, sl], fbs[0][:], b_fbs[0], [b_fbs[0]], (), [dB["QF"]])
                        for r_ in range(3):
                            K.dma("sp", KF[:, 65 + r_, sl], fbs[1 + r_][:], b_fbs[1 + r_], [b_fbs[1 + r_]], (), [dB["KF"]])

                    vctr = [0]

                    def post_ckv(tcn, sl, bank, bank2, si):
                        K.op("act", lambda e: e.activation(out=ckvT[:], in_=psum[:, bank, :], func=AF.Copy), [PB[bank]], [b_ckv])
                        K.op("act", lambda e: e.activation(out=sqb[:], in_=psum[:, bank, :], func=AF.Square), [PB[bank]], [b_sqb])
                        K.op("pe", lambda e: e.matmul(psum[:, 4, :], lhsT=ones_bf[:], rhs=sqb[:], start=True, stop=True, skip_group_check=True),
                             [b_sqb, b_const], [PB[4]])
                        K.op("dve", lambda e: e.tensor_scalar(out=rstd[:], in0=psum[:, 4, :], scalar1=float(1.0 / 128), scalar2=EPS, op0=ALU.mult, op1=ALU.add),
                             [PB[4]], [b_rstd])
                        K.op("act", lambda e: e.activation(out=rstd[:], in_=rstd[:], func=AF.Sqrt), [b_rstd], [b_rstd])
                        K.op("dve", lambda e: e.reciprocal(out=rstd[:], in_=rstd[:]), [b_rstd], [b_rstd])
                        K.op("pe", lambda e: e.matmul(psum[:, 5, :], lhsT=wkv_b[:], rhs=ckvT[:], start=True, stop=True, skip_group_check=True),
                             [b_ckv, b_wkv], [PB[5]])
                        K.op("pe", lambda e: e.matmul(psum[0:64, 6, :], lhsT=wkvr_b[:], rhs=ckvT[:], start=True, stop=True, skip_group_check=True),
                             [b_ckv, b_wkv], [PB[6]])
                        j = tcn % 2
                        K.op("dve", lambda e: e.tensor_tensor(out=t1[j][0:64, :], in0=psum[0:64, 5, :], in1=CT[0:64, sl], op=ALU.mult), [PB[5], b_tab], [b_t1[j]])
                        K.op("dve", lambda e: e.tensor_tensor(out=t2[j][0:64, :], in0=psum[0:64, 6, :], in1=SS[0:64, sl], op=ALU.mult), [PB[6], b_tab], [b_t2[j]])
                        K.op("pool", lambda e: e.tensor_tensor(out=t1[j][0:64, :], in0=t1[j][0:64, :], in1=t2[j][0:64, :], op=ALU.add), [b_t2[j]], [b_t1[j]])
                        K.op("pool", lambda e: e.tensor_tensor(out=stg[si][0:64, sl], in0=t1[j][0:64, :], in1=rstd[0:64, :], op=ALU.mult),
                             [b_t1[j], b_rstd], (), [b_stg[si]])
                        K.op("dve", lambda e: e.tensor_tensor(out=dvT[64:128, :], in0=psum[64:128, 5, :], in1=rstd[64:128, :], op=ALU.mult),
                             [PB[5], b_rstd], [b_dvT])
                        pbf = psum[:, 7, :].bitcast(BF16)
                        for q in range(4):
                            K.op("pe", lambda e, q=q: e.transpose(out=pbf[:, q * 64:(q + 1) * 64], in_=dvT[64:128, q * P:(q + 1) * P], identity=ident[64:128, 64:128]),
                                 [b_dvT, b_const], [PB[7]])
                        for q in range(4):
                            vs = vctr[0] % 3
                            vctr[0] += 1
                            K.op("act", lambda e, q=q, vs=vs: e.activation(out=tstg[vs][:, 0:64], in_=pbf[:, q * 64:(q + 1) * 64], func=AF.Copy), [PB[7]], [b_tstg[vs]])
                            tok0 = tcn * 512 + q * P
                            K.dma("sp", VD[tok0:tok0 + P, :], tstg[vs][:, 0:65], b_tstg[vs], [b_tstg[vs]], (), [dB["VD"]])

                    tctr = [0]

                    def tm_job(col0, ncols, post):
                        box = {}

                        def load():
                            box["wi"] = load_w(w_in[l, :, col0:col0 + ncols], ncols)

                        def run():
                            wi = box["wi"]
                            for i in range(NT):
                                bank = next_bank()
                                for kc in range(8):
                                    K.op("pe", lambda e, kc=kc, bank=bank, i=i: e.matmul(psum[:, bank, 0:ncols], lhsT=hT[:, kc, i * P:(i + 1) * P], rhs=wbuf[wi][:, kc, 0:ncols],
                                                                                         start=(kc == 0), stop=(kc == 7), skip_group_check=True),
                                         [b_w[wi], b_hT], [PB[bank]])
                                ts = tctr[0] % 3
                                tctr[0] += 1
                                post(i, bank, ts)
                        jobs.append((load, run))

                    def post_tm_act(func, dst, dcol0, dbuf):
                        def f(i, bank, ts):
                            K.op("act", lambda e: e.activation(out=tstg[ts][:, 0:512], in_=psum[:, bank, :], func=func), [PB[bank]], [b_tstg[ts]])
                            K.dma("sp", dst[i * P:(i + 1) * P, dcol0:dcol0 + 512], tstg[ts][:, 0:512], b_tstg[ts], [b_tstg[ts]], (), [dbuf])
                        return f

                    def post_fv(i, bank, ts):
                        tv = tstg[ts][:, 0:520].rearrange("p (h e) -> p h e", h=8)
                        K.op("act", lambda e: e.activation(out=tv[:, :, 0:64], in_=psum[:, bank, :].rearrange("p (h e) -> p h e", h=8), func=AF.Copy),
                             [PB[bank]], [b_tstg[ts]])
                        K.dma("sp", VF[i * P:(i + 1) * P, :, :].rearrange("p h e -> p (h e)"), tstg[ts][:, 0:520], b_tstg[ts], [b_tstg[ts]], (), [dB["VF"]])

                    def post_diw(i, bank, ts):
                        cst = float((64 ** -0.5) * (8 ** -0.5))
                        K.op("act", lambda e: e.activation(out=wab[:, i, :], in_=psum[:, bank, 0:8], func=AF.Abs, scale=cst), [PB[bank]], (), [b_wab])
                        K.op("act", lambda e: e.activation(out=wsg[:, i, :], in_=psum[:, bank, 0:8], func=AF.Sign), [PB[bank]], (), [b_wab])

                    def remset():
                        for i in range(3):
                            K.op("pool", lambda e, i=i: e.memset(tstg[i][:], 1.0), (), [b_tstg[i]])

                    fm_job(OFF["dckv"], P, None, post_ckv,
                           lambda si: K.dma("sp", KD[:, :], stg[si][0:64, :], b_stg[si], [b_stg[si]], (), [dB["KD"]]))
                    fm_job(OFF["ff"], 8, None, post_ff, None)
                    for pr in range(4):
                        fm_job(OFF["fq"] + pr * P, P, None, post_scale(0.125), store_heads(QF, pr, dB["QF"]))
                    for pr in range(4):
                        fm_job(OFF["fk"] + pr * P, P, None, post_scale(1.0), store_heads(KF, pr, dB["KF"]))
                    for pr in range(4):
                        fm_job(OFF["sq"] + pr * P, P, None, post_scale(0.125), store_heads(QS, pr, dB["QS"]))
                    for pr in range(4):
                        fm_job(OFF["sk"] + pr * P, P, None, post_scale(1.0), store_heads(KS, pr, dB["KS"]))
                    for pr in range(4):
                        fm_job(OFF["dq"] + pr * P, P, pr * P, post_rope(0.125, P), store_fm_heads(QD, pr, dB["QD"]))
                    for pr in range(4):
                        fm_job(OFF["diq"] + pr * P, P, 512 + pr * P, post_rope(1.0, P, FP16), store_fm_heads(IQ, pr, dB["IQ"], FP16))
                    fm_job(OFF["dik"], 64, 1024, post_rope(1.0, 64, FP16),
                           lambda si: K.dma("sp", IK[:, :], stg[si][:, :].bitcast(FP16)[0:64, :], b_stg[si], [b_stg[si]], (), [dB["IK"]]))
                    jobs.append((lambda: None, remset))
                    tm_job(OFF["fv"], 512, post_fv)
                    tm_job(OFF["sv"], 512, post_tm_act(AF.Copy, VS, 0, dB["VS"]))
                    tm_job(OFF["fg"], 512, post_tm_act(AF.Silu, GF, 0, dB["GF"]))
                    tm_job(OFF["dg"], 512, post_tm_act(AF.Silu, GD, 0, dB["GD"]))
                    tm_job(OFF["sg"], 512, post_tm_act(AF.Silu, GS, 0, dB["GS"]))
                    for m in range(6):
                        tm_job(OFF["merge"] + m * 512, 512, post_tm_act(AF.Sigmoid, MG, m * 512, dB["MG"]))
                    tm_job(OFF["diw"], 8, post_diw)
                    jobs[0][0]()
                    for jn, (ld, rn) in enumerate(jobs):
                        if jn + 1 < len(jobs):
                            jobs[jn + 1][0]()
                        rn()
                    if debug:
                        K.dma("sp", dbg["WAB"][:, :, :], wab[:], s_misc, [b_wab], ())
                        K.dma("sp", dbg["WSG"][:, :, :], wsg[:], s_misc, [b_wab], ())
                    K.barrier()
            if stop_after == "B":
                break

            with ExitStack() as cs:
                qT = [sbt(cs, "qT%d" % i, [P, S], BF16) for i in range(2)]; b_qT = [Buf("qT0"), Buf("qT1")]
                kT = [sbt(cs, "kT%d" % i, [P, S], BF16) for i in range(2)]; b_kT = [Buf("kT0"), Buf("kT1")]
                Vt = [sbt(cs, "Vt%d" % i, [P, NT, 65], BF16) for i in range(2)]; b_V = [Buf("V0"), Buf("V1")]
                Gt = [sbt(cs, "Gt%d" % i, [P, NT, 64], BF16) for i in range(2)]; b_G = [Buf("G0"), Buf("G1")]
                ystg = [sbt(cs, "ystg%d" % i, [P, NT, 64], BF16) for i in range(2)]; b_ys = [Buf("ys0"), Buf("ys1")]
                pT = [sbt(cs, "pT%d" % i, [P, 1024], BF16) for i in range(3)]; b_pT = [Buf("pT%d" % i) for i in range(3)]
                spb = [sbt(cs, "spb%d" % i, [P, 512], BF16) for i in range(2)]; b_spb = [Buf("spb0"), Buf("spb1")]
                etm = [sbt(cs, "etm%d" % i, [P, 512], F32) for i in range(2)]; b_etm = [Buf("etm0"), Buf("etm1")]
                lsum = sbt(cs, "lsum", [P, 512], BF16); b_lsum = Buf("lsum")
                rden = sbt(cs, "rden", [P, 8], F32); b_rden = Buf("rden")

                def load_head(kind, h, s):
                    if kind == "fox":
                        K.dma("sp", qT[s][0:68, :], QF[h, :, :], b_qT[s], [dB["QF"]], [b_qT[s]])
                        K.dma("sp", kT[s][0:68, :], KF[h, :, :], b_kT[s], [dB["KF"]], [b_kT[s]])
                        K.dma("sp", Vt[s][:, :, :], VF.rearrange("(j p) h e -> p j h e", p=P)[:, :, h, :], b_V[s], [dB["VF"]], [b_V[s]])
                        K.dma("sp", Gt[s][:, :, :], GF.rearrange("(j p) (h e) -> p j h e", p=P, h=8)[:, :, h, :], b_G[s], [dB["GF"]], [b_G[s]])
                    else:
                        K.dma("sp", qT[s][0:64, :], QS[h, :, :], b_qT[s], [dB["QS"]], [b_qT[s]])
                        K.dma("sp", kT[s][0:64, :], KS[h, :, :], b_kT[s], [dB["KS"]], [b_kT[s]])
                        K.dma("sp", Vt[s][:, :, 0:64], VS.rearrange("(j p) (h e) -> p j h e", p=P, h=8)[:, :, h, :], b_V[s], [dB["VS"]], [b_V[s]])
                        K.dma("sp", Gt[s][:, :, :], GS.rearrange("(j p) (h e) -> p j h e", p=P, h=8)[:, :, h, :], b_G[s], [dB["GS"]], [b_G[s]])

                heads_seq = [("fox", h) for h in range(8)] + [("sb", h) for h in range(8)]
                for s_ in range(2):
                    K.op("pool", lambda e, s_=s_: e.memset(qT[s_][:], 0.0), (), [b_qT[s_]])
                    K.op("pool", lambda e, s_=s_: e.memset(kT[s_][:], 0.0), (), [b_kT[s_]])

                def load_head_idx(hi):
                    if hi >= len(heads_seq):
                        return
                    kind_, h_ = heads_seq[hi]
                    s_ = hi % 2
                    if kind_ == "sb" and h_ < 2:
                        K.op("pool", lambda e: e.memset(qT[s_][64:128, :], 0.0), (), [b_qT[s_]])
                        K.op("pool", lambda e: e.memset(kT[s_][64:128, :], 0.0), (), [b_kT[s_]])
                    load_head(kind_, h_, s_)

                steps = []
                cg = 0
                for hi, (kind, h) in enumerate(heads_seq):
                    for c in range(8):
                        Js = list(range(4 * c + 4))
                        if kind == "sb":
                            Js = Js[::-1]
                        for idx, J in enumerate(Js):
                            r = J - 4 * c
                            steps.append(dict(hi=hi, kind=kind, h=h, s=hi % 2, c=c, cg=cg, idx=idx, J=J, r=r, qlo=max(0, r) * P,
                                              first=(idx == 0), last=(idx == len(Js) - 1), last_head=(idx == len(Js) - 1 and c == 7),
                                              g=len(steps)))
                        cg += 1
                first_pv = {}
                pslot = {}
                KR = 128

                def st1(t):
                    kind, s, c, J, r, qlo = t["kind"], t["s"], t["c"], t["J"], t["r"], t["qlo"]
                    bank = t["g"] % 4
                    tri = tri_le if kind == "fox" else tri_lt
                    K.op("pe", lambda e: e.matmul(psum[:, bank, qlo:512], lhsT=kT[s][0:KR, J * P:(J + 1) * P], rhs=qT[s][0:KR, c * 512 + qlo:(c + 1) * 512],
                                                  start=True, stop=(r < 0), skip_group_check=True), [b_kT[s], b_qT[s]], [PB[bank]])
                    if r >= 0:
                        K.op("pe", lambda e: e.matmul(psum[:, bank, qlo:qlo + P], lhsT=ident[:], rhs=tri[:], start=False, stop=True, skip_group_check=True),
                             [b_const], (), [PB[bank]])
                    if kind == "sb":
                        j2 = t["g"] % 2
                        K.op("act", lambda e: e.activation(out=etm[j2][:, qlo:512], in_=psum[:, bank, qlo:512], func=AF.Exp), [PB[bank]], [b_etm[j2]])
                        K.op("act", lambda e: e.activation(out=spb[j2][:, qlo:512], in_=etm[j2][:, qlo:512], func=AF.Ln, bias=1.0), [b_etm[j2]], [b_spb[j2]])

                def st2(t):
                    kind, qlo = t["kind"], t["qlo"]
                    bank = t["g"] % 4
                    ps_ = t["g"] % 3
                    pslot[t["g"]] = ps_
                    if kind == "sb":
                        j2 = t["g"] % 2
                        if t["first"]:
                            K.op("dve", lambda e: e.memset(lsum[:], 0.0), (), [b_lsum])
                        K.op("pe", lambda e: e.matmul(psum[:, bank, qlo:512], lhsT=negU[:], rhs=spb[j2][:, qlo:512], start=False, stop=False, skip_group_check=True),
                             [b_spb[j2], b_const], (), [PB[bank]])
                        if not t["first"]:
                            K.op("pe", lambda e: e.matmul(psum[:, bank, qlo:512], lhsT=negones[:], rhs=lsum[:, qlo:512], start=False, stop=True, skip_group_check=True),
                                 [b_lsum, b_const], (), [PB[bank]])
                    K.op("act", lambda e: e.activation(out=pT[ps_][:, qlo:512], in_=psum[:, bank, qlo:512], func=AF.Exp), [PB[bank]], [b_pT[ps_]])
                    if kind == "sb":
                        K.op("dve", lambda e: e.tensor_tensor(out=lsum[:, qlo:512], in0=lsum[:, qlo:512], in1=spb[j2][:, qlo:512], op=ALU.add),
                             [b_spb[j2]], [b_lsum])

                def st3(t):
                    kind, s, c, J, r, h = t["kind"], t["s"], t["c"], t["J"], t["r"], t["h"]
                    VW = 65 if kind == "fox" else 64
                    OB = 6 + (t["cg"] % 2)
                    ob = psum[:, OB, 0:4 * 65].rearrange("p (a b) -> p a b", a=4)
                    ps_ = pslot.pop(t["g"])
                    for sub in range(max(0, r), 4):
                        st_ = t["cg"] not in first_pv
                        first_pv[t["cg"]] = True
                        K.op("pe", lambda e, sub=sub, st_=st_: e.matmul(ob[:, sub, 0:VW], lhsT=pT[ps_][:, sub * P:(sub + 1) * P], rhs=Vt[s][:, J, 0:VW],
                                                                        start=st_, stop=False, skip_group_check=True),
                             [b_pT[ps_], b_V[s]], (), [PB[OB]])
                    if t["last"]:
                        if kind == "fox":
                            K.op("dve", lambda e: e.reciprocal(out=rden[:, 0:4], in_=ob[:, :, 64]), [PB[OB]], [b_rden])
                            for sub in range(4):
                                K.op("dve", lambda e, sub=sub: e.scalar_tensor_tensor(out=ystg[s][:, 4 * c + sub, :], in0=ob[:, sub, 0:64], scalar=rden[:, sub:sub + 1],
                                                                                      in1=Gt[s][:, 4 * c + sub, :], op0=ALU.mult, op1=ALU.mult),
                                     [PB[OB], b_rden, b_G[s]], (), [b_ys[s]])
                        else:
                            for sub in range(4):
                                K.op("dve", lambda e, sub=sub: e.tensor_tensor(out=ystg[s][:, 4 * c + sub, :], in0=ob[:, sub, 0:64], in1=Gt[s][:, 4 * c + sub, :], op=ALU.mult),
                                     [PB[OB], b_G[s]], (), [b_ys[s]])
                    if t["last_head"]:
                        ydst = YF if kind == "fox" else YS
                        K.dma("sp", ydst.rearrange("(j p) (h e) -> p j h e", p=P, h=8)[:, :, h, :], ystg[s][:, :, :], b_ys[s], [b_ys[s]], (),
                              [dB["YF" if kind == "fox" else "YS"]])
                        load_head_idx(t["hi"] + 2)

                load_head_idx(0)
                load_head_idx(1)
                nst = len(steps)
                for k in range(nst + 2):
                    if k < nst:
                        st1(steps[k])
                    if 0 <= k - 1 < nst:
                        st2(steps[k - 1])
                    if 0 <= k - 2 < nst:
                        st3(steps[k - 2])
                K.barrier()
            if stop_after == "C1":
                break

            with ExitStack() as cs:
                kd = sbt(cs, "kd", [P, S], BF16); ikd = sbt(cs, "ikd", [P, S], FP16); vd = sbt(cs, "vd", [P, NT, 65], BF16)
                b_kd = Buf("kd"); b_ikd = Buf("ikd"); b_vd = Buf("vd")
                iqt = [sbt(cs, "iqt%d" % i, [P, 8, P], FP16) for i in range(2)]; b_iqt = [Buf("iqt0"), Buf("iqt1")]
                qdt = [sbt(cs, "qdt%d" % i, [P, 8, P], BF16) for i in range(2)]; b_qdt = [Buf("qdt0"), Buf("qdt1")]
                gdt = [sbt(cs, "gdt%d" % i, [P, 512], BF16) for i in range(2)]; b_gdt = [Buf("gdt0"), Buf("gdt1")]
                dg = [sbt(cs, "dg%d" % i, [P, 8, P], FP16) for i in range(2)]; b_dg = [Buf("dg0"), Buf("dg1")]
                ident16 = sbt(cs, "ident16", [P, P], FP16); caus16 = sbt(cs, "caus16", [P, P], FP16); b_c16 = Buf("c16")
                acc = [sbt(cs, "acc%d" % i, [P, S], F32) for i in range(2)]; b_acc = [Buf("acc0"), Buf("acc1")]
                NRT = 4
                rt = [sbt(cs, "rt%d" % i, [P, 512], FP16) for i in range(NRT)]; b_rt = [Buf("rt%d" % i) for i in range(NRT)]
                Mb = sbt(cs, "Mb", [P, S], BF16); b_M = Buf("Mb")
                MT = [sbt(cs, "MT%d" % i, [P, NT, P], BF16) for i in range(2)]; b_MT = [Buf("MT0"), Buf("MT1")]
                pTd = [sbt(cs, "pTd%d" % i, [P, 8, P], BF16) for i in range(3)]; b_pTd = [Buf("pTd%d" % i) for i in range(3)]
                sm = sbt(cs, "smD", [P, 8], F32); b_sm = Buf("smD")
                nW = sbt(cs, "nWD", [P, NIT], F32); hW = sbt(cs, "hWD", [P, NIT], F32); b_W = Buf("WD")
                rden = sbt(cs, "rdenD", [P, 8], F32); b_rden = Buf("rdenD")
                yd = [sbt(cs, "yd%d" % i, [P, 512], BF16) for i in range(2)]; b_yd = [Buf("yd0"), Buf("yd1")]
                K.op("pool", lambda e: e.tensor_copy(out=ident16[:], in_=ident[:]), [b_const], [b_c16])
                K.op("pool", lambda e: e.memset(caus16[:], 0.0), (), [b_c16])
                K.op("pool", lambda e: e.affine_select(out=caus16[:], in_=caus16[:], pattern=[[-1, P]], compare_op=ALU.is_ge,
                                                       fill=NEG, base=0, channel_multiplier=1), (), [b_c16])
                K.op("pool", lambda e: e.memset(kd[64:128, :], 0.0), (), [b_kd])
                K.op("pool", lambda e: e.memset(ikd[64:128, :], 0.0), (), [b_ikd])
                for s_ in range(2):
                    K.op("pool", lambda e, s_=s_: e.memset(iqt[s_][64:128, :, :], 0.0), (), [b_iqt[s_]])
                    K.op("pool", lambda e, s_=s_: e.memset(qdt[s_][64:128, :, :], 0.0), (), [b_qdt[s_]])
                K.dma("sp", kd[0:64, :], KD[:, :], b_kd, [dB["KD"]], [b_kd])
                K.dma("sp", ikd[0:64, :], IK[:, :], b_ikd, [dB["IK"]], [b_ikd])
                K.dma("sp", vd[:], VD.rearrange("(j p) e -> p j e", p=P), b_vd, [dB["VD"]], [b_vd])
                rctr = [0]
                pctr = [0]

                def build_dg(i):
                    s = i % 2
                    for h in range(8):
                        K.op("dve", lambda e, h=h: e.tensor_scalar(out=dg[s][:, h, :], in0=ident16[:], scalar1=wsg[:, i, h:h + 1], scalar2=None, op0=ALU.mult),
                             [b_c16, b_wab], (), [b_dg[s]])

                def stageA(i):
                    s = i % 2
                    K.dma("sp", iqt[s][0:64, :, :], IQ[:, :, i * P:(i + 1) * P], b_iqt[s], [dB["IQ"]], [b_iqt[s]])
                    if i == 0:
                        build_dg(0)
                    if i + 1 < NT:
                        build_dg(i + 1)
                    yield
                    L = (i + 1) * P
                    nkc = (L + 511) // 512
                    for kc in range(nkc):
                        w = min(512, L - kc * 512)
                        ksl = slice(kc * 512, kc * 512 + w)
                        SCB = 2
                        def qk(h):
                            K.op("pe", lambda e: e.matmul(psum[:, h % 2, 0:w], lhsT=iqt[s][:, h, :], rhs=ikd[:, ksl], start=True, stop=True,
                                                          skip_group_check=True), [b_iqt[s], b_ikd], [PB[h % 2]])
                        qk(0)
                        for h in range(8):
                            bank = h % 2
                            if h + 1 < 8:
                                qk(h + 1)
                            ri = rctr[0] % NRT
                            rctr[0] += 1
                            if h % 2 == 0:
                                K.op("act", lambda e, h=h, bank=bank, ri=ri: e.activation(out=rt[ri][:, 0:w], in_=psum[:, bank, 0:w], func=AF.Relu, scale=wab[:, i, h:h + 1]),
                                     [PB[bank], b_wab], [b_rt[ri]])
                            else:
                                K.op("dve", lambda e, h=h, bank=bank, ri=ri: e.tensor_scalar(out=rt[ri][:, 0:w], in0=psum[:, bank, 0:w], scalar1=wab[:, i, h:h + 1], scalar2=0.0,
                                                                                             op0=ALU.mult, op1=ALU.max), [PB[bank], b_wab], [b_rt[ri]])
                            if h == 0:
                                K.op("pe", lambda e, ri=ri: e.matmul(psum[:, SCB, 0:w], lhsT=dg[s][:, 0, :], rhs=rt[ri][:, 0:w], start=True, stop=False, skip_group_check=True),
                                     [b_dg[s], b_rt[ri]], [PB[SCB]])
                            else:
                                K.op("pe", lambda e, h=h, ri=ri: e.matmul(psum[:, SCB, 0:w], lhsT=dg[s][:, h, :], rhs=rt[ri][:, 0:w], start=False, stop=False, skip_group_check=True),
                                     [b_dg[s], b_rt[ri]], (), [PB[SCB]])
                            if h % 2 == 1:
                                yield
                        if kc == nkc - 1:
                            d0 = i * P - kc * 512
                            K.op("pe", lambda e: e.matmul(psum[:, SCB, d0:d0 + P], lhsT=ident16[:], rhs=caus16[:], start=False, stop=True, skip_group_check=True),
                                 [b_c16], (), [PB[SCB]])
                        K.op("act", lambda e: e.activation(out=acc[s][:, ksl], in_=psum[:, SCB, 0:w], func=AF.Identity), [PB[SCB]], (), [b_acc[s]])
                        yield

                def stageB(i):
                    s = i % 2
                    L = (i + 1) * P
                    K.dma("sp", qdt[s][0:64, :, :], QD[:, :, i * P:(i + 1) * P], b_qdt[s], [dB["QD"]], [b_qdt[s]])
                    K.dma("sp", gdt[s][:], GD[i * P:(i + 1) * P, :], b_gdt[s], [dB["GD"]], [b_gdt[s]])
                    if debug and "SC" in dbg:
                        K.dma("sp", dbg["SC"][i * P:(i + 1) * P, 0:L], acc[s][:, 0:L], s_misc, [b_acc[s]], ())
                    if i >= 2:
                        K.op("dve", lambda e: e.tensor_reduce(out=sm[:, 0:1], in_=acc[s][:, 0:i * P], axis=mybir.AxisListType.X, op=ALU.min), [b_acc[s]], [b_sm])
                        K.op("dve", lambda e: e.tensor_reduce(out=sm[:, 1:2], in_=acc[s][:, 0:L], axis=mybir.AxisListType.X, op=ALU.max), [b_acc[s]], [b_sm])
                        yield
                        K.op("dve", lambda e: e.tensor_tensor(out=sm[:, 2:3], in0=sm[:, 1:2], in1=sm[:, 0:1], op=ALU.subtract), [b_sm], [b_sm])
                        use_act = (i % 2 == 0)
                        if use_act:
                            K.op("dve", lambda e: e.tensor_scalar(out=nW[:], in0=pow2[:], scalar1=sm[:, 2:3], scalar2=-1.0, op0=ALU.mult, op1=ALU.mult), [b_sm, b_const], [b_W])
                            K.op("dve", lambda e: e.tensor_scalar(out=hW[:], in0=pow2[:], scalar1=sm[:, 2:3], scalar2=0.5, op0=ALU.mult, op1=ALU.mult), [b_sm, b_const], (), [b_W])
                            K.op("dve", lambda e: e.scalar_tensor_tensor(out=sm[:, 4:5], in0=sm[:, 0:1], scalar=-1.0, in1=nW[:, 0:1], op0=ALU.mult, op1=ALU.add), [b_W], [b_sm])
                            for k in range(NIT):
                                K.op("act", lambda e: e.activation(out=Mb[:, 0:L], in_=acc[s][:, 0:L], func=AF.Sign, bias=sm[:, 4:5], scale=1.0, accum_out=sm[:, 5:6]),
                                     [b_acc[s], b_sm], [b_M, b_sm])
                                K.op("dve", lambda e, k=k: e.scalar_tensor_tensor(out=sm[:, 6:7], in0=sm[:, 5:6], scalar=float(511 - L), in1=nW[:, k:k + 1], op0=ALU.is_ge, op1=ALU.mult),
                                     [b_W], [b_sm])
                                K.op("dve", lambda e, k=k: e.scalar_tensor_tensor(out=sm[:, 4:5], in0=sm[:, 6:7], scalar=hW[:, k:k + 1], in1=sm[:, 4:5], op0=ALU.add, op1=ALU.add),
                                     [b_W], [b_sm])
                                yield
                            K.op("dve", lambda e: e.scalar_tensor_tensor(out=sm[:, 3:4], in0=sm[:, 4:5], scalar=-1.0, in1=hW[:, NIT - 1:NIT], op0=ALU.mult, op1=ALU.subtract),
                                 [b_W], [b_sm])
                        else:
                            K.op("dve", lambda e: e.tensor_scalar(out=nW[:], in0=pow2[:], scalar1=sm[:, 2:3], scalar2=None, op0=ALU.mult), [b_sm, b_const], [b_W])
                            K.op("dve", lambda e: e.tensor_scalar(out=hW[:], in0=pow2[:], scalar1=sm[:, 2:3], scalar2=-0.5, op0=ALU.mult, op1=ALU.mult), [b_sm, b_const], (), [b_W])
                            K.op("dve", lambda e: e.tensor_tensor(out=sm[:, 4:5], in0=sm[:, 0:1], in1=nW[:, 0:1], op=ALU.add), [b_W], [b_sm])
                            for k in range(NIT):
                                K.op("dve", lambda e: e.tensor_scalar(out=Mb[:, 0:L], in0=acc[s][:, 0:L], scalar1=sm[:, 4:5], scalar2=None, op0=ALU.is_ge, op1=ALU.add,
                                                                      accum_out=sm[:, 5:6]), [b_acc[s], b_sm], [b_M, b_sm])
                                K.op("dve", lambda e, k=k: e.scalar_tensor_tensor(out=sm[:, 6:7], in0=sm[:, 5:6], scalar=255.5, in1=nW[:, k:k + 1], op0=ALU.is_ge, op1=ALU.mult),
                                     [b_W], [b_sm])
                                K.op("dve", lambda e, k=k: e.scalar_tensor_tensor(out=sm[:, 4:5], in0=sm[:, 6:7], scalar=hW[:, k:k + 1], in1=sm[:, 4:5], op0=ALU.add, op1=ALU.add),
                                     [b_W], [b_sm])
                                yield
                            K.op("dve", lambda e: e.tensor_tensor(out=sm[:, 3:4], in0=sm[:, 4:5], in1=hW[:, NIT - 1:NIT], op=ALU.add), [b_W], [b_sm])
                    else:
                        K.op("dve", lambda e: e.memset(sm[:, 3:4], -20000.0), (), [b_sm])
                    if debug and "TH" in dbg:
                        K.dma("sp", dbg["TH"][i * P:(i + 1) * P, :], sm[:, 3:4], s_misc, [b_sm], ())
                    K.op("dve", lambda e: e.tensor_scalar(out=Mb[:, 0:L], in0=acc[s][:, 0:L], scalar1=sm[:, 3:4], scalar2=None, op0=ALU.is_ge), [b_acc[s], b_sm], [b_M])
                    yield
                    for g in range((i + 8) // 8):
                        nj = min(8, i + 1 - g * 8)
                        bank = 3
                        pbf = psum[:, bank, :].bitcast(BF16)
                        for jj in range(nj):
                            J = g * 8 + jj
                            K.op("pe", lambda e, jj=jj, J=J, pbf=pbf: e.transpose(out=pbf[:, jj * P:(jj + 1) * P], in_=Mb[:, J * P:(J + 1) * P], identity=ident[:]),
                                 [b_M, b_const], [PB[bank]])
                        K.op("act", lambda e, g=g, nj=nj, pbf=pbf: e.activation(out=MT[s][:, g * 8:g * 8 + nj, :], in_=pbf[:, 0:nj * P].rearrange("p (a b) -> p a b", a=nj),
                                                                                func=AF.Copy), [PB[bank]], (), [b_MT[s]])
                        yield

                def stageC(i):
                    s = i % 2
                    oA = psum[:, 6, 0:260].rearrange("p (a b) -> p a b", a=4)
                    oB = psum[:, 7, 0:260].rearrange("p (a b) -> p a b", a=4)
                    firstA = [True, True]
                    pend = None
                    for J in range(i + 1):
                        for half in range(2):
                            K.op("pe", lambda e, half=half: e.matmul(psum[:, 4 + half, :].rearrange("p (a b) -> p a b", a=4), lhsT=kd[:, J * P:(J + 1) * P],
                                                                     rhs=qdt[s][:, half * 4:(half + 1) * 4, :], start=True, stop=True, skip_group_check=True),
                                 [b_kd, b_qdt[s]], [PB[4 + half]])
                        ps_ = pctr[0] % 3
                        pctr[0] += 1
                        K.op("act", lambda e, ps_=ps_: e.activation(out=pTd[ps_][:].rearrange("p a b -> p (a b)"), in_=psum[:, 4:6, :].rearrange("p a b -> p (a b)"), func=AF.Exp),
                             [PB[4], PB[5]], [b_pTd[ps_]])
                        K.op("dve", lambda e, ps_=ps_: e.tensor_tensor(out=pTd[ps_][:], in0=pTd[ps_][:], in1=MT[s][:, J:J + 1, :].to_broadcast([P, 8, P]), op=ALU.mult),
                             [b_MT[s]], [b_pTd[ps_]])
                        if pend is not None:
                            Jp, pp = pend
                            for h in range(8):
                                o_ = oA if h < 4 else oB
                                hb = 0 if h < 4 else 1
                                st_ = firstA[hb]
                                firstA[hb] = False
                                K.op("pe", lambda e, h=h, o_=o_, st_=st_, Jp=Jp, pp=pp: e.matmul(o_[:, h % 4, :], lhsT=pTd[pp][:, h, :], rhs=vd[:, Jp, :], start=st_, stop=False,
                                                                                                 skip_group_check=True), [b_pTd[pp], b_vd], (), [PB[6 + hb]])
                        pend = (J, ps_)
                        yield
                    Jp, pp = pend
                    for h in range(8):
                        o_ = oA if h < 4 else oB
                        hb = 0 if h < 4 else 1
                        st_ = firstA[hb]
                        firstA[hb] = False
                        K.op("pe", lambda e, h=h, o_=o_, st_=st_: e.matmul(o_[:, h % 4, :], lhsT=pTd[pp][:, h, :], rhs=vd[:, Jp, :], start=st_, stop=False,
                                                                           skip_group_check=True), [b_pTd[pp], b_vd], (), [PB[6 + hb]])
                    K.op("dve", lambda e: e.reciprocal(out=rden[:, 0:4], in_=oA[:, :, 64]), [PB[6]], [b_rden])
                    K.op("dve", lambda e: e.reciprocal(out=rden[:, 4:8], in_=oB[:, :, 64]), [PB[7]], [b_rden])
                    for h in range(8):
                        o_ = oA if h < 4 else oB
                        K.op("dve", lambda e, h=h, o_=o_: e.scalar_tensor_tensor(out=yd[s][:, h * 64:(h + 1) * 64], in0=o_[:, h % 4, 0:64], scalar=rden[:, h:h + 1],
                                                                                 in1=gdt[s][:, h * 64:(h + 1) * 64], op0=ALU.mult, op1=ALU.mult),
                             [PB[6 + (h // 4)], b_rden, b_gdt[s]], (), [b_yd[s]])
                    K.dma("sp", YD[i * P:(i + 1) * P, :], yd[s][:], b_yd[s], [b_yd[s]], (), [dB["YD"]])
                    yield

                for n in range(NT + 2):
                    gens = []
                    if n < NT:
                        gens.append(stageA(n))
                    if 0 <= n - 1 < NT and DSA_STAGES >= 2:
                        gens.append(stageB(n - 1))
                    if 0 <= n - 2 < NT and DSA_STAGES >= 3:
                        gens.append(stageC(n - 2))
                    while gens:
                        for g in list(gens):
                            try:
                                next(g)
                            except StopIteration:
                                gens.remove(g)
                K.barrier()
            if stop_after == "C2":
                break

            with ExitStack() as cs:
                wbr = [sbt(cs, "wbr%d" % b, [P, 4, D], BF16) for b in range(3)]
                wo = sbt(cs, "wo", [P, 8, D], BF16)
                b_wD = Buf("wD")
                gfin = sbt(cs, "gfin", [P, D], F32)
                yt = [sbt(cs, "yt%d" % i, [P, 3, 512], BF16) for i in range(2)]; b_yt = [Buf("yt0"), Buf("yt1")]
                mg = [sbt(cs, "mg%d" % i, [P, 3 * D], BF16) for i in range(2)]; b_mg = [Buf("mg0"), Buf("mg1")]
                xt = [sbt(cs, "xtD%d" % i, [P, D], F32) for i in range(2)]; b_xt = [Buf("xtD0"), Buf("xtD1")]
                yT = sbt(cs, "yT", [P, 12, P], BF16); b_yT = Buf("yT")
                mx = [sbt(cs, "mx%d" % b, [P, D], F32) for b in range(3)]; b_mx = [Buf("mx%d" % b) for b in range(3)]
                mxb = sbt(cs, "mxb", [P, D], BF16); b_mxb = Buf("mxb")
                mT = sbt(cs, "mT", [P, 8, P], BF16); b_mT = Buf("mT")
                xo = [sbt(cs, "xo%d" % i, [P, D], F32) for i in range(2)]; b_xo = [Buf("xo0"), Buf("xo1")]
                junk = sbt(cs, "junkE", [P, D], F32); b_junk = Buf("junkE")
                st = sbt(cs, "statD", [P, 4], F32); b_st = Buf("statD")
                b_wDs = [Buf("wD%d" % i) for i in range(5)]
                for b in range(3):
                    K.dma("pool", wbr[b][:], w_brs[b][l, :, :].rearrange("(kc p) n -> p kc n", p=P), b_wDs[b], (), (), [b_wD])
                K.dma("pool", wo[:], w_out[l, :, :].rearrange("(kc p) n -> p kc n", p=P), b_wDs[3], (), (), [b_wD])
                if last:
                    K.dma("sp", gfin[:], g_final[0:1, :].to_broadcast([P, D]), b_wDs[4], (), (), [b_wD])
                ysrc = [YF, YD, YS]
                ybuf = [dB["YF"], dB["YD"], dB["YS"]]
                b_xdst = Buf("xdst", loose=True)

                def e_load(i):
                    s = i % 2
                    for b in range(3):
                        K.dma("sp", yt[s][:, b, :], ysrc[b][i * P:(i + 1) * P, :], b_yt[s], [ybuf[b]], (), [b_yt[s]])
                    K.dma("sp", mg[s][:], MG[i * P:(i + 1) * P, :], b_mg[s], [dB["MG"]], [b_mg[s]])
                    K.dma("sp", xt[s][:], x_cur[i * P:(i + 1) * P, :], b_xt[s], [b_xsrc], [b_xt[s]])

                e_load(0)
                for i in range(NT):
                    s = i % 2
                    if i + 1 < NT:
                        e_load(i + 1)
                    for g in range(2):
                        bank = g
                        pbf = psum[:, bank, :].bitcast(BF16)
                        for jj in range(6):
                            q = g * 6 + jj
                            b, kc = q // 4, q % 4
                            K.op("pe", lambda e, jj=jj, b=b, kc=kc, pbf=pbf: e.transpose(out=pbf[:, jj * P:(jj + 1) * P], in_=yt[s][:, b, kc * P:(kc + 1) * P], identity=ident[:]),
                                 [b_yt[s], b_const], [PB[bank]])
                        K.op("act", lambda e, g=g, pbf=pbf: e.activation(out=yT[:, g * 6:(g + 1) * 6, :], in_=pbf[:, 0:6 * P].rearrange("p (a b) -> p a b", a=6), func=AF.Copy),
                             [PB[bank]], (), [b_yT])
                    for b in range(3):
                        for half in range(2):
                            bank = 2 + half
                            for kc in range(4):
                                K.op("pe", lambda e, b=b, half=half, kc=kc, bank=bank: e.matmul(psum[:, bank, :], lhsT=yT[:, b * 4 + kc, :], rhs=wbr[b][:, kc, half * 512:(half + 1) * 512],
                                                                                                start=(kc == 0), stop=(kc == 3), skip_group_check=True),
                                     [b_yT, b_wD], [PB[bank]])
                            K.op("dve", lambda e, b=b, half=half, bank=bank: e.tensor_tensor(out=mx[b][:, half * 512:(half + 1) * 512], in0=psum[:, bank, :],
                                                                                             in1=mg[s][:, b * D + half * 512: b * D + (half + 1) * 512], op=ALU.mult),
                                 [PB[bank], b_mg[s]], (), [b_mx[b]])
                    K.op("pool", lambda e: e.tensor_tensor(out=mx[0][:], in0=mx[0][:], in1=mx[1][:], op=ALU.add), [b_mx[1]], [b_mx[0]])
                    K.op("pool", lambda e: e.tensor_tensor(out=mxb[:], in0=mx[0][:], in1=mx[2][:], op=ALU.add), [b_mx[0], b_mx[2]], [b_mxb])
                    bank = 4
                    pbf = psum[:, bank, :].bitcast(BF16)
                    for kc in range(8):
                        K.op("pe", lambda e, kc=kc, pbf=pbf: e.transpose(out=pbf[:, kc * P:(kc + 1) * P], in_=mxb[:, kc * P:(kc + 1) * P], identity=ident[:]),
                             [b_mxb, b_const], [PB[bank]])
                    K.op("act", lambda e, pbf=pbf: e.activation(out=mT[:], in_=pbf.rearrange("p (a b) -> p a b", a=8), func=AF.Copy), [PB[bank]], [b_mT])
                    for half in range(2):
                        bank = 5 + half
                        for kc in range(8):
                            K.op("pe", lambda e, half=half, kc=kc, bank=bank: e.matmul(psum[:, bank, :], lhsT=mT[:, kc, :], rhs=wo[:, kc, half * 512:(half + 1) * 512],
                                                                                       start=(kc == 0), stop=(kc == 7), skip_group_check=True), [b_mT, b_wD], [PB[bank]])
                        hs_ = slice(half * 512, (half + 1) * 512)
                        K.op("dve", lambda e, bank=bank, hs_=hs_: e.tensor_tensor(out=xo[s][:, hs_], in0=psum[:, bank, :], in1=gate_rep[:, hs_], op=ALU.mult),
                             [PB[bank], b_mod], (), [b_xo[s]])
                    K.op("pool", lambda e: e.tensor_tensor(out=xo[s][:], in0=xo[s][:], in1=xt[s][:], op=ALU.add), [b_xt[s]], [b_xo[s]])
                    if last:
                        K.op("act", lambda e: e.activation(out=junk[:], in_=xo[s][:], func=AF.Square, accum_out=st[:, 0:1]), [b_xo[s]], [b_junk, b_st])
                        K.op("dve", lambda e: e.tensor_scalar(out=st[:, 1:2], in0=st[:, 0:1], scalar1=float(1.0 / D), scalar2=EPS, op0=ALU.mult, op1=ALU.add), [b_st], [b_st])
                        K.op("act", lambda e: e.activation(out=st[:, 2:3], in_=st[:, 1:2], func=AF.Sqrt), [b_st], [b_st])
                        K.op("dve", lambda e: e.reciprocal(out=st[:, 3:4], in_=st[:, 2:3]), [b_st], [b_st])
                        K.op("dve", lambda e: e.scalar_tensor_tensor(out=xo[s][:], in0=xo[s][:], scalar=st[:, 3:4], in1=gfin[:], op0=ALU.mult, op1=ALU.mult),
                             [b_st, b_wD], [b_xo[s]])
                    K.dma("sp", x_dst[i * P:(i + 1) * P, :], xo[s][:], b_xo[s], [b_xo[s]], (), [b_xdst])
                K.barrier()
            b_xsrc = b_xdst
            x_cur = x_dst
        K.barrier(engines=("sp",))
        print("semaphores used:", K.nsem, "instr counts:", {n: E.cnt for n, E in K.E.items()})
    return nc


def _rot_cols(w, nheads):
    n = w.shape[-1]
    idx = np.arange(n).reshape(nheads, 2, 32)[:, ::-1, :].reshape(-1)
    return w[..., idx]


def prep_shared(inputs):
    w_in = np.ascontiguousarray(inputs["w_in"], dtype=np.float32)
    w_rot = np.concatenate([
        _rot_cols(w_in[:, :, OFF["dq"]:OFF["dq"] + 512], 8),
        _rot_cols(w_in[:, :, OFF["diq"]:OFF["diq"] + 512], 8),
        _rot_cols(w_in[:, :, OFF["dik"]:OFF["dik"] + 64], 1)], axis=-1)
    w_kv = np.ascontiguousarray(inputs["w_kv_up"], dtype=np.float32)
    half = 32
    j = np.arange(P)
    invf = (np.float32(10000.0) ** (-(np.arange(half, dtype=np.float32)) / np.float32(half))).astype(np.float32)
    sgn = np.where((j % 64) < 32, -1.0, 1.0).astype(np.float32)
    invf_t = np.stack([invf[j % 32], sgn * invf[j % 32]], axis=1).astype(np.float32)
    return {
        "invf": np.ascontiguousarray(invf_t),
        "w_ada": np.ascontiguousarray(inputs["w_ada"], dtype=np.float32),
        "b_ada": np.ascontiguousarray(inputs["b_ada"], dtype=np.float32),
        "g_norm": np.ascontiguousarray(inputs["g_norm"], dtype=np.float32),
        "w_in": w_in,
        "w_rot": np.ascontiguousarray(w_rot),
        "b_fgt": np.ascontiguousarray(inputs["b_fgt"], dtype=np.float32).reshape(DEPTH, 8, 1),
        "g_kv": np.ascontiguousarray(inputs["g_kv"], dtype=np.float32).reshape(DEPTH, P, 1),
        "w_kv": w_kv,
        "w_kvr": np.ascontiguousarray(_rot_cols(w_kv[:, :, 0:64], 1)),
        "w_br_fox": np.ascontiguousarray(inputs["w_br_fox"], dtype=np.float32),
        "w_br_dsa": np.ascontiguousarray(inputs["w_br_dsa"], dtype=np.float32),
        "w_br_sb": np.ascontiguousarray(inputs["w_br_sb"], dtype=np.float32),
        "w_out": np.ascontiguousarray(inputs["w_out"], dtype=np.float32),
        "g_final": np.ascontiguousarray(inputs["g_final"], dtype=np.float32).reshape(1, D),
    }


def prep_core(inputs, b):
    c = np.asarray(inputs["c"][b], dtype=np.float32)
    crep = np.ascontiguousarray(np.broadcast_to(c.reshape(8, P).T[:, :, None], (P, 8, P)))
    return {
        "x": np.ascontiguousarray(inputs["x"][b], dtype=np.float32),
        "crep": crep,
        "pos": np.ascontiguousarray(inputs["positions"][b], dtype=np.int32).reshape(1, S),
    }


def kernel(**inputs):
    shared = prep_shared(inputs)
    nb = inputs["x"].shape[0]
    in_maps = []
    for b in range(nb):
        m = dict(shared)
        m.update(prep_core(inputs, b))
        in_maps.append(m)
    nc = build_program()
    res = run_bass_kernel_spmd(nc, in_maps, core_ids=list(range(nb)))
    return np.stack([np.asarray(r["out"], dtype=np.float32) for r in res.results], axis=0)
```

```python
import numpy as np
from contextlib import ExitStack
import concourse.bass as bass
import concourse.mybir as mybir
from concourse.bass_utils import run_bass_kernel_spmd

F32 = mybir.dt.float32
BF16 = mybir.dt.bfloat16
FP16 = mybir.dt.float16
I32 = mybir.dt.int32
AF = mybir.ActivationFunctionType
ALU = mybir.AluOpType

S = 4096
D = 1024
NT = 32
P = 128
DEPTH = 2
N_IN = 8912
OFF = dict(fq=0, fk=512, fv=1024, ff=1536, fg=1544, dq=2056, dckv=2568, diq=2696, dik=3208,
           diw=3272, dg=3280, sq=3792, sk=4304, sv=4816, sg=5328, merge=5840)
NEG = -30000.0
NIT = 14
import os
DSA_STAGES = int(os.environ.get('DSA_STAGES', '3'))
EPS = 1e-6
TWO_PI = 2.0 * np.pi
C1 = 6.28125
C2 = float(TWO_PI - 6.28125)


class Buf:
    __slots__ = ("name", "w", "r", "sem", "cnt", "loose", "q")

    def __init__(self, name, loose=False):
        self.name = name
        self.loose = loose
        self.w = {}
        self.r = {}
        self.sem = None
        self.cnt = 0


class Eng:
    def __init__(self, name, eng, sem, self_sync):
        self.name = name
        self.eng = eng
        self.sem = sem
        self.cnt = 0
        self.waited = {}
        self.self_sync = self_sync


class KB:
    def __init__(self, nc, es):
        self.nc = nc
        self.es = es
        self.nsem = 0
        self.E = {}
        for n, e, ss in [("pe", nc.tensor, False), ("act", nc.scalar, True), ("dve", nc.vector, True),
                         ("pool", nc.gpsimd, True), ("sp", nc.sync, True)]:
            self.E[n] = Eng(n, e, self.new_sem("e_" + n), ss)
        self.slots = {}
        self.all_slots = []
        self.free_sems = {"sp": [], "pool": [], "act": []}

    def new_sem(self, name):
        self.nsem += 1
        return self.es.enter_context(self.nc.semaphore("%s_%d" % (name, self.nsem)))

    def _wait(self, E, deps):
        for sid, (sem, val) in deps.items():
            slot = self.slots.get(sid)
            if slot is not None:
                val = slot.cnt
            elif sem is E.sem and not E.self_sync:
                continue
            if E.waited.get(sid, 0) >= val:
                continue
            E.eng.wait_ge(sem, val)
            E.waited[sid] = val

    @staticmethod
    def _merge(d, src):
        for sid, ev in src.items():
            o = d.get(sid)
            if o is None or o[1] < ev[1]:
                d[sid] = ev

    def _deps(self, reads, writes, add, own_sid=None):
        deps = {}
        for b in reads:
            self._merge(deps, b.w)
        for b in writes:
            self._merge(deps, b.w)
            self._merge(deps, b.r)
        for b in add:
            self._merge(deps, b.r)
            if not b.loose:
                for sid, ev in b.w.items():
                    if sid != own_sid:
                        self._merge(deps, {sid: ev})
        return deps

    def _commit(self, ev, reads, writes, add):
        sid = id(ev[0])
        for b in writes:
            b.w = {sid: ev}
            b.r = {}
        for b in add:
            b.w[sid] = ev
        for b in reads:
            b.r[sid] = ev

    def op(self, e, fn, reads=(), writes=(), add=()):
        E = self.E[e]
        self._wait(E, self._deps(reads, writes, add, id(E.sem)))
        inst = fn(E.eng)
        E.cnt += 1
        inst.then_inc(E.sem, 1)
        self._commit((E.sem, E.cnt), reads, writes, add)

    def dma(self, q, out, in_, slot, reads=(), writes=(), add=()):
        E = self.E[q]
        if slot.sem is None:
            if self.free_sems[q]:
                slot.sem, slot.cnt = self.free_sems[q].pop()
            else:
                slot.sem, slot.cnt = self.new_sem("d" + q), 0
            slot.q = q
            self.slots[id(slot.sem)] = slot
            self.all_slots.append(slot)
        assert slot.q == q, "DMA slot %s used from two queues" % slot.name
        self._wait(E, self._deps(reads, writes, add, id(slot.sem)))
        inst = E.eng.dma_start(out=out, in_=in_)
        slot.cnt += 16
        inst.then_inc(slot.sem, 16)
        self._commit((slot.sem, slot.cnt), reads, writes, add)

    def barrier(self, engines=("pe", "act", "dve", "pool", "sp")):
        deps = {}
        for n, X in self.E.items():
            if X.cnt > 0:
                deps[id(X.sem)] = (X.sem, X.cnt)
        for s in self.all_slots:
            if s.cnt > 0:
                deps[id(s.sem)] = (s.sem, s.cnt)
        for n in engines:
            E = self.E[n]
            ss = E.self_sync
            E.self_sync = True
            self._wait(E, deps)
            E.self_sync = ss
        if len(engines) == 5:
            for sl in self.all_slots:
                self.free_sems[sl.q].append((sl.sem, sl.cnt))
                del self.slots[id(sl.sem)]
                sl.sem = None
            self.all_slots = []


def act_kwargs(**kw):
    return {k: v for k, v in kw.items() if v is not None}


def build_program(nlayers=DEPTH, debug=False, stop_after=None):
    nc = bass.Bass("TRN2", target_bir_lowering=False)
    dt_in = lambda name, shape, dt=F32: nc.dram_tensor(name, list(shape), dt, kind="ExternalInput").ap()
    skind = "ExternalOutput" if debug else "Internal"
    dt_sc = lambda name, shape, dt=BF16: nc.dram_tensor(name, list(shape), dt, kind=skind).ap()

    x_in = dt_in("x", [S, D])
    crep_in = dt_in("crep", [P, 8, P])
    pos_in = dt_in("pos", [1, S], I32)
    invf_in = dt_in("invf", [P, 2])
    w_ada = dt_in("w_ada", [DEPTH, D, 3 * D])
    b_ada = dt_in("b_ada", [DEPTH, 3 * D])
    g_norm = dt_in("g_norm", [DEPTH, D])
    w_in = dt_in("w_in", [DEPTH, D, N_IN])
    w_rot = dt_in("w_rot", [DEPTH, D, 1088])
    b_fgt = dt_in("b_fgt", [DEPTH, 8, 1])
    g_kv = dt_in("g_kv", [DEPTH, P, 1])
    w_kv = dt_in("w_kv", [DEPTH, P, P])
    w_kvr = dt_in("w_kvr", [DEPTH, P, 64])
    w_brs = [dt_in("w_br_fox", [DEPTH, 512, D]), dt_in("w_br_dsa", [DEPTH, 512, D]), dt_in("w_br_sb", [DEPTH, 512, D])]
    w_out = dt_in("w_out", [DEPTH, D, D])
    g_final = dt_in("g_final", [1, D])
    out_d = nc.dram_tensor("out", [S, D], F32, kind="ExternalOutput").ap()

    XR = dt_sc("XR", [S, D], F32)
    QF = dt_sc("QF", [8, 68, S]); KF = dt_sc("KF", [8, 68, S]); VF = dt_sc("VF", [S, 8, 65]); GF = dt_sc("GF", [S, 512])
    QD = dt_sc("QD", [64, 8, S]); KD = dt_sc("KD", [64, S]); VD = dt_sc("VD", [S, 65]); GD = dt_sc("GD", [S, 512])
    IQ = dt_sc("IQ", [64, 8, S], FP16); IK = dt_sc("IK", [64, S], FP16)
    QS = dt_sc("QS", [8, 64, S]); KS = dt_sc("KS", [8, 64, S]); VS = dt_sc("VS", [S, 512]); GS = dt_sc("GS", [S, 512])
    MG = dt_sc("MG", [S, 3 * D])
    YF = dt_sc("YF", [S, 512]); YD = dt_sc("YD", [S, 512]); YS = dt_sc("YS", [S, 512])
    dbg = {}
    if debug:
        dbg["WAB"] = dt_sc("WAB", [P, NT, 8], F32)
        dbg["WSG"] = dt_sc("WSG", [P, NT, 8], F32)
        dbg["HT"] = dt_sc("HT", [P, 8, S], BF16)
        dbg["TAB"] = dt_sc("TAB", [P, 2, S], F32)
        dbg["SC"] = dt_sc("SC", [S, S], F32)
        dbg["TH"] = dt_sc("TH", [S, 1], F32)

    es = ExitStack()
    with es:
        K = KB(nc, es)
        uid = [0]

        def sbt(ctx, n, s, d):
            uid[0] += 1
            return ctx.enter_context(nc.sbuf_tensor("%s_%d" % (n, uid[0]), list(s), d))

        psum = es.enter_context(nc.psum_tensor("psum", [P, 8, 512], F32))
        PB = [Buf("pb%d" % i) for i in range(8)]
        ident = sbt(es, "ident", [P, P], BF16); b_ident = Buf("ident")
        tri_le = sbt(es, "tri_le", [P, P], BF16)
        tri_lt = sbt(es, "tri_lt", [P, P], BF16)
        negU = sbt(es, "negU", [P, P], BF16)
        negones = sbt(es, "negones", [P, P], BF16)
        ones_bf = sbt(es, "ones_bf", [P, P], BF16)
        caus = sbt(es, "caus", [P, P], F32)
        pow2 = sbt(es, "pow2", [P, NIT], F32)
        invf = sbt(es, "invf_sb", [P, 2], F32)
        CT = sbt(es, "CT", [P, S], F32)
        SS = sbt(es, "SS", [P, S], F32)
        b_const = Buf("const")
        b_tab = Buf("tab")
        A_rep = sbt(es, "A_rep", [P, D], F32); shift_rep = sbt(es, "shift_rep", [P, D], F32)
        gate_rep = sbt(es, "gate_rep", [P, D], F32)
        b_mod = Buf("mod")
        wab = sbt(es, "wab", [P, NT, 8], F32); wsg = sbt(es, "wsg", [P, NT, 8], F32)
        b_wab = Buf("wab")

        def pool_op(fn, writes):
            K.op("pool", fn, (), writes)
        pool_op(lambda e: e.memset(ident[:], 1.0), [b_const])
        pool_op(lambda e: e.affine_select(out=ident[:], in_=ident[:], pattern=[[1, P]], compare_op=ALU.is_equal,
                                          fill=0.0, base=0, channel_multiplier=-1), [b_const])
        pool_op(lambda e: e.memset(tri_le[:], 0.0), [b_const])
        pool_op(lambda e: e.affine_select(out=tri_le[:], in_=tri_le[:], pattern=[[1, P]], compare_op=ALU.is_ge,
                                          fill=NEG, base=0, channel_multiplier=-1), [b_const])
        pool_op(lambda e: e.memset(tri_lt[:], 0.0), [b_const])
        pool_op(lambda e: e.affine_select(out=tri_lt[:], in_=tri_lt[:], pattern=[[1, P]], compare_op=ALU.is_ge,
                                          fill=NEG, base=-1, channel_multiplier=-1), [b_const])
        pool_op(lambda e: e.memset(negU[:], -1.0), [b_const])
        pool_op(lambda e: e.affine_select(out=negU[:], in_=negU[:], pattern=[[-1, P]], compare_op=ALU.is_ge,
                                          fill=0.0, base=0, channel_multiplier=1), [b_const])
        pool_op(lambda e: e.memset(negones[:], -1.0), [b_const])
        pool_op(lambda e: e.memset(ones_bf[:], 1.0), [b_const])
        pool_op(lambda e: e.memset(caus[:], 0.0), [b_const])
        pool_op(lambda e: e.affine_select(out=caus[:], in_=caus[:], pattern=[[-1, P]], compare_op=ALU.is_ge,
                                          fill=-1e30, base=0, channel_multiplier=1), [b_const])
        for k in range(NIT):
            pool_op(lambda e, k=k: e.memset(pow2[:, k:k + 1], float(2.0 ** -(k + 1))), [b_const])
        s_misc = Buf("misc")
        K.dma("sp", invf[:], invf_in[:, :], s_misc, (), [b_const])

        with ExitStack() as cs:
            pi_t = sbt(cs, "pi_t", [P, 512], I32); pf_t = sbt(cs, "pf_t", [P, 512], F32)
            a_t = sbt(cs, "a_t", [P, 512], F32); k_t = sbt(cs, "k_t", [P, 512], F32)
            ki_t = sbt(cs, "ki_t", [P, 512], I32); r_t = sbt(cs, "r_t", [P, 512], F32)
            b_pi = Buf("pi"); b_pf = Buf("pf"); b_a = Buf("a"); b_k = Buf("k"); b_ki = Buf("ki"); b_r = Buf("r")
            for tcn in range(8):
                sl = slice(tcn * 512, (tcn + 1) * 512)
                K.dma("sp", pi_t[:], pos_in[0:1, sl].to_broadcast([P, 512]), b_pi, (), [b_pi])
                K.op("dve", lambda e: e.tensor_copy(out=pf_t[:], in_=pi_t[:]), [b_pi], [b_pf])
                for which in range(2):
                    tab = CT if which == 0 else SS
                    if which == 0:
                        K.op("dve", lambda e: e.tensor_scalar(out=a_t[:], in0=pf_t[:], scalar1=invf[:, 0:1], scalar2=float(np.pi / 2),
                                                              op0=ALU.mult, op1=ALU.add), [b_pf, b_const], [b_a])
                    else:
                        K.op("dve", lambda e: e.tensor_scalar(out=a_t[:], in0=pf_t[:], scalar1=invf[:, 1:2], scalar2=None,
                                                              op0=ALU.mult), [b_pf, b_const], [b_a])
                    K.op("dve", lambda e: e.tensor_scalar(out=k_t[:], in0=a_t[:], scalar1=float(1.0 / TWO_PI), scalar2=None,
                                                          op0=ALU.mult), [b_a], [b_k])
                    K.op("dve", lambda e: e.tensor_copy(out=ki_t[:], in_=k_t[:]), [b_k], [b_ki])
                    K.op("dve", lambda e: e.tensor_copy(out=k_t[:], in_=ki_t[:]), [b_ki], [b_k])
                    K.op("dve", lambda e: e.scalar_tensor_tensor(out=r_t[:], in0=k_t[:], scalar=-C1, in1=a_t[:],
                                                                 op0=ALU.mult, op1=ALU.add), [b_k, b_a], [b_r])
                    K.op("dve", lambda e: e.scalar_tensor_tensor(out=a_t[:], in0=k_t[:], scalar=-C2, in1=r_t[:],
                                                                 op0=ALU.mult, op1=ALU.add), [b_k, b_r], [b_a])
                    K.op("dve", lambda e: e.tensor_scalar(out=r_t[:], in0=a_t[:], scalar1=float(-np.pi), scalar2=float(np.pi),
                                                          op0=ALU.max, op1=ALU.min), [b_a], [b_r])
                    K.op("act", lambda e, tab=tab, sl=sl: e.activation(out=tab[:, sl], in_=r_t[:], func=AF.Sin), [b_r], (), [b_tab])
            if debug:
                K.dma("sp", dbg["TAB"][:, 0, :], CT[:], s_misc, [b_tab], ())
                K.dma("sp", dbg["TAB"][:, 1, :], SS[:], s_misc, [b_tab], ())
            K.barrier()

        x_cur = x_in
        b_xsrc = Buf("xsrc")
        for l in range(nlayers):
            last = (l == DEPTH - 1)
            x_dst = out_d if last else XR
            dB = {n: Buf("d_" + n, loose=True) for n in ["QF", "KF", "VF", "GF", "QD", "KD", "VD", "GD", "IQ", "IK", "QS", "KS", "VS", "GS",
                                             "MG", "YF", "YD", "YS"]}
            with ExitStack() as cs:
                hT = sbt(cs, "hT", [P, 8, S], BF16); b_hT = Buf("hT")
                NW = 2
                wbuf = [sbt(cs, "wbuf%d" % i, [P, 8, 512], BF16) for i in range(NW)]
                b_w = [Buf("wbuf%d" % i) for i in range(NW)]
                wrb = [sbt(cs, "wrb%d" % i, [P, 8, P], BF16) for i in range(NW)]
                b_wr = [Buf("wrb%d" % i) for i in range(NW)]
                wctr = [0]

                def load_w(src_ap, ncols, rot_ap=None):
                    i = wctr[0] % NW
                    wctr[0] += 1
                    K.dma("pool", wbuf[i][:, :, 0:ncols], src_ap.rearrange("(kc p) n -> p kc n", p=P), b_w[i], (), [b_w[i]])
                    if rot_ap is not None:
                        K.dma("pool", wrb[i][:, :, 0:ncols], rot_ap.rearrange("(kc p) n -> p kc n", p=P), b_wr[i], (), [b_wr[i]])
                    return i

                with ExitStack() as ms:
                    csb = sbt(ms, "csb", [P, 8, P], F32); scb = sbt(ms, "scb", [P, 8, P], BF16)
                    brep = [sbt(ms, "brep%d" % i, [P, 512], F32) for i in range(2)]
                    gnrep = sbt(ms, "gnrep", [P, D], F32)
                    mtmp = sbt(ms, "mtmp", [P, 512], F32)
                    b_c = Buf("csb"); b_sc = Buf("scb"); b_br = [Buf("brep0"), Buf("brep1")]; b_gn = Buf("gnrep"); b_mt = Buf("mtmp")
                    K.dma("sp", csb[:], crep_in[:, :, :], b_c, (), [b_c])
                    K.dma("sp", gnrep[:], g_norm[l:l + 1, :].to_broadcast([P, D]), b_gn, (), [b_gn])
                    K.op("act", lambda e: e.activation(out=scb[:], in_=csb[:], func=AF.Silu), [b_c], [b_sc])
                    for n in range(6):
                        wi = load_w(w_ada[l, :, n * 512:(n + 1) * 512], 512)
                        bi = n % 2
                        K.dma("sp", brep[bi][:], b_ada[l:l + 1, n * 512:(n + 1) * 512].to_broadcast([P, 512]), b_br[bi], (), [b_br[bi]])
                        bank = n % 2
                        for kc in range(8):
                            K.op("pe", lambda e, kc=kc, wi=wi, bank=bank: e.matmul(psum[:, bank, :], lhsT=scb[:, kc, :], rhs=wbuf[wi][:, kc, :],
                                                                                   start=(kc == 0), stop=(kc == 7), skip_group_check=True),
                                 [b_sc, b_w[wi]], [PB[bank]])
                        csl = slice((n % 2) * 512, (n % 2) * 512 + 512)
                        if n < 2:
                            K.op("dve", lambda e, bank=bank, bi=bi, csl=csl: e.tensor_tensor(out=shift_rep[:, csl], in0=psum[:, bank, :], in1=brep[bi][:], op=ALU.add),
                                 [PB[bank], b_br[bi]], (), [b_mod])
                        elif n < 4:
                            K.op("dve", lambda e, bank=bank, bi=bi: e.tensor_tensor(out=mtmp[:], in0=psum[:, bank, :], in1=brep[bi][:], op=ALU.add),
                                 [PB[bank], b_br[bi]], [b_mt])
                            K.op("dve", lambda e, csl=csl: e.scalar_tensor_tensor(out=A_rep[:, csl], in0=mtmp[:], scalar=1.0, in1=gnrep[:, csl],
                                                                                  op0=ALU.add, op1=ALU.mult), [b_mt, b_gn], (), [b_mod])
                        else:
                            K.op("dve", lambda e, bank=bank, bi=bi, csl=csl: e.tensor_tensor(out=gate_rep[:, csl], in0=psum[:, bank, :], in1=brep[bi][:], op=ALU.add),
                                 [PB[bank], b_br[bi]], (), [b_mod])
                    K.barrier()

                with ExitStack() as hs:
                    xt = [sbt(hs, "xt%d" % i, [P, D], F32) for i in range(2)]; b_xt = [Buf("xt0"), Buf("xt1")]
                    ht = [sbt(hs, "ht%d" % i, [P, D], BF16) for i in range(2)]; b_ht = [Buf("ht0"), Buf("ht1")]
                    junk = sbt(hs, "junkA", [P, D], F32); b_junk = Buf("junkA")
                    htmp = sbt(hs, "htmp", [P, D], F32); b_htmp = Buf("htmp")
                    st = sbt(hs, "statA", [P, 4], F32); b_st = Buf("statA")

                    def h_load(i):
                        K.dma("sp", xt[i % 2][:], x_cur[i * P:(i + 1) * P, :], b_xt[i % 2], [b_xsrc], [b_xt[i % 2]])

                    def hs1(i):
                        s = i % 2
                        K.op("act", lambda e: e.activation(out=junk[:], in_=xt[s][:], func=AF.Square, accum_out=st[:, 0:1]),
                             [b_xt[s]], [b_junk, b_st])
                        K.op("dve", lambda e: e.tensor_scalar(out=st[:, 1:2], in0=st[:, 0:1], scalar1=float(1.0 / D), scalar2=EPS,
                                                              op0=ALU.mult, op1=ALU.add), [b_st], [b_st])
                        K.op("act", lambda e: e.activation(out=st[:, 2:3], in_=st[:, 1:2], func=AF.Sqrt), [b_st], [b_st])
                        K.op("dve", lambda e: e.reciprocal(out=st[:, 3:4], in_=st[:, 2:3]), [b_st], [b_st])
                        K.op("dve", lambda e: e.scalar_tensor_tensor(out=htmp[:], in0=xt[s][:], scalar=st[:, 3:4], in1=A_rep[:],
                                                                     op0=ALU.mult, op1=ALU.mult), [b_xt[s], b_st, b_mod], [b_htmp])
                        K.op("pool", lambda e: e.tensor_tensor(out=ht[s][:], in0=htmp[:], in1=shift_rep[:], op=ALU.add),
                             [b_htmp, b_mod], [b_ht[s]])

                    def hs2(i):
                        s = i % 2
                        bank = 2 + (i % 2)
                        pbf = psum[:, bank, :].bitcast(BF16)
                        for kc in range(8):
                            K.op("pe", lambda e, kc=kc: e.transpose(out=pbf[:, kc * P:(kc + 1) * P], in_=ht[s][:, kc * P:(kc + 1) * P],
                                                                    identity=ident[:]), [b_ht[s], b_const], [PB[bank]])
                        K.op("act", lambda e: e.activation(out=hT[:, :, i * P:(i + 1) * P], in_=pbf.rearrange("p (a b) -> p a b", a=8),
                                                           func=AF.Copy), [PB[bank]], (), [b_hT])

                    h_load(0)
                    for i in range(NT + 1):
                        if i < NT:
                            if i + 1 < NT:
                                h_load(i + 1)
                            hs1(i)
                        if i >= 1:
                            hs2(i - 1)
                    if debug:
                        K.dma("sp", dbg["HT"][:, :, :], hT[:], s_misc, [b_hT], ())
                    K.barrier()

                with ExitStack() as ps_:
                    stg = [sbt(ps_, "stg%d" % i, [P, S], BF16) for i in range(2)]; b_stg = [Buf("stg0"), Buf("stg1")]
                    tstg = [sbt(ps_, "tstg%d" % i, [P, 520], BF16) for i in range(3)]; b_tstg = [Buf("tstg%d" % i) for i in range(3)]
                    t1 = [sbt(ps_, "t1_%d" % i, [P, 512], F32) for i in range(2)]; b_t1 = [Buf("t1_0"), Buf("t1_1")]
                    t2 = [sbt(ps_, "t2_%d" % i, [P, 512], F32) for i in range(2)]; b_t2 = [Buf("t2_0"), Buf("t2_1")]
                    ckvT = sbt(ps_, "ckvT", [P, 512], BF16); b_ckv = Buf("ckvT")
                    sqb = sbt(ps_, "sqb", [P, 512], BF16); b_sqb = Buf("sqb")
                    rstd = sbt(ps_, "rstd", [P, 512], F32); b_rstd = Buf("rstd")
                    dvT = sbt(ps_, "dvT", [P, 512], BF16); b_dvT = Buf("dvT")
                    wkv_f = sbt(ps_, "wkv_f", [P, P], F32); wkvr_f = sbt(ps_, "wkvr_f", [P, 64], F32)
                    wkv_b = sbt(ps_, "wkv_b", [P, P], BF16); wkvr_b = sbt(ps_, "wkvr_b", [P, 64], BF16)
                    gkv = sbt(ps_, "gkv", [P, 1], F32); b_wkv = Buf("wkv")
                    bf = sbt(ps_, "bfg", [8, 1], F32); b_bf = Buf("bfg")
                    etmp = sbt(ps_, "etmpB", [8, 512], F32); b_etmp = Buf("etmpB")
                    spc = sbt(ps_, "spc", [8, 512], F32); b_spc = Buf("spc")
                    fnc = [sbt(ps_, "fnc%d" % i, [8, 512], F32) for i in range(2)]; b_fnc = [Buf("fnc0"), Buf("fnc1")]
                    fr1 = sbt(ps_, "fr1", [8, 512], F32); b_fr1 = Buf("fr1")
                    fr2 = sbt(ps_, "fr2", [8, 512], F32); b_fr2 = Buf("fr2")
                    fbs = [sbt(ps_, "fbs%d" % i, [8, 512], BF16) for i in range(4)]; b_fbs = [Buf("fbs%d" % i) for i in range(4)]
                    for i in range(3):
                        K.op("pool", lambda e, i=i: e.memset(tstg[i][:], 1.0), (), [b_tstg[i]])
                    K.op("pool", lambda e: e.memset(stg[0][0:8, :], 1.0), (), [b_stg[0]])
                    for r_ in range(3):
                        K.dma("sp", QF[:, 65 + r_, :], stg[0][0:8, :], b_stg[0], [b_stg[0]], (), [dB["QF"]])
                    K.dma("sp", KF[:, 64, :], stg[0][0:8, :], b_stg[0], [b_stg[0]], (), [dB["KF"]])
                    K.dma("sp", bf[:], b_fgt[l, :, :], b_bf, (), [b_bf])
                    K.op("dve", lambda e: e.tensor_scalar(out=bf[:], in0=bf[:], scalar1=-1.0, scalar2=None, op0=ALU.mult), [b_bf], [b_bf])
                    K.dma("sp", wkv_f[:], w_kv[l, :, :], b_wkv, (), [b_wkv])
                    K.dma("sp", wkvr_f[:], w_kvr[l, :, :], b_wkv, (), [b_wkv])
                    K.dma("sp", gkv[:], g_kv[l, :, :], b_wkv, (), [b_wkv])
                    K.op("dve", lambda e: e.tensor_scalar(out=wkv_b[:], in0=wkv_f[:], scalar1=gkv[:, 0:1], scalar2=None, op0=ALU.mult), [b_wkv], [b_wkv])
                    K.op("dve", lambda e: e.tensor_scalar(out=wkvr_b[:], in0=wkvr_f[:], scalar1=gkv[:, 0:1], scalar2=None, op0=ALU.mult), [b_wkv], [b_wkv])
                    bctr = [0]

                    def next_bank():
                        b = bctr[0] % 4
                        bctr[0] += 1
                        return b

                    sctr = [0]
                    jobs = []

                    def fm_job(col0, M, rotcol0, post, dst_fn):
                        rot_ap = w_rot[l, :, rotcol0:rotcol0 + M] if rotcol0 is not None else None
                        box = {}

                        def load():
                            box["wi"] = load_w(w_in[l, :, col0:col0 + M], M, rot_ap)

                        def run():
                            wi = box["wi"]
                            si = sctr[0] % 2
                            sctr[0] += 1
                            for tcn in range(8):
                                sl = slice(tcn * 512, (tcn + 1) * 512)
                                bank = next_bank()
                                for kc in range(8):
                                    K.op("pe", lambda e, kc=kc, bank=bank, sl=sl: e.matmul(psum[0:M, bank, :], lhsT=wbuf[wi][:, kc, 0:M], rhs=hT[:, kc, sl],
                                                                                         start=(kc == 0), stop=(kc == 7), skip_group_check=True),
                                         [b_w[wi], b_hT], [PB[bank]])
                                bank2 = None
                                if rot_ap is not None:
                                    bank2 = next_bank()
                                    for kc in range(8):
                                        K.op("pe", lambda e, kc=kc, bank2=bank2, sl=sl: e.matmul(psum[0:M, bank2, :], lhsT=wrb[wi][:, kc, 0:M], rhs=hT[:, kc, sl],
                                                                                               start=(kc == 0), stop=(kc == 7), skip_group_check=True),
                                             [b_wr[wi], b_hT], [PB[bank2]])
                                post(tcn, sl, bank, bank2, si)
                            if dst_fn is not None:
                                dst_fn(si)
                        jobs.append((load, run))

                    def post_scale(scale):
                        def f(tcn, sl, bank, bank2, si, M=P):
                            K.op("act", lambda e: e.activation(out=stg[si][0:M, sl], in_=psum[0:M, bank, :], func=AF.Copy, scale=float(scale)),
                                 [PB[bank]], (), [b_stg[si]])
                        return f

                    def post_rope(scale, M, odt=None):
                        def f(tcn, sl, bank, bank2, si):
                            j = tcn % 2
                            so = stg[si][:, :] if odt is None else stg[si][:, :].bitcast(odt)
                            K.op("dve", lambda e: e.scalar_tensor_tensor(out=t1[j][0:M, :], in0=psum[0:M, bank, :], scalar=float(scale), in1=CT[0:M, sl],
                                                                         op0=ALU.mult, op1=ALU.mult), [PB[bank], b_tab], [b_t1[j]])
                            K.op("dve", lambda e: e.scalar_tensor_tensor(out=t2[j][0:M, :], in0=psum[0:M, bank2, :], scalar=float(scale), in1=SS[0:M, sl],
                                                                         op0=ALU.mult, op1=ALU.mult), [PB[bank2], b_tab], [b_t2[j]])
                            K.op("pool", lambda e: e.tensor_tensor(out=so[0:M, sl], in0=t1[j][0:M, :], in1=t2[j][0:M, :], op=ALU.add),
                                 [b_t1[j], b_t2[j]], (), [b_stg[si]])
                        return f

                    def store_heads(dst, pair, dbuf):
                        def f(si):
                            for hh in range(2):
                                K.dma("sp", dst[2 * pair + hh, 0:64, :], stg[si][hh * 64:(hh + 1) * 64, :], b_stg[si], [b_stg[si]], (), [dbuf])
                        return f

                    def store_fm_heads(dst, pair, dbuf, odt=None):
                        def f(si):
                            so = stg[si][:, :] if odt is None else stg[si][:, :].bitcast(odt)
                            for hh in range(2):
                                K.dma("sp", dst[:, 2 * pair + hh, :], so[hh * 64:(hh + 1) * 64, :], b_stg[si], [b_stg[si]], (), [dbuf])
                        return f

                    def post_ff(tcn, sl, bank, bank2, si):
                        K.op("act", lambda e: e.activation(out=etmp[:], in_=psum[0:8, bank, :], func=AF.Exp, bias=bf[:, 0:1], scale=-1.0),
                             [PB[bank], b_bf], [b_etmp])
                        K.op("act", lambda e: e.activation(out=spc[:], in_=etmp[:], func=AF.Ln, bias=1.0), [b_etmp], [b_spc])
                        j = tcn % 2
                        init = 0.0 if tcn == 0 else fnc[1 - j][:, 511:512]
                        rd = [b_spc] if tcn == 0 else [b_spc, b_fnc[1 - j]]
                        K.op("dve", lambda e: e.tensor_tensor_scan(out=fnc[j][:], data0=spc[:], data1=spc[:], initial=init, op0=ALU.add, op1=ALU.max),
                             rd, [b_fnc[j]])
                        K.op("dve", lambda e: e.tensor_scalar(out=fbs[0][:], in0=fnc[j][:], scalar1=-1.0, scalar2=None, op0=ALU.mult), [b_fnc[j]], [b_fbs[0]])
                        K.op("dve", lambda e: e.tensor_copy(out=fbs[1][:], in_=fnc[j][:]), [b_fnc[j]], [b_fbs[1]])
                        K.op("dve", lambda e: e.tensor_tensor(out=fr1[:], in0=fnc[j][:], in1=fbs[1][:], op=ALU.subtract), [b_fnc[j], b_fbs[1]], [b_fr1])
                        K.op("dve", lambda e: e.tensor_copy(out=fbs[2][:], in_=fr1[:]), [b_fr1], [b_fbs[2]])
                        K.op("dve", lambda e: e.tensor_tensor(out=fr2[:], in0=fr1[:], in1=fbs[2][:], op=ALU.subtract), [b_fr1, b_fbs[2]], [b_fr2])
                        K.op("dve", lambda e: e.tensor_copy(out=fbs[3][:], in_=fr2[:]), [b_fr2], [b_fbs[3]])
                        K.dma("sp", QF[:, 64, sl], fbs[0][:], b_fbs[0], [b_fbs[0]], (), [dB["QF"]])
                        for r_ in range(3):
                            K.dma("sp", KF[:, 65 + r_, sl], fbs[1 + r_][:], b_fbs[1 + r_], [b_fbs[1 + r_]], (), [dB["KF"]])

                    vctr = [0]

                    def post_ckv(tcn, sl, bank, bank2, si):
                        K.op("act", lambda e: e.activation(out=ckvT[:], in_=psum[:, bank, :], func=AF.Copy), [PB[bank]], [b_ckv])
                        K.op("act", lambda e: e.activation(out=sqb[:], in_=psum[:, bank, :], func=AF.Square), [PB[bank]], [b_sqb])
                        K.op("pe", lambda e: e.matmul(psum[:, 4, :], lhsT=ones_bf[:], rhs=sqb[:], start=True, stop=True, skip_group_check=True),
                             [b_sqb, b_const], [PB[4]])
                        K.op("dve", lambda e: e.tensor_scalar(out=rstd[:], in0=psum[:, 4, :], scalar1=float(1.0 / 128), scalar2=EPS, op0=ALU.mult, op1=ALU.add),
                             [PB[4]], [b_rstd])
                        K.op("act", lambda e: e.activation(out=rstd[:], in_=rstd[:], func=AF.Sqrt), [b_rstd], [b_rstd])
                        K.op("dve", lambda e: e.reciprocal(out=rstd[:], in_=rstd[:]), [b_rstd], [b_rstd])
                        K.op("pe", lambda e: e.matmul(psum[:, 5, :], lhsT=wkv_b[:], rhs=ckvT[:], start=True, stop=True, skip_group_check=True),
                             [b_ckv, b_wkv], [PB[5]])
                        K.op("pe", lambda e: e.matmul(psum[0:64, 6, :], lhsT=wkvr_b[:], rhs=ckvT[:], start=True, stop=True, skip_group_check=True),
                             [b_ckv, b_wkv], [PB[6]])
                        j = tcn % 2
                        K.op("dve", lambda e: e.tensor_tensor(out=t1[j][0:64, :], in0=psum[0:64, 5, :], in1=CT[0:64, sl], op=ALU.mult), [PB[5], b_tab], [b_t1[j]])
                        K.op("dve", lambda e: e.tensor_tensor(out=t2[j][0:64, :], in0=psum[0:64, 6, :], in1=SS[0:64, sl], op=ALU.mult), [PB[6], b_tab], [b_t2[j]])
                        K.op("pool", lambda e: e.tensor_tensor(out=t1[j][0:64, :], in0=t1[j][0:64, :], in1=t2[j][0:64, :], op=ALU.add), [b_t2[j]], [b_t1[j]])
                        K.op("pool", lambda e: e.tensor_tensor(out=stg[si][0:64, sl], in0=t1[j][0:64, :], in1=rstd[0:64, :], op=ALU.mult),
                             [b_t1[j], b_rstd], (), [b_stg[si]])
                        K.op("dve", lambda e: e.tensor_tensor(out=dvT[64:128, :], in0=psum[64:128, 5, :], in1=rstd[64:128, :], op=ALU.mult),
                             [PB[5], b_rstd], [b_dvT])
                        pbf = psum[:, 7, :].bitcast(BF16)
                        for q in range(4):
                            K.op("pe", lambda e, q=q: e.transpose(out=pbf[:, q * 64:(q + 1) * 64], in_=dvT[64:128, q * P:(q + 1) * P], identity=ident[64:128, 64:128]),
                                 [b_dvT, b_const], [PB[7]])
                        for q in range(4):
                            vs = vctr[0] % 3
                            vctr[0] += 1
                            K.op("act", lambda e, q=q, vs=vs: e.activation(out=tstg[vs][:, 0:64], in_=pbf[:, q * 64:(q + 1) * 64], func=AF.Copy), [PB[7]], [b_tstg[vs]])
                            tok0 = tcn * 512 + q * P
                            K.dma("sp", VD[tok0:tok0 + P, :], tstg[vs][:, 0:65], b_tstg[vs], [b_tstg[vs]], (), [dB["VD"]])

                    tctr = [0]

                    def tm_job(col0, ncols, post):
                        box = {}

                        def load():
                            box["wi"] = load_w(w_in[l, :, col0:col0 + ncols], ncols)

                        def run():
                            wi = box["wi"]
                            for i in range(NT):
                                bank = next_bank()
                                for kc in range(8):
                                    K.op("pe", lambda e, kc=kc, bank=bank, i=i: e.matmul(psum[:, bank, 0:ncols], lhsT=hT[:, kc, i * P:(i + 1) * P], rhs=wbuf[wi][:, kc, 0:ncols],
                                                                                         start=(kc == 0), stop=(kc == 7), skip_group_check=True),
                                         [b_w[wi], b_hT], [PB[bank]])
                                ts = tctr[0] % 3
                                tctr[0] += 1
                                post(i, bank, ts)
                        jobs.append((load, run))

                    def post_tm_act(func, dst, dcol0, dbuf):
                        def f(i, bank, ts):
                            K.op("act", lambda e: e.activation(out=tstg[ts][:, 0:512], in_=psum[:, bank, :], func=func), [PB[bank]], [b_tstg[ts]])
                            K.dma("sp", dst[i * P:(i + 1) * P, dcol0:dcol0 + 512], tstg[ts][:, 0:512], b_tstg[ts], [b_tstg[ts]], (), [dbuf])
                        return f

                    def post_fv(i, bank, ts):
                        tv = tstg[ts][:, 0:520].rearrange("p (h e) -> p h e", h=8)
                        K.op("act", lambda e: e.activation(out=tv[:, :, 0:64], in_=psum[:, bank, :].rearrange("p (h e) -> p h e", h=8), func=AF.Copy),
                             [PB[bank]], [b_tstg[ts]])
                        K.dma("sp", VF[i * P:(i + 1) * P, :, :].rearrange("p h e -> p (h e)"), tstg[ts][:, 0:520], b_tstg[ts], [b_tstg[ts]], (), [dB["VF"]])

                    def post_diw(i, bank, ts):
                        cst = float((64 ** -0.5) * (8 ** -0.5))
                        K.op("act", lambda e: e.activation(out=wab[:, i, :], in_=psum[:, bank, 0:8], func=AF.Abs, scale=cst), [PB[bank]], (), [b_wab])
                        K.op("act", lambda e: e.activation(out=wsg[:, i, :], in_=psum[:, bank, 0:8], func=AF.Sign), [PB[bank]], (), [b_wab])

                    def remset():
                        for i in range(3):
                            K.op("pool", lambda e, i=i: e.memset(tstg[i][:], 1.0), (), [b_tstg[i]])

                    fm_job(OFF["dckv"], P, None, post_ckv,
                           lambda si: K.dma("sp", KD[:, :], stg[si][0:64, :], b_stg[si], [b_stg[si]], (), [dB["KD"]]))
                    fm_job(OFF["ff"], 8, None, post_ff, None)
                    for pr in range(4):
                        fm_job(OFF["fq"] + pr * P, P, None, post_scale(0.125), store_heads(QF, pr, dB["QF"]))
                    for pr in range(4):
                        fm_job(OFF["fk"] + pr * P, P, None, post_scale(1.0), store_heads(KF, pr, dB["KF"]))
                    for pr in range(4):
                        fm_job(OFF["sq"] + pr * P, P, None, post_scale(0.125), store_heads(QS, pr, dB["QS"]))
                    for pr in range(4):
                        fm_job(OFF["sk"] + pr * P, P, None, post_scale(1.0), store_heads(KS, pr, dB["KS"]))
                    for pr in range(4):
                        fm_job(OFF["dq"] + pr * P, P, pr * P, post_rope(0.125, P), store_fm_heads(QD, pr, dB["QD"]))
                    for pr in range(4):
                        fm_job(OFF["diq"] + pr * P, P, 512 + pr * P, post_rope(1.0, P, FP16), store_fm_heads(IQ, pr, dB["IQ"], FP16))
                    fm_job(OFF["dik"], 64, 1024, post_rope(1.0, 64, FP16),
                           lambda si: K.dma("sp", IK[:, :], stg[si][:, :].bitcast(FP16)[0:64, :], b_stg[si], [b_stg[si]], (), [dB["IK"]]))
                    jobs.append((lambda: None, remset))
                    tm_job(OFF["fv"], 512, post_fv)
                    tm_job(OFF["sv"], 512, post_tm_act(AF.Copy, VS, 0, dB["VS"]))
                    tm_job(OFF["fg"], 512, post_tm_act(AF.Silu, GF, 0, dB["GF"]))
                    tm_job(OFF["dg"], 512, post_tm_act(AF.Silu, GD, 0, dB["GD"]))
                    tm_job(OFF["sg"], 512, post_tm_act(AF.Silu, GS, 0, dB["GS"]))
                    for m in range(6):
                        tm_job(OFF["merge"] + m * 512, 512, post_tm_act(AF.Sigmoid, MG, m * 512, dB["MG"]))
                    tm_job(OFF["diw"], 8, post_diw)
                    jobs[0][0]()
                    for jn, (ld, rn) in enumerate(jobs):
                        if jn + 1 < len(jobs):
                            jobs[jn + 1][0]()
                        rn()
                    if debug:
                        K.dma("sp", dbg["WAB"][:, :, :], wab[:], s_misc, [b_wab], ())
                        K.dma("sp", dbg["WSG"][:, :, :], wsg[:], s_misc, [b_wab], ())
                    K.barrier()
            if stop_after == "B":
                break

            with ExitStack() as cs:
                qT = [sbt(cs, "qT%d" % i, [P, S], BF16) for i in range(2)]; b_qT = [Buf("qT0"), Buf("qT1")]
                kT = [sbt(cs, "kT%d" % i, [P, S], BF16) for i in range(2)]; b_kT = [Buf("kT0"), Buf("kT1")]
                Vt = [sbt(cs, "Vt%d" % i, [P, NT, 65], BF16) for i in range(2)]; b_V = [Buf("V0"), Buf("V1")]
                Gt = [sbt(cs, "Gt%d" % i, [P, NT, 64], BF16) for i in range(2)]; b_G = [Buf("G0"), Buf("G1")]
                ystg = [sbt(cs, "ystg%d" % i, [P, NT, 64], BF16) for i in range(2)]; b_ys = [Buf("ys0"), Buf("ys1")]
                pT = [sbt(cs, "pT%d" % i, [P, 1024], BF16) for i in range(3)]; b_pT = [Buf("pT%d" % i) for i in range(3)]
                spb = [sbt(cs, "spb%d" % i, [P, 512], BF16) for i in range(2)]; b_spb = [Buf("spb0"), Buf("spb1")]
                etm = [sbt(cs, "etm%d" % i, [P, 512], F32) for i in range(2)]; b_etm = [Buf("etm0"), Buf("etm1")]
                lsum = sbt(cs, "lsum", [P, 512], BF16); b_lsum = Buf("lsum")
                rden = sbt(cs, "rden", [P, 8], F32); b_rden = Buf("rden")

                def load_head(kind, h, s):
                    if kind == "fox":
                        K.dma("sp", qT[s][0:68, :], QF[h, :, :], b_qT[s], [dB["QF"]], [b_qT[s]])
                        K.dma("sp", kT[s][0:68, :], KF[h, :, :], b_kT[s], [dB["KF"]], [b_kT[s]])
                        K.dma("sp", Vt[s][:, :, :], VF.rearrange("(j p) h e -> p j h e", p=P)[:, :, h, :], b_V[s], [dB["VF"]], [b_V[s]])
                        K.dma("sp", Gt[s][:, :, :], GF.rearrange("(j p) (h e) -> p j h e", p=P, h=8)[:, :, h, :], b_G[s], [dB["GF"]], [b_G[s]])
                    else:
                        K.dma("sp", qT[s][0:64, :], QS[h, :, :], b_qT[s], [dB["QS"]], [b_qT[s]])
                        K.dma("sp", kT[s][0:64, :], KS[h, :, :], b_kT[s], [dB["KS"]], [b_kT[s]])
                        K.dma("sp", Vt[s][:, :, 0:64], VS.rearrange("(j p) (h e) -> p j h e", p=P, h=8)[:, :, h, :], b_V[s], [dB["VS"]], [b_V[s]])
                        K.dma("sp", Gt[s][:, :, :], GS.rearrange("(j p) (h e) -> p j h e", p=P, h=8)[:, :, h, :], b_G[s], [dB["GS"]], [b_G[s]])

                heads_seq = [("fox", h) for h in range(8)] + [("sb", h) for h in range(8)]
                for s_ in range(2):
                    K.op("pool", lambda e, s_=s_: e.memset(qT[s_][:], 0.0), (), [b_qT[s_]])
                    K.op("pool", lambda e, s_=s_: e.memset(kT[s_][:], 0.0), (), [b_kT[s_]])

                def load_head_idx(hi):
                    if hi >= len(heads_seq):
                        return
                    kind_, h_ = heads_seq[hi]
                    s_ = hi % 2
                    if kind_ == "sb" and h_ < 2:
                        K.op("pool", lambda e: e.memset(qT[s_][64:128, :], 0.0), (), [b_qT[s_]])
                        K.op("pool", lambda e: e.memset(kT[s_][64:128, :], 0.0), (), [b_kT[s_]])
                    load_head(kind_, h_, s_)

                steps = []
                cg = 0
                for hi, (kind, h) in enumerate(heads_seq):
                    for c in range(8):
                        Js = list(range(4 * c + 4))
                        if kind == "sb":
                            Js = Js[::-1]
                        for idx, J in enumerate(Js):
                            r = J - 4 * c
                            steps.append(dict(hi=hi, kind=kind, h=h, s=hi % 2, c=c, cg=cg, idx=idx, J=J, r=r, qlo=max(0, r) * P,
                                              first=(idx == 0), last=(idx == len(Js) - 1), last_head=(idx == len(Js) - 1 and c == 7),
                                              g=len(steps)))
                        cg += 1
                first_pv = {}
                pslot = {}
                KR = 128

                def st1(t):
                    kind, s, c, J, r, qlo = t["kind"], t["s"], t["c"], t["J"], t["r"], t["qlo"]
                    bank = t["g"] % 4
                    tri = tri_le if kind == "fox" else tri_lt
                    K.op("pe", lambda e: e.matmul(psum[:, bank, qlo:512], lhsT=kT[s][0:KR, J * P:(J + 1) * P], rhs=qT[s][0:KR, c * 512 + qlo:(c + 1) * 512],
                                                  start=True, stop=(r < 0), skip_group_check=True), [b_kT[s], b_qT[s]], [PB[bank]])
                    if r >= 0:
                        K.op("pe", lambda e: e.matmul(psum[:, bank, qlo:qlo + P], lhsT=ident[:], rhs=tri[:], start=False, stop=True, skip_group_check=True),
                             [b_const], (), [PB[bank]])
                    if kind == "sb":
                        j2 = t["g"] % 2
                        K.op("act", lambda e: e.activation(out=etm[j2][:, qlo:512], in_=psum[:, bank, qlo:512], func=AF.Exp), [PB[bank]], [b_etm[j2]])
                        K.op("act", lambda e: e.activation(out=spb[j2][:, qlo:512], in_=etm[j2][:, qlo:512], func=AF.Ln, bias=1.0), [b_etm[j2]], [b_spb[j2]])

                def st2(t):
                    kind, qlo = t["kind"], t["qlo"]
                    bank = t["g"] % 4
                    ps_ = t["g"] % 3
                    pslot[t["g"]] = ps_
                    if kind == "sb":
                        j2 = t["g"] % 2
                        if t["first"]:
                            K.op("dve", lambda e: e.memset(lsum[:], 0.0), (), [b_lsum])
                        K.op("pe", lambda e: e.matmul(psum[:, bank, qlo:512], lhsT=negU[:], rhs=spb[j2][:, qlo:512], start=False, stop=False, skip_group_check=True),
                             [b_spb[j2], b_const], (), [PB[bank]])
                        if not t["first"]:
                            K.op("pe", lambda e: e.matmul(psum[:, bank, qlo:512], lhsT=negones[:], rhs=lsum[:, qlo:512], start=False, stop=True, skip_group_check=True),
                                 [b_lsum, b_const], (), [PB[bank]])
                    K.op("act", lambda e: e.activation(out=pT[ps_][:, qlo:512], in_=psum[:, bank, qlo:512], func=AF.Exp), [PB[bank]], [b_pT[ps_]])
                    if kind == "sb":
                        K.op("dve", lambda e: e.tensor_tensor(out=lsum[:, qlo:512], in0=lsum[:, qlo:512], in1=spb[j2][:, qlo:512], op=ALU.add),
                             [b_spb[j2]], [b_lsum])

                def st3(t):
                    kind, s, c, J, r, h = t["kind"], t["s"], t["c"], t["J"], t["r"], t["h"]
                    VW = 65 if kind == "fox" else 64
                    OB = 6 + (t["cg"] % 2)
                    ob = psum[:, OB, 0:4 * 65].rearrange("p (a b) -> p a b", a=4)
                    ps_ = pslot.pop(t["g"])
                    for sub in range(max(0, r), 4):
                        st_ = t["cg"] not in first_pv
                        first_pv[t["cg"]] = True
                        K.op("pe", lambda e, sub=sub, st_=st_: e.matmul(ob[:, sub, 0:VW], lhsT=pT[ps_][:, sub * P:(sub + 1) * P], rhs=Vt[s][:, J, 0:VW],
                                                                        start=st_, stop=False, skip_group_check=True),
                             [b_pT[ps_], b_V[s]], (), [PB[OB]])
                    if t["last"]:
                        if kind == "fox":
                            K.op("dve", lambda e: e.reciprocal(out=rden[:, 0:4], in_=ob[:, :, 64]), [PB[OB]], [b_rden])
                            for sub in range(4):
                                K.op("dve", lambda e, sub=sub: e.scalar_tensor_tensor(out=ystg[s][:, 4 * c + sub, :], in0=ob[:, sub, 0:64], scalar=rden[:, sub:sub + 1],
                                                                                      in1=Gt[s][:, 4 * c + sub, :], op0=ALU.mult, op1=ALU.mult),
                                     [PB[OB], b_rden, b_G[s]], (), [b_ys[s]])
                        else:
                            for sub in range(4):
                                K.op("dve", lambda e, sub=sub: e.tensor_tensor(out=ystg[s][:, 4 * c + sub, :], in0=ob[:, sub, 0:64], in1=Gt[s][:, 4 * c + sub, :], op=ALU.mult),
                                     [PB[OB], b_G[s]], (), [b_ys[s]])
                    if t["last_head"]:
                        ydst = YF if kind == "fox" else YS
                        K.dma("sp", ydst.rearrange("(j p) (h e) -> p j h e", p=P, h=8)[:, :, h, :], ystg[s][:, :, :], b_ys[s], [b_ys[s]], (),
                              [dB["YF" if kind == "fox" else "YS"]])
                        load_head_idx(t["hi"] + 2)

                load_head_idx(0)
                load_head_idx(1)
                nst = len(steps)
                for k in range(nst + 2):
                    if k < nst:
                        st1(steps[k])
                    if 0 <= k - 1 < nst:
                        st2(steps[k - 1])
                    if 0 <= k - 2 < nst:
                        st3(steps[k - 2])
                K.barrier()
            if stop_after == "C1":
                break

            with ExitStack() as cs:
                kd = sbt(cs, "kd", [P, S], BF16); ikd = sbt(cs, "ikd", [P, S], FP16); vd = sbt(cs, "vd", [P, NT, 65], BF16)
                b_kd = Buf("kd"); b_ikd = Buf("ikd"); b_vd = Buf("vd")
                iqt = [sbt(cs, "iqt%d" % i, [P, 8, P], FP16) for i in range(2)]; b_iqt = [Buf("iqt0"), Buf("iqt1")]
                qdt = [sbt(cs, "qdt%d" % i, [P, 8, P], BF16) for i in range(2)]; b_qdt = [Buf("qdt0"), Buf("qdt1")]
                gdt = [sbt(cs, "gdt%d" % i, [P, 512], BF16) for i in range(2)]; b_gdt = [Buf("gdt0"), Buf("gdt1")]
                dg = [sbt(cs, "dg%d" % i, [P, 8, P], FP16) for i in range(2)]; b_dg = [Buf("dg0"), Buf("dg1")]
                ident16 = sbt(cs, "ident16", [P, P], FP16); caus16 = sbt(cs, "caus16", [P, P], FP16); b_c16 = Buf("c16")
                acc = [sbt(cs, "acc%d" % i, [P, S], F32) for i in range(2)]; b_acc = [Buf("acc0"), Buf("acc1")]
                NRT = 4
                rt = [sbt(cs, "rt%d" % i, [P, 512], FP16) for i in range(NRT)]; b_rt = [Buf("rt%d" % i) for i in range(NRT)]
                Mb = sbt(cs, "Mb", [P, S], BF16); b_M = Buf("Mb")
                MT = [sbt(cs, "MT%d" % i, [P, NT, P], BF16) for i in range(2)]; b_MT = [Buf("MT0"), Buf("MT1")]
                pTd = [sbt(cs, "pTd%d" % i, [P, 8, P], BF16) for i in range(3)]; b_pTd = [Buf("pTd%d" % i) for i in range(3)]
                sm = sbt(cs, "smD", [P, 8], F32); b_sm = Buf("smD")
                nW = sbt(cs, "nWD", [P, NIT], F32); hW = sbt(cs, "hWD", [P, NIT], F32); b_W = Buf("WD")
                rden = sbt(cs, "rdenD", [P, 8], F32); b_rden = Buf("rdenD")
                yd = [sbt(cs, "yd%d" % i, [P, 512], BF16) for i in range(2)]; b_yd = [Buf("yd0"), Buf("yd1")]
                K.op("pool", lambda e: e.tensor_copy(out=ident16[:], in_=ident[:]), [b_const], [b_c16])
                K.op("pool", lambda e: e.memset(caus16[:], 0.0), (), [b_c16])
                K.op("pool", lambda e: e.affine_select(out=caus16[:], in_=caus16[:], pattern=[[-1, P]], compare_op=ALU.is_ge,
                                                       fill=NEG, base=0, channel_multiplier=1), (), [b_c16])
                K.op("pool", lambda e: e.memset(kd[64:128, :], 0.0), (), [b_kd])
                K.op("pool", lambda e: e.memset(ikd[64:128, :], 0.0), (), [b_ikd])
                for s_ in range(2):
                    K.op("pool", lambda e, s_=s_: e.memset(iqt[s_][64:128, :, :], 0.0), (), [b_iqt[s_]])
                    K.op("pool", lambda e, s_=s_: e.memset(qdt[s_][64:128, :, :], 0.0), (), [b_qdt[s_]])
                K.dma("sp", kd[0:64, :], KD[:, :], b_kd, [dB["KD"]], [b_kd])
                K.dma("sp", ikd[0:64, :], IK[:, :], b_ikd, [dB["IK"]], [b_ikd])
                K.dma("sp", vd[:], VD.rearrange("(j p) e -> p j e", p=P), b_vd, [dB["VD"]], [b_vd])
                rctr = [0]
                pctr = [0]

                def build_dg(i):
                    s = i % 2
                    for h in range(8):
                        K.op("dve", lambda e, h=h: e.tensor_scalar(out=dg[s][:, h, :], in0=ident16[:], scalar1=wsg[:, i, h:h + 1], scalar2=None, op0=ALU.mult),
                             [b_c16, b_wab], (), [b_dg[s]])

                def stageA(i):
                    s = i % 2
                    K.dma("sp", iqt[s][0:64, :, :], IQ[:, :, i * P:(i + 1) * P], b_iqt[s], [dB["IQ"]], [b_iqt[s]])
                    if i == 0:
                        build_dg(0)
                    if i + 1 < NT:
                        build_dg(i + 1)
                    yield
                    L = (i + 1) * P
                    nkc = (L + 511) // 512
                    for kc in range(nkc):
                        w = min(512, L - kc * 512)
                        ksl = slice(kc * 512, kc * 512 + w)
                        SCB = 2
                        def qk(h):
                            K.op("pe", lambda e: e.matmul(psum[:, h % 2, 0:w], lhsT=iqt[s][:, h, :], rhs=ikd[:, ksl], start=True, stop=True,
                                                          skip_group_check=True), [b_iqt[s], b_ikd], [PB[h % 2]])
                        qk(0)
                        for h in range(8):
                            bank = h % 2
                            if h + 1 < 8:
                                qk(h + 1)
                            ri = rctr[0] % NRT
                            rctr[0] += 1
                            if h % 2 == 0:
                                K.op("act", lambda e, h=h, bank=bank, ri=ri: e.activation(out=rt[ri][:, 0:w], in_=psum[:, bank, 0:w], func=AF.Relu, scale=wab[:, i, h:h + 1]),
                                     [PB[bank], b_wab], [b_rt[ri]])
                            else:
                                K.op("dve", lambda e, h=h, bank=bank, ri=ri: e.tensor_scalar(out=rt[ri][:, 0:w], in0=psum[:, bank, 0:w], scalar1=wab[:, i, h:h + 1], scalar2=0.0,
                                                                                             op0=ALU.mult, op1=ALU.max), [PB[bank], b_wab], [b_rt[ri]])
                            if h == 0:
                                K.op("pe", lambda e, ri=ri: e.matmul(psum[:, SCB, 0:w], lhsT=dg[s][:, 0, :], rhs=rt[ri][:, 0:w], start=True, stop=False, skip_group_check=True),
                                     [b_dg[s], b_rt[ri]], [PB[SCB]])
                            else:
                                K.op("pe", lambda e, h=h, ri=ri: e.matmul(psum[:, SCB, 0:w], lhsT=dg[s][:, h, :], rhs=rt[ri][:, 0:w], start=False, stop=False, skip_group_check=True),
                                     [b_dg[s], b_rt[ri]], (), [PB[SCB]])
                            if h % 2 == 1:
                                yield
                        if kc == nkc - 1:
                            d0 = i * P - kc * 512
                            K.op("pe", lambda e: e.matmul(psum[:, SCB, d0:d0 + P], lhsT=ident16[:], rhs=caus16[:], start=False, stop=True, skip_group_check=True),
                                 [b_c16], (), [PB[SCB]])
                        K.op("act", lambda e: e.activation(out=acc[s][:, ksl], in_=psum[:, SCB, 0:w], func=AF.Identity), [PB[SCB]], (), [b_acc[s]])
                        yield

                def stageB(i):
                    s = i % 2
                    L = (i + 1) * P
                    K.dma("sp", qdt[s][0:64, :, :], QD[:, :, i * P:(i + 1) * P], b_qdt[s], [dB["QD"]], [b_qdt[s]])
                    K.dma("sp", gdt[s][:], GD[i * P:(i + 1) * P, :], b_gdt[s], [dB["GD"]], [b_gdt[s]])
                    if debug and "SC" in dbg:
                        K.dma("sp", dbg["SC"][i * P:(i + 1) * P, 0:L], acc[s][:, 0:L], s_misc, [b_acc[s]], ())
                    if i >= 2:
                        K.op("dve", lambda e: e.tensor_reduce(out=sm[:, 0:1], in_=acc[s][:, 0:i * P], axis=mybir.AxisListType.X, op=ALU.min), [b_acc[s]], [b_sm])
                        K.op("dve", lambda e: e.tensor_reduce(out=sm[:, 1:2], in_=acc[s][:, 0:L], axis=mybir.AxisListType.X, op=ALU.max), [b_acc[s]], [b_sm])
                        yield
                        K.op("dve", lambda e: e.tensor_tensor(out=sm[:, 2:3], in0=sm[:, 1:2], in1=sm[:, 0:1], op=ALU.subtract), [b_sm], [b_sm])
                        use_act = (i % 2 == 0)
                        if use_act:
                            K.op("dve", lambda e: e.tensor_scalar(out=nW[:], in0=pow2[:], scalar1=sm[:, 2:3], scalar2=-1.0, op0=ALU.mult, op1=ALU.mult), [b_sm, b_const], [b_W])
                            K.op("dve", lambda e: e.tensor_scalar(out=hW[:], in0=pow2[:], scalar1=sm[:, 2:3], scalar2=0.5, op0=ALU.mult, op1=ALU.mult), [b_sm, b_const], (), [b_W])
                            K.op("dve", lambda e: e.scalar_tensor_tensor(out=sm[:, 4:5], in0=sm[:, 0:1], scalar=-1.0, in1=nW[:, 0:1], op0=ALU.mult, op1=ALU.add), [b_W], [b_sm])
                            for k in range(NIT):
                                K.op("act", lambda e: e.activation(out=Mb[:, 0:L], in_=acc[s][:, 0:L], func=AF.Sign, bias=sm[:, 4:5], scale=1.0, accum_out=sm[:, 5:6]),
                                     [b_acc[s], b_sm], [b_M, b_sm])
                                K.op("dve", lambda e, k=k: e.scalar_tensor_tensor(out=sm[:, 6:7], in0=sm[:, 5:6], scalar=float(511 - L), in1=nW[:, k:k + 1], op0=ALU.is_ge, op1=ALU.mult),
                                     [b_W], [b_sm])
                                K.op("dve", lambda e, k=k: e.scalar_tensor_tensor(out=sm[:, 4:5], in0=sm[:, 6:7], scalar=hW[:, k:k + 1], in1=sm[:, 4:5], op0=ALU.add, op1=ALU.add),
                                     [b_W], [b_sm])
                                yield
                            K.op("dve", lambda e: e.scalar_tensor_tensor(out=sm[:, 3:4], in0=sm[:, 4:5], scalar=-1.0, in1=hW[:, NIT - 1:NIT], op0=ALU.mult, op1=ALU.subtract),
                                 [b_W], [b_sm])
                        else:
                            K.op("dve", lambda e: e.tensor_scalar(out=nW[:], in0=pow2[:], scalar1=sm[:, 2:3], scalar2=None, op0=ALU.mult), [b_sm, b_const], [b_W])
                            K.op("dve", lambda e: e.tensor_scalar(out=hW[:], in0=pow2[:], scalar1=sm[:, 2:3], scalar2=-0.5, op0=ALU.mult, op1=ALU.mult), [b_sm, b_const], (), [b_W])
                            K.op("dve", lambda e: e.tensor_tensor(out=sm[:, 4:5], in0=sm[:, 0:1], in1=nW[:, 0:1], op=ALU.add), [b_W], [b_sm])
                            for k in range(NIT):
                                K.op("dve", lambda e: e.tensor_scalar(out=Mb[:, 0:L], in0=acc[s][:, 0:L], scalar1=sm[:, 4:5], scalar2=None, op0=ALU.is_ge, op1=ALU.add,
                                                                      accum_out=sm[:, 5:6]), [b_acc[s], b_sm], [b_M, b_sm])
                                K.op("dve", lambda e, k=k: e.scalar_tensor_tensor(out=sm[:, 6:7], in0=sm[:, 5:6], scalar=255.5, in1=nW[:, k:k + 1], op0=ALU.is_ge, op1=ALU.mult),
                                     [b_W], [b_sm])
                                K.op("dve", lambda e, k=k: e.scalar_tensor_tensor(out=sm[:, 4:5], in0=sm[:, 6:7], scalar=hW[:, k:k + 1], in1=sm[:, 4:5], op0=ALU.add, op1=ALU.add),
                                     [b_W], [b_sm])
                                yield
                            K.op("dve", lambda e: e.tensor_tensor(out=sm[:, 3:4], in0=sm[:, 4:5], in1=hW[:, NIT - 1:NIT], op=ALU.add), [b_W], [b_sm])
                    else:
                        K.op("dve", lambda e: e.memset(sm[:, 3:4], -20000.0), (), [b_sm])
                    if debug and "TH" in dbg:
                        K.dma("sp", dbg["TH"][i * P:(i + 1) * P, :], sm[:, 3:4], s_misc, [b_sm], ())
                    K.op("dve", lambda e: e.tensor_scalar(out=Mb[:, 0:L], in0=acc[s][:, 0:L], scalar1=sm[:, 3:4], scalar2=None, op0=ALU.is_ge), [b_acc[s], b_sm], [b_M])
                    yield
                    for g in range((i + 8) // 8):
                        nj = min(8, i + 1 - g * 8)
                        bank = 3
                        pbf = psum[:, bank, :].bitcast(BF16)
                        for jj in range(nj):
                            J = g * 8 + jj
                            K.op("pe", lambda e, jj=jj, J=J, pbf=pbf: e.transpose(out=pbf[:, jj * P:(jj + 1) * P], in_=Mb[:, J * P:(J + 1) * P], identity=ident[:]),
                                 [b_M, b_const], [PB[bank]])
                        K.op("act", lambda e, g=g, nj=nj, pbf=pbf: e.activation(out=MT[s][:, g * 8:g * 8 + nj, :], in_=pbf[:, 0:nj * P].rearrange("p (a b) -> p a b", a=nj),
                                                                                func=AF.Copy), [PB[bank]], (), [b_MT[s]])
                        yield

                def stageC(i):
                    s = i % 2
                    oA = psum[:, 6, 0:260].rearrange("p (a b) -> p a b", a=4)
                    oB = psum[:, 7, 0:260].rearrange("p (a b) -> p a b", a=4)
                    firstA = [True, True]
                    pend = None
                    for J in range(i + 1):
                        for half in range(2):
                            K.op("pe", lambda e, half=half: e.matmul(psum[:, 4 + half, :].rearrange("p (a b) -> p a b", a=4), lhsT=kd[:, J * P:(J + 1) * P],
                                                                     rhs=qdt[s][:, half * 4:(half + 1) * 4, :], start=True, stop=True, skip_group_check=True),
                                 [b_kd, b_qdt[s]], [PB[4 + half]])
                        ps_ = pctr[0] % 3
                        pctr[0] += 1
                        K.op("act", lambda e, ps_=ps_: e.activation(out=pTd[ps_][:].rearrange("p a b -> p (a b)"), in_=psum[:, 4:6, :].rearrange("p a b -> p (a b)"), func=AF.Exp),
                             [PB[4], PB[5]], [b_pTd[ps_]])
                        K.op("dve", lambda e, ps_=ps_: e.tensor_tensor(out=pTd[ps_][:], in0=pTd[ps_][:], in1=MT[s][:, J:J + 1, :].to_broadcast([P, 8, P]), op=ALU.mult),
                             [b_MT[s]], [b_pTd[ps_]])
                        if pend is not None:
                            Jp, pp = pend
                            for h in range(8):
                                o_ = oA if h < 4 else oB
                                hb = 0 if h < 4 else 1
                                st_ = firstA[hb]
                                firstA[hb] = False
                                K.op("pe", lambda e, h=h, o_=o_, st_=st_, Jp=Jp, pp=pp: e.matmul(o_[:, h % 4, :], lhsT=pTd[pp][:, h, :], rhs=vd[:, Jp, :], start=st_, stop=False,
                                                                                                 skip_group_check=True), [b_pTd[pp], b_vd], (), [PB[6 + hb]])
                        pend = (J, ps_)
                        yield
                    Jp, pp = pend
                    for h in range(8):
                        o_ = oA if h < 4 else oB
                        hb = 0 if h < 4 else 1
                        st_ = firstA[hb]
                        firstA[hb] = False
                        K.op("pe", lambda e, h=h, o_=o_, st_=st_: e.matmul(o_[:, h % 4, :], lhsT=pTd[pp][:, h, :], rhs=vd[:, Jp, :], start=st_, stop=False,
                                                                           skip_group_check=True), [b_pTd[pp], b_vd], (), [PB[6 + hb]])
                    K.op("dve", lambda e: e.reciprocal(out=rden[:, 0:4], in_=oA[:, :, 64]), [PB[6]], [b_rden])
                    K.op("dve", lambda e: e.reciprocal(out=rden[:, 4:8], in_=oB[:, :, 64]), [PB[7]], [b_rden])
                    for h in range(8):
                        o_ = oA if h < 4 else oB
                        K.op("dve", lambda e, h=h, o_=o_: e.scalar_tensor_tensor(out=yd[s][:, h * 64:(h + 1) * 64], in0=o_[:, h % 4, 0:64], scalar=rden[:, h:h + 1],
                                                                                 in1=gdt[s][:, h * 64:(h + 1) * 64], op0=ALU.mult, op1=ALU.mult),
                             [PB[6 + (h // 4)], b_rden, b_gdt[s]], (), [b_yd[s]])
                    K.dma("sp", YD[i * P:(i + 1) * P, :], yd[s][:], b_yd[s], [b_yd[s]], (), [dB["YD"]])
                    yield

                for n in range(NT + 2):
                    gens = []
                    if n < NT:
                        gens.append(stageA(n))
                    if 0 <= n - 1 < NT and DSA_STAGES >= 2:
                        gens.append(stageB(n - 1))
                    if 0 <= n - 2 < NT and DSA_STAGES >= 3:
                        gens.append(stageC(n - 2))
                    while gens:
                        for g in list(gens):
                            try:
                                next(g)
                            except StopIteration:
                                gens.remove(g)
                K.barrier()
            if stop_after == "C2":
                break

            with ExitStack() as cs:
                wbr = [sbt(cs, "wbr%d" % b, [P, 4, D], BF16) for b in range(3)]
                wo = sbt(cs, "wo", [P, 8, D], BF16)
                b_wD = Buf("wD")
                gfin = sbt(cs, "gfin", [P, D], F32)
                yt = [sbt(cs, "yt%d" % i, [P, 3, 512], BF16) for i in range(2)]; b_yt = [Buf("yt0"), Buf("yt1")]
                mg = [sbt(cs, "mg%d" % i, [P, 3 * D], BF16) for i in range(2)]; b_mg = [Buf("mg0"), Buf("mg1")]
                xt = [sbt(cs, "xtD%d" % i, [P, D], F32) for i in range(2)]; b_xt = [Buf("xtD0"), Buf("xtD1")]
                yT = sbt(cs, "yT", [P, 12, P], BF16); b_yT = Buf("yT")
                mx = [sbt(cs, "mx%d" % b, [P, D], F32) for b in range(3)]; b_mx = [Buf("mx%d" % b) for b in range(3)]
                mxb = sbt(cs, "mxb", [P, D], BF16); b_mxb = Buf("mxb")
                mT = sbt(cs, "mT", [P, 8, P], BF16); b_mT = Buf("mT")
                xo = [sbt(cs, "xo%d" % i, [P, D], F32) for i in range(2)]; b_xo = [Buf("xo0"), Buf("xo1")]
                junk = sbt(cs, "junkE", [P, D], F32); b_junk = Buf("junkE")
                st = sbt(cs, "statD", [P, 4], F32); b_st = Buf("statD")
                b_wDs = [Buf("wD%d" % i) for i in range(5)]
                for b in range(3):
                    K.dma("pool", wbr[b][:], w_brs[b][l, :, :].rearrange("(kc p) n -> p kc n", p=P), b_wDs[b], (), (), [b_wD])
                K.dma("pool", wo[:], w_out[l, :, :].rearrange("(kc p) n -> p kc n", p=P), b_wDs[3], (), (), [b_wD])
                if last:
                    K.dma("sp", gfin[:], g_final[0:1, :].to_broadcast([P, D]), b_wDs[4], (), (), [b_wD])
                ysrc = [YF, YD, YS]
                ybuf = [dB["YF"], dB["YD"], dB["YS"]]
                b_xdst = Buf("xdst", loose=True)

                mxb2 = [mxb, sbt(cs, "mxb_b", [P, D], BF16)]; b_mxb2 = [b_mxb, Buf("mxb_b")]

                def e_load(i):
                    s = i % 2
                    for b in range(3):
                        K.dma("sp", yt[s][:, b, :], ysrc[b][i * P:(i + 1) * P, :], b_yt[s], [ybuf[b]], (), [b_yt[s]])
                    K.dma("sp", mg[s][:], MG[i * P:(i + 1) * P, :], b_mg[s], [dB["MG"]], [b_mg[s]])

                def x_load(i):
                    s = i % 2
                    K.dma("sp", xt[s][:], x_cur[i * P:(i + 1) * P, :], b_xt[s], [b_xsrc], [b_xt[s]])

                def d1(i):
                    s = i % 2
                    for g in range(2):
                        bank = g
                        pbf = psum[:, bank, :].bitcast(BF16)
                        for jj in range(6):
                            q = g * 6 + jj
                            b, kc = q // 4, q % 4
                            K.op("pe", lambda e, jj=jj, b=b, kc=kc, pbf=pbf: e.transpose(out=pbf[:, jj * P:(jj + 1) * P], in_=yt[s][:, b, kc * P:(kc + 1) * P], identity=ident[:]),
                                 [b_yt[s], b_const], [PB[bank]])
                        K.op("act", lambda e, g=g, pbf=pbf: e.activation(out=yT[:, g * 6:(g + 1) * 6, :], in_=pbf[:, 0:6 * P].rearrange("p (a b) -> p a b", a=6), func=AF.Copy),
                             [PB[bank]], (), [b_yT])
                    for b in range(3):
                        for half in range(2):
                            bank = 2 + half
                            for kc in range(4):
                                K.op("pe", lambda e, b=b, half=half, kc=kc, bank=bank: e.matmul(psum[:, bank, :], lhsT=yT[:, b * 4 + kc, :], rhs=wbr[b][:, kc, half * 512:(half + 1) * 512],
                                                                                                start=(kc == 0), stop=(kc == 3), skip_group_check=True),
                                     [b_yT, b_wD], [PB[bank]])
                            K.op("dve", lambda e, b=b, half=half, bank=bank: e.tensor_tensor(out=mx[b][:, half * 512:(half + 1) * 512], in0=psum[:, bank, :],
                                                                                             in1=mg[s][:, b * D + half * 512: b * D + (half + 1) * 512], op=ALU.mult),
                                 [PB[bank], b_mg[s]], (), [b_mx[b]])
                    K.op("pool", lambda e: e.tensor_tensor(out=mx[0][:], in0=mx[0][:], in1=mx[1][:], op=ALU.add), [b_mx[1]], [b_mx[0]])
                    K.op("pool", lambda e: e.tensor_tensor(out=mxb2[s][:], in0=mx[0][:], in1=mx[2][:], op=ALU.add), [b_mx[0], b_mx[2]], [b_mxb2[s]])

                def d2(i):
                    s = i % 2
                    bank = 4
                    pbf = psum[:, bank, :].bitcast(BF16)
                    for kc in range(8):
                        K.op("pe", lambda e, kc=kc, pbf=pbf: e.transpose(out=pbf[:, kc * P:(kc + 1) * P], in_=mxb2[s][:, kc * P:(kc + 1) * P], identity=ident[:]),
                             [b_mxb2[s], b_const], [PB[bank]])
                    K.op("act", lambda e, pbf=pbf: e.activation(out=mT[:], in_=pbf.rearrange("p (a b) -> p a b", a=8), func=AF.Copy), [PB[bank]], [b_mT])
                    for half in range(2):
                        bank = 5 + half
                        for kc in range(8):
                            K.op("pe", lambda e, half=half, kc=kc, bank=bank: e.matmul(psum[:, bank, :], lhsT=mT[:, kc, :], rhs=wo[:, kc, half * 512:(half + 1) * 512],
                                                                                       start=(kc == 0), stop=(kc == 7), skip_group_check=True), [b_mT, b_wD], [PB[bank]])
                        hs_ = slice(half * 512, (half + 1) * 512)
                        K.op("dve", lambda e, bank=bank, hs_=hs_: e.tensor_tensor(out=xo[s][:, hs_], in0=psum[:, bank, :], in1=gate_rep[:, hs_], op=ALU.mult),
                             [PB[bank], b_mod], (), [b_xo[s]])
                    K.op("pool", lambda e: e.tensor_tensor(out=xo[s][:], in0=xo[s][:], in1=xt[s][:], op=ALU.add), [b_xt[s]], [b_xo[s]])
                    if last:
                        K.op("act", lambda e: e.activation(out=junk[:], in_=xo[s][:], func=AF.Square, accum_out=st[:, 0:1]), [b_xo[s]], [b_junk, b_st])
                        K.op("dve", lambda e: e.tensor_scalar(out=st[:, 1:2], in0=st[:, 0:1], scalar1=float(1.0 / D), scalar2=EPS, op0=ALU.mult, op1=ALU.add), [b_st], [b_st])
                        K.op("act", lambda e: e.activation(out=st[:, 2:3], in_=st[:, 1:2], func=AF.Sqrt), [b_st], [b_st])
                        K.op("dve", lambda e: e.reciprocal(out=st[:, 3:4], in_=st[:, 2:3]), [b_st], [b_st])
                        K.op("dve", lambda e: e.scalar_tensor_tensor(out=xo[s][:], in0=xo[s][:], scalar=st[:, 3:4], in1=gfin[:], op0=ALU.mult, op1=ALU.mult),
                             [b_st, b_wD], [b_xo[s]])
                    K.dma("sp", x_dst[i * P:(i + 1) * P, :], xo[s][:], b_xo[s], [b_xo[s]], (), [b_xdst])

                e_load(0)
                for i in range(NT + 1):
                    if i < NT:
                        if i + 1 < NT:
                            e_load(i + 1)
                        x_load(i)
                        d1(i)
                    if i >= 1:
                        d2(i - 1)
                K.barrier()
            b_xsrc = b_xdst
            x_cur = x_dst
        K.barrier(engines=("sp",))
        print("semaphores used:", K.nsem, "instr counts:", {n: E.cnt for n, E in K.E.items()})
    return nc


def _rot_cols(w, nheads):
    n = w.shape[-1]
    idx = np.arange(n).reshape(nheads, 2, 32)[:, ::-1, :].reshape(-1)
    return w[..., idx]


def prep_shared(inputs):
    w_in = np.ascontiguousarray(inputs["w_in"], dtype=np.float32)
    w_rot = np.concatenate([
        _rot_cols(w_in[:, :, OFF["dq"]:OFF["dq"] + 512], 8),
        _rot_cols(w_in[:, :, OFF["diq"]:OFF["diq"] + 512], 8),
        _rot_cols(w_in[:, :, OFF["dik"]:OFF["dik"] + 64], 1)], axis=-1)
    w_kv = np.ascontiguousarray(inputs["w_kv_up"], dtype=np.float32)
    half = 32
    j = np.arange(P)
    invf = (np.float32(10000.0) ** (-(np.arange(half, dtype=np.float32)) / np.float32(half))).astype(np.float32)
    sgn = np.where((j % 64) < 32, -1.0, 1.0).astype(np.float32)
    invf_t = np.stack([invf[j % 32], sgn * invf[j % 32]], axis=1).astype(np.float32)
    return {
        "invf": np.ascontiguousarray(invf_t),
        "w_ada": np.ascontiguousarray(inputs["w_ada"], dtype=np.float32),
        "b_ada": np.ascontiguousarray(inputs["b_ada"], dtype=np.float32),
        "g_norm": np.ascontiguousarray(inputs["g_norm"], dtype=np.float32),
        "w_in": w_in,
        "w_rot": np.ascontiguousarray(w_rot),
        "b_fgt": np.ascontiguousarray(inputs["b_fgt"], dtype=np.float32).reshape(DEPTH, 8, 1),
        "g_kv": np.ascontiguousarray(inputs["g_kv"], dtype=np.float32).reshape(DEPTH, P, 1),
        "w_kv": w_kv,
        "w_kvr": np.ascontiguousarray(_rot_cols(w_kv[:, :, 0:64], 1)),
        "w_br_fox": np.ascontiguousarray(inputs["w_br_fox"], dtype=np.float32),
        "w_br_dsa": np.ascontiguousarray(inputs["w_br_dsa"], dtype=np.float32),
        "w_br_sb": np.ascontiguousarray(inputs["w_br_sb"], dtype=np.float32),
        "w_out": np.ascontiguousarray(inputs["w_out"], dtype=np.float32),
        "g_final": np.ascontiguousarray(inputs["g_final"], dtype=np.float32).reshape(1, D),
    }


def prep_core(inputs, b):
    c = np.asarray(inputs["c"][b], dtype=np.float32)
    crep = np.ascontiguousarray(np.broadcast_to(c.reshape(8, P).T[:, :, None], (P, 8, P)))
    return {
        "x": np.ascontiguousarray(inputs["x"][b], dtype=np.float32),
        "crep": crep,
        "pos": np.ascontiguousarray(inputs["positions"][b], dtype=np.int32).reshape(1, S),
    }


def kernel(**inputs):
    shared = prep_shared(inputs)
    nb = inputs["x"].shape[0]
    in_maps = []
    for b in range(nb):
        m = dict(shared)
        m.update(prep_core(inputs, b))
        in_maps.append(m)
    nc = build_program()
    res = run_bass_kernel_spmd(nc, in_maps, core_ids=list(range(nb)))
    return np.stack([np.asarray(r["out"], dtype=np.float32) for r in res.results], axis=0)
```

```python
import numpy as np
from contextlib import ExitStack
import concourse.bass as bass
import concourse.mybir as mybir
from concourse.bass_utils import run_bass_kernel_spmd

F32 = mybir.dt.float32
BF16 = mybir.dt.bfloat16
FP16 = mybir.dt.float16
I32 = mybir.dt.int32
AF = mybir.ActivationFunctionType
ALU = mybir.AluOpType

S = 4096
D = 1024
NT = 32
P = 128
DEPTH = 2
N_IN = 8912
OFF = dict(fq=0, fk=512, fv=1024, ff=1536, fg=1544, dq=2056, dckv=2568, diq=2696, dik=3208,
           diw=3272, dg=3280, sq=3792, sk=4304, sv=4816, sg=5328, merge=5840)
NEG = -30000.0
NIT = 14
import os
DSA_STAGES = int(os.environ.get('DSA_STAGES', '3'))
EPS = 1e-6
TWO_PI = 2.0 * np.pi
C1 = 6.28125
C2 = float(TWO_PI - 6.28125)


class Buf:
    __slots__ = ("name", "w", "r", "sem", "cnt", "loose", "q")

    def __init__(self, name, loose=False):
        self.name = name
        self.loose = loose
        self.w = {}
        self.r = {}
        self.sem = None
        self.cnt = 0


class Eng:
    def __init__(self, name, eng, sem, self_sync):
        self.name = name
        self.eng = eng
        self.sem = sem
        self.cnt = 0
        self.waited = {}
        self.self_sync = self_sync


class KB:
    def __init__(self, nc, es):
        self.nc = nc
        self.es = es
        self.nsem = 0
        self.E = {}
        for n, e, ss in [("pe", nc.tensor, False), ("act", nc.scalar, True), ("dve", nc.vector, True),
                         ("pool", nc.gpsimd, True), ("sp", nc.sync, True)]:
            self.E[n] = Eng(n, e, self.new_sem("e_" + n), ss)
        self.slots = {}
        self.all_slots = []
        self.free_sems = {"sp": [], "pool": [], "act": []}

    def new_sem(self, name):
        self.nsem += 1
        return self.es.enter_context(self.nc.semaphore("%s_%d" % (name, self.nsem)))

    def _wait(self, E, deps):
        for sid, (sem, val) in deps.items():
            slot = self.slots.get(sid)
            if slot is not None:
                val = slot.cnt
            elif sem is E.sem and not E.self_sync:
                continue
            if E.waited.get(sid, 0) >= val:
                continue
            E.eng.wait_ge(sem, val)
            E.waited[sid] = val

    @staticmethod
    def _merge(d, src):
        for sid, ev in src.items():
            o = d.get(sid)
            if o is None or o[1] < ev[1]:
                d[sid] = ev

    def _deps(self, reads, writes, add, own_sid=None):
        deps = {}
        for b in reads:
            self._merge(deps, b.w)
        for b in writes:
            self._merge(deps, b.w)
            self._merge(deps, b.r)
        for b in add:
            self._merge(deps, b.r)
            if not b.loose:
                for sid, ev in b.w.items():
                    if sid != own_sid:
                        self._merge(deps, {sid: ev})
        return deps

    def _commit(self, ev, reads, writes, add):
        sid = id(ev[0])
        for b in writes:
            b.w = {sid: ev}
            b.r = {}
        for b in add:
            b.w[sid] = ev
        for b in reads:
            b.r[sid] = ev

    def op(self, e, fn, reads=(), writes=(), add=()):
        E = self.E[e]
        self._wait(E, self._deps(reads, writes, add, id(E.sem)))
        inst = fn(E.eng)
        E.cnt += 1
        inst.then_inc(E.sem, 1)
        self._commit((E.sem, E.cnt), reads, writes, add)

    def dma(self, q, out, in_, slot, reads=(), writes=(), add=()):
        E = self.E[q]
        if slot.sem is None:
            if self.free_sems[q]:
                slot.sem, slot.cnt = self.free_sems[q].pop()
            else:
                slot.sem, slot.cnt = self.new_sem("d" + q), 0
            slot.q = q
            self.slots[id(slot.sem)] = slot
            self.all_slots.append(slot)
        assert slot.q == q, "DMA slot %s used from two queues" % slot.name
        self._wait(E, self._deps(reads, writes, add, id(slot.sem)))
        inst = E.eng.dma_start(out=out, in_=in_)
        slot.cnt += 16
        inst.then_inc(slot.sem, 16)
        self._commit((slot.sem, slot.cnt), reads, writes, add)

    def barrier(self, engines=("pe", "act", "dve", "pool", "sp")):
        deps = {}
        for n, X in self.E.items():
            if X.cnt > 0:
                deps[id(X.sem)] = (X.sem, X.cnt)
        for s in self.all_slots:
            if s.cnt > 0:
                deps[id(s.sem)] = (s.sem, s.cnt)
        for n in engines:
            E = self.E[n]
            ss = E.self_sync
            E.self_sync = True
            self._wait(E, deps)
            E.self_sync = ss
        if len(engines) == 5:
            for sl in self.all_slots:
                self.free_sems[sl.q].append((sl.sem, sl.cnt))
                del self.slots[id(sl.sem)]
                sl.sem = None
            self.all_slots = []


def act_kwargs(**kw):
    return {k: v for k, v in kw.items() if v is not None}


def build_program(nlayers=DEPTH, debug=False, stop_after=None):
    nc = bass.Bass("TRN2", target_bir_lowering=False)
    dt_in = lambda name, shape, dt=F32: nc.dram_tensor(name, list(shape), dt, kind="ExternalInput").ap()
    skind = "ExternalOutput" if debug else "Internal"
    dt_sc = lambda name, shape, dt=BF16: nc.dram_tensor(name, list(shape), dt, kind=skind).ap()

    x_in = dt_in("x", [S, D])
    crep_in = dt_in("crep", [P, 8, P])
    pos_in = dt_in("pos", [1, S], I32)
    invf_in = dt_in("invf", [P, 2])
    w_ada = dt_in("w_ada", [DEPTH, D, 3 * D])
    b_ada = dt_in("b_ada", [DEPTH, 3 * D])
    g_norm = dt_in("g_norm", [DEPTH, D])
    w_in = dt_in("w_in", [DEPTH, D, N_IN])
    w_rot = dt_in("w_rot", [DEPTH, D, 1088])
    b_fgt = dt_in("b_fgt", [DEPTH, 8, 1])
    g_kv = dt_in("g_kv", [DEPTH, P, 1])
    w_kv = dt_in("w_kv", [DEPTH, P, P])
    w_kvr = dt_in("w_kvr", [DEPTH, P, 64])
    w_brs = [dt_in("w_br_fox", [DEPTH, 512, D]), dt_in("w_br_dsa", [DEPTH, 512, D]), dt_in("w_br_sb", [DEPTH, 512, D])]
    w_out = dt_in("w_out", [DEPTH, D, D])
    g_final = dt_in("g_final", [1, D])
    out_d = nc.dram_tensor("out", [S, D], F32, kind="ExternalOutput").ap()

    XR = dt_sc("XR", [S, D], F32)
    QF = dt_sc("QF", [8, 68, S]); KF = dt_sc("KF", [8, 68, S]); VF = dt_sc("VF", [S, 8, 65]); GF = dt_sc("GF", [S, 512])
    QD = dt_sc("QD", [64, 8, S]); KD = dt_sc("KD", [64, S]); VD = dt_sc("VD", [S, 65]); GD = dt_sc("GD", [S, 512])
    IQ = dt_sc("IQ", [64, 8, S], FP16); IK = dt_sc("IK", [64, S], FP16)
    QS = dt_sc("QS", [8, 64, S]); KS = dt_sc("KS", [8, 64, S]); VS = dt_sc("VS", [S, 512]); GS = dt_sc("GS", [S, 512])
    MG = dt_sc("MG", [S, 3 * D])
    YF = dt_sc("YF", [S, 512]); YD = dt_sc("YD", [S, 512]); YS = dt_sc("YS", [S, 512])
    dbg = {}
    if debug:
        dbg["WAB"] = dt_sc("WAB", [P, NT, 8], F32)
        dbg["WSG"] = dt_sc("WSG", [P, NT, 8], F32)
        dbg["HT"] = dt_sc("HT", [P, 8, S], BF16)
        dbg["TAB"] = dt_sc("TAB", [P, 2, S], F32)
        dbg["SC"] = dt_sc("SC", [S, S], F32)
        dbg["TH"] = dt_sc("TH", [S, 1], F32)

    es = ExitStack()
    with es:
        K = KB(nc, es)
        uid = [0]

        def sbt(ctx, n, s, d):
            uid[0] += 1
            return ctx.enter_context(nc.sbuf_tensor("%s_%d" % (n, uid[0]), list(s), d))

        psum = es.enter_context(nc.psum_tensor("psum", [P, 8, 512], F32))
        PB = [Buf("pb%d" % i) for i in range(8)]
        ident = sbt(es, "ident", [P, P], BF16); b_ident = Buf("ident")
        tri_le = sbt(es, "tri_le", [P, P], BF16)
        tri_lt = sbt(es, "tri_lt", [P, P], BF16)
        negU = sbt(es, "negU", [P, P], BF16)
        negones = sbt(es, "negones", [P, P], BF16)
        ones_bf = sbt(es, "ones_bf", [P, P], BF16)
        caus = sbt(es, "caus", [P, P], F32)
        pow2 = sbt(es, "pow2", [P, NIT], F32)
        invf = sbt(es, "invf_sb", [P, 2], F32)
        CT = sbt(es, "CT", [P, S], F32)
        SS = sbt(es, "SS", [P, S], F32)
        b_const = Buf("const")
        b_tab = Buf("tab")
        A_rep = sbt(es, "A_rep", [P, D], F32); shift_rep = sbt(es, "shift_rep", [P, D], F32)
        gate_rep = sbt(es, "gate_rep", [P, D], F32)
        b_mod = Buf("mod")
        wab = sbt(es, "wab", [P, NT, 8], F32); wsg = sbt(es, "wsg", [P, NT, 8], F32)
        b_wab = Buf("wab")

        def pool_op(fn, writes):
            K.op("pool", fn, (), writes)
        pool_op(lambda e: e.memset(ident[:], 1.0), [b_const])
        pool_op(lambda e: e.affine_select(out=ident[:], in_=ident[:], pattern=[[1, P]], compare_op=ALU.is_equal,
                                          fill=0.0, base=0, channel_multiplier=-1), [b_const])
        pool_op(lambda e: e.memset(tri_le[:], 0.0), [b_const])
        pool_op(lambda e: e.affine_select(out=tri_le[:], in_=tri_le[:], pattern=[[1, P]], compare_op=ALU.is_ge,
                                          fill=NEG, base=0, channel_multiplier=-1), [b_const])
        pool_op(lambda e: e.memset(tri_lt[:], 0.0), [b_const])
        pool_op(lambda e: e.affine_select(out=tri_lt[:], in_=tri_lt[:], pattern=[[1, P]], compare_op=ALU.is_ge,
                                          fill=NEG, base=-1, channel_multiplier=-1), [b_const])
        pool_op(lambda e: e.memset(negU[:], -1.0), [b_const])
        pool_op(lambda e: e.affine_select(out=negU[:], in_=negU[:], pattern=[[-1, P]], compare_op=ALU.is_ge,
                                          fill=0.0, base=0, channel_multiplier=1), [b_const])
        pool_op(lambda e: e.memset(negones[:], -1.0), [b_const])
        pool_op(lambda e: e.memset(ones_bf[:], 1.0), [b_const])
        pool_op(lambda e: e.memset(caus[:], 0.0), [b_const])
        pool_op(lambda e: e.affine_select(out=caus[:], in_=caus[:], pattern=[[-1, P]], compare_op=ALU.is_ge,
                                          fill=-1e30, base=0, channel_multiplier=1), [b_const])
        for k in range(NIT):
            pool_op(lambda e, k=k: e.memset(pow2[:, k:k + 1], float(2.0 ** -(k + 1))), [b_const])
        s_misc = Buf("misc")
        K.dma("sp", invf[:], invf_in[:, :], s_misc, (), [b_const])

        with ExitStack() as cs:
            pi_t = sbt(cs, "pi_t", [P, 512], I32); pf_t = sbt(cs, "pf_t", [P, 512], F32)
            a_t = sbt(cs, "a_t", [P, 512], F32); k_t = sbt(cs, "k_t", [P, 512], F32)
            ki_t = sbt(cs, "ki_t", [P, 512], I32); r_t = sbt(cs, "r_t", [P, 512], F32)
            b_pi = Buf("pi"); b_pf = Buf("pf"); b_a = Buf("a"); b_k = Buf("k"); b_ki = Buf("ki"); b_r = Buf("r")
            for tcn in range(8):
                sl = slice(tcn * 512, (tcn + 1) * 512)
                K.dma("sp", pi_t[:], pos_in[0:1, sl].to_broadcast([P, 512]), b_pi, (), [b_pi])
                K.op("dve", lambda e: e.tensor_copy(out=pf_t[:], in_=pi_t[:]), [b_pi], [b_pf])
                for which in range(2):
                    tab = CT if which == 0 else SS
                    if which == 0:
                        K.op("dve", lambda e: e.tensor_scalar(out=a_t[:], in0=pf_t[:], scalar1=invf[:, 0:1], scalar2=float(np.pi / 2),
                                                              op0=ALU.mult, op1=ALU.add), [b_pf, b_const], [b_a])
                    else:
                        K.op("dve", lambda e: e.tensor_scalar(out=a_t[:], in0=pf_t[:], scalar1=invf[:, 1:2], scalar2=None,
                                                              op0=ALU.mult), [b_pf, b_const], [b_a])
                    K.op("dve", lambda e: e.tensor_scalar(out=k_t[:], in0=a_t[:], scalar1=float(1.0 / TWO_PI), scalar2=None,
                                                          op0=ALU.mult), [b_a], [b_k])
                    K.op("dve", lambda e: e.tensor_copy(out=ki_t[:], in_=k_t[:]), [b_k], [b_ki])
                    K.op("dve", lambda e: e.tensor_copy(out=k_t[:], in_=ki_t[:]), [b_ki], [b_k])
                    K.op("dve", lambda e: e.scalar_tensor_tensor(out=r_t[:], in0=k_t[:], scalar=-C1, in1=a_t[:],
                                                                 op0=ALU.mult, op1=ALU.add), [b_k, b_a], [b_r])
                    K.op("dve", lambda e: e.scalar_tensor_tensor(out=a_t[:], in0=k_t[:], scalar=-C2, in1=r_t[:],
                                                                 op0=ALU.mult, op1=ALU.add), [b_k, b_r], [b_a])
                    K.op("dve", lambda e: e.tensor_scalar(out=r_t[:], in0=a_t[:], scalar1=float(-np.pi), scalar2=float(np.pi),
                                                          op0=ALU.max, op1=ALU.min), [b_a], [b_r])
                    K.op("act", lambda e, tab=tab, sl=sl: e.activation(out=tab[:, sl], in_=r_t[:], func=AF.Sin), [b_r], (), [b_tab])
            if debug:
                K.dma("sp", dbg["TAB"][:, 0, :], CT[:], s_misc, [b_tab], ())
                K.dma("sp", dbg["TAB"][:, 1, :], SS[:], s_misc, [b_tab], ())
            K.barrier()

        x_cur = x_in
        b_xsrc = Buf("xsrc")
        for l in range(nlayers):
            last = (l == DEPTH - 1)
            x_dst = out_d if last else XR
            dB = {n: Buf("d_" + n, loose=True) for n in ["QF", "KF", "VF", "GF", "QD", "KD", "VD", "GD", "IQ", "IK", "QS", "KS", "VS", "GS",
                                             "MG", "YF", "YD", "YS"]}
            with ExitStack() as cs:
                hT = sbt(cs, "hT", [P, 8, S], BF16); b_hT = Buf("hT")
                NW = 3
                wbuf = [sbt(cs, "wbuf%d" % i, [P, 8, 512], BF16) for i in range(NW)]
                b_w = [Buf("wbuf%d" % i) for i in range(NW)]
                wrb = [sbt(cs, "wrb%d" % i, [P, 8, P], BF16) for i in range(NW)]
                b_wr = [Buf("wrb%d" % i) for i in range(NW)]
                wctr = [0]

                def load_w(src_ap, ncols, rot_ap=None):
                    i = wctr[0] % NW
                    wctr[0] += 1
                    K.dma("pool", wbuf[i][:, :, 0:ncols], src_ap.rearrange("(kc p) n -> p kc n", p=P), b_w[i], (), [b_w[i]])
                    if rot_ap is not None:
                        K.dma("pool", wrb[i][:, :, 0:ncols], rot_ap.rearrange("(kc p) n -> p kc n", p=P), b_wr[i], (), [b_wr[i]])
                    return i

                with ExitStack() as ms:
                    csb = sbt(ms, "csb", [P, 8, P], F32); scb = sbt(ms, "scb", [P, 8, P], BF16)
                    brep = [sbt(ms, "brep%d" % i, [P, 512], F32) for i in range(2)]
                    gnrep = sbt(ms, "gnrep", [P, D], F32)
                    mtmp = sbt(ms, "mtmp", [P, 512], F32)
                    b_c = Buf("csb"); b_sc = Buf("scb"); b_br = [Buf("brep0"), Buf("brep1")]; b_gn = Buf("gnrep"); b_mt = Buf("mtmp")
                    K.dma("sp", csb[:], crep_in[:, :, :], b_c, (), [b_c])
                    K.dma("sp", gnrep[:], g_norm[l:l + 1, :].to_broadcast([P, D]), b_gn, (), [b_gn])
                    K.op("act", lambda e: e.activation(out=scb[:], in_=csb[:], func=AF.Silu), [b_c], [b_sc])
                    for n in range(6):
                        wi = load_w(w_ada[l, :, n * 512:(n + 1) * 512], 512)
                        bi = n % 2
                        K.dma("sp", brep[bi][:], b_ada[l:l + 1, n * 512:(n + 1) * 512].to_broadcast([P, 512]), b_br[bi], (), [b_br[bi]])
                        bank = n % 2
                        for kc in range(8):
                            K.op("pe", lambda e, kc=kc, wi=wi, bank=bank: e.matmul(psum[:, bank, :], lhsT=scb[:, kc, :], rhs=wbuf[wi][:, kc, :],
                                                                                   start=(kc == 0), stop=(kc == 7), skip_group_check=True),
                                 [b_sc, b_w[wi]], [PB[bank]])
                        csl = slice((n % 2) * 512, (n % 2) * 512 + 512)
                        if n < 2:
                            K.op("dve", lambda e, bank=bank, bi=bi, csl=csl: e.tensor_tensor(out=shift_rep[:, csl], in0=psum[:, bank, :], in1=brep[bi][:], op=ALU.add),
                                 [PB[bank], b_br[bi]], (), [b_mod])
                        elif n < 4:
                            K.op("dve", lambda e, bank=bank, bi=bi: e.tensor_tensor(out=mtmp[:], in0=psum[:, bank, :], in1=brep[bi][:], op=ALU.add),
                                 [PB[bank], b_br[bi]], [b_mt])
                            K.op("dve", lambda e, csl=csl: e.scalar_tensor_tensor(out=A_rep[:, csl], in0=mtmp[:], scalar=1.0, in1=gnrep[:, csl],
                                                                                  op0=ALU.add, op1=ALU.mult), [b_mt, b_gn], (), [b_mod])
                        else:
                            K.op("dve", lambda e, bank=bank, bi=bi, csl=csl: e.tensor_tensor(out=gate_rep[:, csl], in0=psum[:, bank, :], in1=brep[bi][:], op=ALU.add),
                                 [PB[bank], b_br[bi]], (), [b_mod])
                    K.barrier()

                with ExitStack() as hs:
                    xt = [sbt(hs, "xt%d" % i, [P, D], F32) for i in range(2)]; b_xt = [Buf("xt0"), Buf("xt1")]
                    ht = [sbt(hs, "ht%d" % i, [P, D], BF16) for i in range(2)]; b_ht = [Buf("ht0"), Buf("ht1")]
                    junk = sbt(hs, "junkA", [P, D], F32); b_junk = Buf("junkA")
                    htmp = sbt(hs, "htmp", [P, D], F32); b_htmp = Buf("htmp")
                    st = sbt(hs, "statA", [P, 4], F32); b_st = Buf("statA")

                    def h_load(i):
                        K.dma("sp", xt[i % 2][:], x_cur[i * P:(i + 1) * P, :], b_xt[i % 2], [b_xsrc], [b_xt[i % 2]])

                    def hs1(i):
                        s = i % 2
                        K.op("act", lambda e: e.activation(out=junk[:], in_=xt[s][:], func=AF.Square, accum_out=st[:, 0:1]),
                             [b_xt[s]], [b_junk, b_st])
                        K.op("dve", lambda e: e.tensor_scalar(out=st[:, 1:2], in0=st[:, 0:1], scalar1=float(1.0 / D), scalar2=EPS,
                                                              op0=ALU.mult, op1=ALU.add), [b_st], [b_st])
                        K.op("act", lambda e: e.activation(out=st[:, 2:3], in_=st[:, 1:2], func=AF.Sqrt), [b_st], [b_st])
                        K.op("dve", lambda e: e.reciprocal(out=st[:, 3:4], in_=st[:, 2:3]), [b_st], [b_st])
                        K.op("dve", lambda e: e.scalar_tensor_tensor(out=htmp[:], in0=xt[s][:], scalar=st[:, 3:4], in1=A_rep[:],
                                                                     op0=ALU.mult, op1=ALU.mult), [b_xt[s], b_st, b_mod], [b_htmp])
                        K.op("pool", lambda e: e.tensor_tensor(out=ht[s][:], in0=htmp[:], in1=shift_rep[:], op=ALU.add),
                             [b_htmp, b_mod], [b_ht[s]])

                    def hs2(i):
                        s = i % 2
                        bank = 2 + (i % 2)
                        pbf = psum[:, bank, :].bitcast(BF16)
                        for kc in range(8):
                            K.op("pe", lambda e, kc=kc: e.transpose(out=pbf[:, kc * P:(kc + 1) * P], in_=ht[s][:, kc * P:(kc + 1) * P],
                                                                    identity=ident[:]), [b_ht[s], b_const], [PB[bank]])
                        K.op("act", lambda e: e.activation(out=hT[:, :, i * P:(i + 1) * P], in_=pbf.rearrange("p (a b) -> p a b", a=8),
                                                           func=AF.Copy), [PB[bank]], (), [b_hT])

                    h_load(0)
                    for i in range(NT + 1):
                        if i < NT:
                            if i + 1 < NT:
                                h_load(i + 1)
                            hs1(i)
                        if i >= 1:
                            hs2(i - 1)
                    if debug:
                        K.dma("sp", dbg["HT"][:, :, :], hT[:], s_misc, [b_hT], ())
                    K.barrier()

                with ExitStack() as ps_:
                    stg = [sbt(ps_, "stg%d" % i, [P, S], BF16) for i in range(2)]; b_stg = [Buf("stg0"), Buf("stg1")]
                    tstg = [sbt(ps_, "tstg%d" % i, [P, 520], BF16) for i in range(3)]; b_tstg = [Buf("tstg%d" % i) for i in range(3)]
                    t1 = [sbt(ps_, "t1_%d" % i, [P, 512], F32) for i in range(2)]; b_t1 = [Buf("t1_0"), Buf("t1_1")]
                    t2 = [sbt(ps_, "t2_%d" % i, [P, 512], F32) for i in range(2)]; b_t2 = [Buf("t2_0"), Buf("t2_1")]
                    ckvT = sbt(ps_, "ckvT", [P, 512], BF16); b_ckv = Buf("ckvT")
                    sqb = sbt(ps_, "sqb", [P, 512], BF16); b_sqb = Buf("sqb")
                    rstd = sbt(ps_, "rstd", [P, 512], F32); b_rstd = Buf("rstd")
                    dvT = sbt(ps_, "dvT", [P, 512], BF16); b_dvT = Buf("dvT")
                    wkv_f = sbt(ps_, "wkv_f", [P, P], F32); wkvr_f = sbt(ps_, "wkvr_f", [P, 64], F32)
                    wkv_b = sbt(ps_, "wkv_b", [P, P], BF16); wkvr_b = sbt(ps_, "wkvr_b", [P, 64], BF16)
                    gkv = sbt(ps_, "gkv", [P, 1], F32); b_wkv = Buf("wkv")
                    bf = sbt(ps_, "bfg", [8, 1], F32); b_bf = Buf("bfg")
                    etmp = sbt(ps_, "etmpB", [8, 512], F32); b_etmp = Buf("etmpB")
                    spc = sbt(ps_, "spc", [8, 512], F32); b_spc = Buf("spc")
                    fnc = [sbt(ps_, "fnc%d" % i, [8, 512], F32) for i in range(2)]; b_fnc = [Buf("fnc0"), Buf("fnc1")]
                    fr1 = sbt(ps_, "fr1", [8, 512], F32); b_fr1 = Buf("fr1")
                    fr2 = sbt(ps_, "fr2", [8, 512], F32); b_fr2 = Buf("fr2")
                    fbs = [sbt(ps_, "fbs%d" % i, [8, 512], BF16) for i in range(4)]; b_fbs = [Buf("fbs%d" % i) for i in range(4)]
                    for i in range(3):
                        K.op("pool", lambda e, i=i: e.memset(tstg[i][:], 1.0), (), [b_tstg[i]])
                    K.op("pool", lambda e: e.memset(stg[0][0:8, :], 1.0), (), [b_stg[0]])
                    for r_ in range(3):
                        K.dma("sp", QF[:, 65 + r_, :], stg[0][0:8, :], b_stg[0], [b_stg[0]], (), [dB["QF"]])
                    K.dma("sp", KF[:, 64, :], stg[0][0:8, :], b_stg[0], [b_stg[0]], (), [dB["KF"]])
                    K.dma("sp", bf[:], b_fgt[l, :, :], b_bf, (), [b_bf])
                    K.op("dve", lambda e: e.tensor_scalar(out=bf[:], in0=bf[:], scalar1=-1.0, scalar2=None, op0=ALU.mult), [b_bf], [b_bf])
                    K.dma("sp", wkv_f[:], w_kv[l, :, :], b_wkv, (), [b_wkv])
                    K.dma("sp", wkvr_f[:], w_kvr[l, :, :], b_wkv, (), [b_wkv])
                    K.dma("sp", gkv[:], g_kv[l, :, :], b_wkv, (), [b_wkv])
                    K.op("dve", lambda e: e.tensor_scalar(out=wkv_b[:], in0=wkv_f[:], scalar1=gkv[:, 0:1], scalar2=None, op0=ALU.mult), [b_wkv], [b_wkv])
                    K.op("dve", lambda e: e.tensor_scalar(out=wkvr_b[:], in0=wkvr_f[:], scalar1=gkv[:, 0:1], scalar2=None, op0=ALU.mult), [b_wkv], [b_wkv])
                    bctr = [0]

                    def next_bank():
                        b = bctr[0] % 4
                        bctr[0] += 1
                        return b

                    sctr = [0]
                    jobs = []

                    def fm_job(col0, M, rotcol0, post, dst_fn):
                        rot_ap = w_rot[l, :, rotcol0:rotcol0 + M] if rotcol0 is not None else None
                        box = {}

                        def load():
                            box["wi"] = load_w(w_in[l, :, col0:col0 + M], M, rot_ap)

                        def run():
                            wi = box["wi"]
                            si = sctr[0] % 2
                            sctr[0] += 1
                            for tcn in range(8):
                                sl = slice(tcn * 512, (tcn + 1) * 512)
                                bank = next_bank()
                                for kc in range(8):
                                    K.op("pe", lambda e, kc=kc, bank=bank, sl=sl: e.matmul(psum[0:M, bank, :], lhsT=wbuf[wi][:, kc, 0:M], rhs=hT[:, kc, sl],
                                                                                         start=(kc == 0), stop=(kc == 7), skip_group_check=True),
                                         [b_w[wi], b_hT], [PB[bank]])
                                bank2 = None
                                if rot_ap is not None:
                                    bank2 = next_bank()
                                    for kc in range(8):
                                        K.op("pe", lambda e, kc=kc, bank2=bank2, sl=sl: e.matmul(psum[0:M, bank2, :], lhsT=wrb[wi][:, kc, 0:M], rhs=hT[:, kc, sl],
                                                                                               start=(kc == 0), stop=(kc == 7), skip_group_check=True),
                                             [b_wr[wi], b_hT], [PB[bank2]])
                                post(tcn, sl, bank, bank2, si)
                            if dst_fn is not None:
                                dst_fn(si)
                        jobs.append((load, run))

                    def post_scale(scale):
                        def f(tcn, sl, bank, bank2, si, M=P):
                            K.op("act", lambda e: e.activation(out=stg[si][0:M, sl], in_=psum[0:M, bank, :], func=AF.Copy, scale=float(scale)),
                                 [PB[bank]], (), [b_stg[si]])
                        return f

                    def post_rope(scale, M, odt=None):
                        def f(tcn, sl, bank, bank2, si):
                            j = tcn % 2
                            so = stg[si][:, :] if odt is None else stg[si][:, :].bitcast(odt)
                            K.op("dve", lambda e: e.scalar_tensor_tensor(out=t1[j][0:M, :], in0=psum[0:M, bank, :], scalar=float(scale), in1=CT[0:M, sl],
                                                                         op0=ALU.mult, op1=ALU.mult), [PB[bank], b_tab], [b_t1[j]])
                            K.op("dve", lambda e: e.scalar_tensor_tensor(out=t2[j][0:M, :], in0=psum[0:M, bank2, :], scalar=float(scale), in1=SS[0:M, sl],
                                                                         op0=ALU.mult, op1=ALU.mult), [PB[bank2], b_tab], [b_t2[j]])
                            K.op("pool", lambda e: e.tensor_tensor(out=so[0:M, sl], in0=t1[j][0:M, :], in1=t2[j][0:M, :], op=ALU.add),
                                 [b_t1[j], b_t2[j]], (), [b_stg[si]])
                        return f

                    def store_heads(dst, pair, dbuf):
                        def f(si):
                            for hh in range(2):
                                K.dma("sp", dst[2 * pair + hh, 0:64, :], stg[si][hh * 64:(hh + 1) * 64, :], b_stg[si], [b_stg[si]], (), [dbuf])
                        return f

                    def store_fm_heads(dst, pair, dbuf, odt=None):
                        def f(si):
                            so = stg[si][:, :] if odt is None else stg[si][:, :].bitcast(odt)
                            for hh in range(2):
                                K.dma("sp", dst[:, 2 * pair + hh, :], so[hh * 64:(hh + 1) * 64, :], b_stg[si], [b_stg[si]], (), [dbuf])
                        return f

                    def post_ff(tcn, sl, bank, bank2, si):
                        K.op("act", lambda e: e.activation(out=etmp[:], in_=psum[0:8, bank, :], func=AF.Exp, bias=bf[:, 0:1], scale=-1.0),
                             [PB[bank], b_bf], [b_etmp])
                        K.op("act", lambda e: e.activation(out=spc[:], in_=etmp[:], func=AF.Ln, bias=1.0), [b_etmp], [b_spc])
                        j = tcn % 2
                        init = 0.0 if tcn == 0 else fnc[1 - j][:, 511:512]
                        rd = [b_spc] if tcn == 0 else [b_spc, b_fnc[1 - j]]
                        K.op("dve", lambda e: e.tensor_tensor_scan(out=fnc[j][:], data0=spc[:], data1=spc[:], initial=init, op0=ALU.add, op1=ALU.max),
                             rd, [b_fnc[j]])
                        K.op("dve", lambda e: e.tensor_scalar(out=fbs[0][:], in0=fnc[j][:], scalar1=-1.0, scalar2=None, op0=ALU.mult), [b_fnc[j]], [b_fbs[0]])
                        K.op("dve", lambda e: e.tensor_copy(out=fbs[1][:], in_=fnc[j][:]), [b_fnc[j]], [b_fbs[1]])
                        K.op("dve", lambda e: e.tensor_tensor(out=fr1[:], in0=fnc[j][:], in1=fbs[1][:], op=ALU.subtract), [b_fnc[j], b_fbs[1]], [b_fr1])
                        K.op("dve", lambda e: e.tensor_copy(out=fbs[2][:], in_=fr1[:]), [b_fr1], [b_fbs[2]])
                        K.op("dve", lambda e: e.tensor_tensor(out=fr2[:], in0=fr1[:], in1=fbs[2][:], op=ALU.subtract), [b_fr1, b_fbs[2]], [b_fr2])
                        K.op("dve", lambda e: e.tensor_copy(out=fbs[3][:], in_=fr2[:]), [b_fr2], [b_fbs[3]])
                        K.dma("sp", QF[:, 64, sl], fbs[0][:], b_fbs[0], [b_fbs[0]], (), [dB["QF"]])
                        for r_ in range(3):
                            K.dma("sp", KF[:, 65 + r_, sl], fbs[1 + r_][:], b_fbs[1 + r_], [b_fbs[1 + r_]], (), [dB["KF"]])

                    vctr = [0]

                    def post_ckv(tcn, sl, bank, bank2, si):
                        K.op("act", lambda e: e.activation(out=ckvT[:], in_=psum[:, bank, :], func=AF.Copy), [PB[bank]], [b_ckv])
                        K.op("act", lambda e: e.activation(out=sqb[:], in_=psum[:, bank, :], func=AF.Square), [PB[bank]], [b_sqb])
                        K.op("pe", lambda e: e.matmul(psum[:, 4, :], lhsT=ones_bf[:], rhs=sqb[:], start=True, stop=True, skip_group_check=True),
                             [b_sqb, b_const], [PB[4]])
                        K.op("dve", lambda e: e.tensor_scalar(out=rstd[:], in0=psum[:, 4, :], scalar1=float(1.0 / 128), scalar2=EPS, op0=ALU.mult, op1=ALU.add),
                             [PB[4]], [b_rstd])
                        K.op("act", lambda e: e.activation(out=rstd[:], in_=rstd[:], func=AF.Sqrt), [b_rstd], [b_rstd])
                        K.op("dve", lambda e: e.reciprocal(out=rstd[:], in_=rstd[:]), [b_rstd], [b_rstd])
                        K.op("pe", lambda e: e.matmul(psum[:, 5, :], lhsT=wkv_b[:], rhs=ckvT[:], start=True, stop=True, skip_group_check=True),
                             [b_ckv, b_wkv], [PB[5]])
                        K.op("pe", lambda e: e.matmul(psum[0:64, 6, :], lhsT=wkvr_b[:], rhs=ckvT[:], start=True, stop=True, skip_group_check=True),
                             [b_ckv, b_wkv], [PB[6]])
                        j = tcn % 2
                        K.op("dve", lambda e: e.tensor_tensor(out=t1[j][0:64, :], in0=psum[0:64, 5, :], in1=CT[0:64, sl], op=ALU.mult), [PB[5], b_tab], [b_t1[j]])
                        K.op("dve", lambda e: e.tensor_tensor(out=t2[j][0:64, :], in0=psum[0:64, 6, :], in1=SS[0:64, sl], op=ALU.mult), [PB[6], b_tab], [b_t2[j]])
                        K.op("pool", lambda e: e.tensor_tensor(out=t1[j][0:64, :], in0=t1[j][0:64, :], in1=t2[j][0:64, :], op=ALU.add), [b_t2[j]], [b_t1[j]])
                        K.op("pool", lambda e: e.tensor_tensor(out=stg[si][0:64, sl], in0=t1[j][0:64, :], in1=rstd[0:64, :], op=ALU.mult),
                             [b_t1[j], b_rstd], (), [b_stg[si]])
                        K.op("dve", lambda e: e.tensor_tensor(out=dvT[64:128, :], in0=psum[64:128, 5, :], in1=rstd[64:128, :], op=ALU.mult),
                             [PB[5], b_rstd], [b_dvT])
                        pbf = psum[:, 7, :].bitcast(BF16)
                        for q in range(4):
                            K.op("pe", lambda e, q=q: e.transpose(out=pbf[:, q * 64:(q + 1) * 64], in_=dvT[64:128, q * P:(q + 1) * P], identity=ident[64:128, 64:128]),
                                 [b_dvT, b_const], [PB[7]])
                        for q in range(4):
                            vs = vctr[0] % 3
                            vctr[0] += 1
                            K.op("act", lambda e, q=q, vs=vs: e.activation(out=tstg[vs][:, 0:64], in_=pbf[:, q * 64:(q + 1) * 64], func=AF.Copy), [PB[7]], [b_tstg[vs]])
                            tok0 = tcn * 512 + q * P
                            K.dma("sp", VD[tok0:tok0 + P, :], tstg[vs][:, 0:65], b_tstg[vs], [b_tstg[vs]], (), [dB["VD"]])

                    tctr = [0]

                    def tm_job(col0, ncols, post):
                        box = {}

                        def load():
                            box["wi"] = load_w(w_in[l, :, col0:col0 + ncols], ncols)

                        def run():
                            wi = box["wi"]
                            for i in range(NT):
                                bank = next_bank()
                                for kc in range(8):
                                    K.op("pe", lambda e, kc=kc, bank=bank, i=i: e.matmul(psum[:, bank, 0:ncols], lhsT=hT[:, kc, i * P:(i + 1) * P], rhs=wbuf[wi][:, kc, 0:ncols],
                                                                                         start=(kc == 0), stop=(kc == 7), skip_group_check=True),
                                         [b_w[wi], b_hT], [PB[bank]])
                                ts = tctr[0] % 3
                                tctr[0] += 1
                                post(i, bank, ts)
                        jobs.append((load, run))

                    def post_tm_act(func, dst, dcol0, dbuf):
                        def f(i, bank, ts):
                            K.op("act", lambda e: e.activation(out=tstg[ts][:, 0:512], in_=psum[:, bank, :], func=func), [PB[bank]], [b_tstg[ts]])
                            K.dma("sp", dst[i * P:(i + 1) * P, dcol0:dcol0 + 512], tstg[ts][:, 0:512], b_tstg[ts], [b_tstg[ts]], (), [dbuf])
                        return f

                    def post_fv(i, bank, ts):
                        tv = tstg[ts][:, 0:520].rearrange("p (h e) -> p h e", h=8)
                        K.op("act", lambda e: e.activation(out=tv[:, :, 0:64], in_=psum[:, bank, :].rearrange("p (h e) -> p h e", h=8), func=AF.Copy),
                             [PB[bank]], [b_tstg[ts]])
                        K.dma("sp", VF[i * P:(i + 1) * P, :, :].rearrange("p h e -> p (h e)"), tstg[ts][:, 0:520], b_tstg[ts], [b_tstg[ts]], (), [dB["VF"]])

                    def post_diw(i, bank, ts):
                        cst = float((64 ** -0.5) * (8 ** -0.5))
                        K.op("act", lambda e: e.activation(out=wab[:, i, :], in_=psum[:, bank, 0:8], func=AF.Abs, scale=cst), [PB[bank]], (), [b_wab])
                        K.op("act", lambda e: e.activation(out=wsg[:, i, :], in_=psum[:, bank, 0:8], func=AF.Sign), [PB[bank]], (), [b_wab])

                    def remset():
                        for i in range(3):
                            K.op("pool", lambda e, i=i: e.memset(tstg[i][:], 1.0), (), [b_tstg[i]])

                    fm_job(OFF["dckv"], P, None, post_ckv,
                           lambda si: K.dma("sp", KD[:, :], stg[si][0:64, :], b_stg[si], [b_stg[si]], (), [dB["KD"]]))
                    fm_job(OFF["ff"], 8, None, post_ff, None)
                    for pr in range(4):
                        fm_job(OFF["fq"] + pr * P, P, None, post_scale(0.125), store_heads(QF, pr, dB["QF"]))
                    for pr in range(4):
                        fm_job(OFF["fk"] + pr * P, P, None, post_scale(1.0), store_heads(KF, pr, dB["KF"]))
                    for pr in range(4):
                        fm_job(OFF["sq"] + pr * P, P, None, post_scale(0.125), store_heads(QS, pr, dB["QS"]))
                    for pr in range(4):
                        fm_job(OFF["sk"] + pr * P, P, None, post_scale(1.0), store_heads(KS, pr, dB["KS"]))
                    for pr in range(4):
                        fm_job(OFF["dq"] + pr * P, P, pr * P, post_rope(0.125, P), store_fm_heads(QD, pr, dB["QD"]))
                    for pr in range(4):
                        fm_job(OFF["diq"] + pr * P, P, 512 + pr * P, post_rope(1.0, P, FP16), store_fm_heads(IQ, pr, dB["IQ"], FP16))
                    fm_job(OFF["dik"], 64, 1024, post_rope(1.0, 64, FP16),
                           lambda si: K.dma("sp", IK[:, :], stg[si][:, :].bitcast(FP16)[0:64, :], b_stg[si], [b_stg[si]], (), [dB["IK"]]))
                    jobs.append((lambda: None, remset))
                    tm_job(OFF["fv"], 512, post_fv)
                    tm_job(OFF["sv"], 512, post_tm_act(AF.Copy, VS, 0, dB["VS"]))
                    tm_job(OFF["fg"], 512, post_tm_act(AF.Silu, GF, 0, dB["GF"]))
                    tm_job(OFF["dg"], 512, post_tm_act(AF.Silu, GD, 0, dB["GD"]))
                    tm_job(OFF["sg"], 512, post_tm_act(AF.Silu, GS, 0, dB["GS"]))
                    for m in range(6):
                        tm_job(OFF["merge"] + m * 512, 512, post_tm_act(AF.Sigmoid, MG, m * 512, dB["MG"]))
                    tm_job(OFF["diw"], 8, post_diw)
                    jobs[0][0]()
                    jobs[1][0]()
                    for jn, (ld, rn) in enumerate(jobs):
                        if jn + 2 < len(jobs):
                            jobs[jn + 2][0]()
                        rn()
                    if debug:
                        K.dma("sp", dbg["WAB"][:, :, :], wab[:], s_misc, [b_wab], ())
                        K.dma("sp", dbg["WSG"][:, :, :], wsg[:], s_misc, [b_wab], ())
                    K.barrier()
            if stop_after == "B":
                break

            with ExitStack() as cs:
                qT = [sbt(cs, "qT%d" % i, [P, S], BF16) for i in range(2)]; b_qT = [Buf("qT0"), Buf("qT1")]
                kT = [sbt(cs, "kT%d" % i, [P, S], BF16) for i in range(2)]; b_kT = [Buf("kT0"), Buf("kT1")]
                Vt = [sbt(cs, "Vt%d" % i, [P, NT, 65], BF16) for i in range(2)]; b_V = [Buf("V0"), Buf("V1")]
                Gt = [sbt(cs, "Gt%d" % i, [P, NT, 64], BF16) for i in range(2)]; b_G = [Buf("G0"), Buf("G1")]
                ystg = [sbt(cs, "ystg%d" % i, [P, NT, 64], BF16) for i in range(2)]; b_ys = [Buf("ys0"), Buf("ys1")]
                pT = [sbt(cs, "pT%d" % i, [P, 1024], BF16) for i in range(3)]; b_pT = [Buf("pT%d" % i) for i in range(3)]
                spb = [sbt(cs, "spb%d" % i, [P, 512], BF16) for i in range(2)]; b_spb = [Buf("spb0"), Buf("spb1")]
                etm = [sbt(cs, "etm%d" % i, [P, 512], F32) for i in range(2)]; b_etm = [Buf("etm0"), Buf("etm1")]
                lsum = sbt(cs, "lsum", [P, 512], BF16); b_lsum = Buf("lsum")
                rden = sbt(cs, "rden", [P, 8], F32); b_rden = Buf("rden")

                def load_head(kind, h, s):
                    if kind == "fox":
                        K.dma("sp", qT[s][0:68, :], QF[h, :, :], b_qT[s], [dB["QF"]], [b_qT[s]])
                        K.dma("sp", kT[s][0:68, :], KF[h, :, :], b_kT[s], [dB["KF"]], [b_kT[s]])
                        K.dma("sp", Vt[s][:, :, :], VF.rearrange("(j p) h e -> p j h e", p=P)[:, :, h, :], b_V[s], [dB["VF"]], [b_V[s]])
                        K.dma("sp", Gt[s][:, :, :], GF.rearrange("(j p) (h e) -> p j h e", p=P, h=8)[:, :, h, :], b_G[s], [dB["GF"]], [b_G[s]])
                    else:
                        K.dma("sp", qT[s][0:64, :], QS[h, :, :], b_qT[s], [dB["QS"]], [b_qT[s]])
                        K.dma("sp", kT[s][0:64, :], KS[h, :, :], b_kT[s], [dB["KS"]], [b_kT[s]])
                        K.dma("sp", Vt[s][:, :, 0:64], VS.rearrange("(j p) (h e) -> p j h e", p=P, h=8)[:, :, h, :], b_V[s], [dB["VS"]], [b_V[s]])
                        K.dma("sp", Gt[s][:, :, :], GS.rearrange("(j p) (h e) -> p j h e", p=P, h=8)[:, :, h, :], b_G[s], [dB["GS"]], [b_G[s]])

                heads_seq = [("fox", h) for h in range(8)] + [("sb", h) for h in range(8)]
                for s_ in range(2):
                    K.op("pool", lambda e, s_=s_: e.memset(qT[s_][:], 0.0), (), [b_qT[s_]])
                    K.op("pool", lambda e, s_=s_: e.memset(kT[s_][:], 0.0), (), [b_kT[s_]])

                def load_head_idx(hi):
                    if hi >= len(heads_seq):
                        return
                    kind_, h_ = heads_seq[hi]
                    s_ = hi % 2
                    if kind_ == "sb" and h_ < 2:
                        K.op("pool", lambda e: e.memset(qT[s_][64:128, :], 0.0), (), [b_qT[s_]])
                        K.op("pool", lambda e: e.memset(kT[s_][64:128, :], 0.0), (), [b_kT[s_]])
                    load_head(kind_, h_, s_)

                steps = []
                cg = 0
                for hi, (kind, h) in enumerate(heads_seq):
                    for c in range(8):
                        Js = list(range(4 * c + 4))
                        if kind == "sb":
                            Js = Js[::-1]
                        for idx, J in enumerate(Js):
                            r = J - 4 * c
                            steps.append(dict(hi=hi, kind=kind, h=h, s=hi % 2, c=c, cg=cg, idx=idx, J=J, r=r, qlo=max(0, r) * P,
                                              first=(idx == 0), last=(idx == len(Js) - 1), last_head=(idx == len(Js) - 1 and c == 7),
                                              g=len(steps)))
                        cg += 1
                first_pv = {}
                pslot = {}
                KR = 128

                def st1(t):
                    kind, s, c, J, r, qlo = t["kind"], t["s"], t["c"], t["J"], t["r"], t["qlo"]
                    bank = t["g"] % 4
                    tri = tri_le if kind == "fox" else tri_lt
                    K.op("pe", lambda e: e.matmul(psum[:, bank, qlo:512], lhsT=kT[s][0:KR, J * P:(J + 1) * P], rhs=qT[s][0:KR, c * 512 + qlo:(c + 1) * 512],
                                                  start=True, stop=(r < 0), skip_group_check=True), [b_kT[s], b_qT[s]], [PB[bank]])
                    if r >= 0:
                        K.op("pe", lambda e: e.matmul(psum[:, bank, qlo:qlo + P], lhsT=ident[:], rhs=tri[:], start=False, stop=True, skip_group_check=True),
                             [b_const], (), [PB[bank]])
                    if kind == "sb":
                        j2 = t["g"] % 2
                        K.op("act", lambda e: e.activation(out=etm[j2][:, qlo:512], in_=psum[:, bank, qlo:512], func=AF.Exp), [PB[bank]], [b_etm[j2]])
                        K.op("act", lambda e: e.activation(out=spb[j2][:, qlo:512], in_=etm[j2][:, qlo:512], func=AF.Ln, bias=1.0), [b_etm[j2]], [b_spb[j2]])

                def st2(t):
                    kind, qlo = t["kind"], t["qlo"]
                    bank = t["g"] % 4
                    ps_ = t["g"] % 3
                    pslot[t["g"]] = ps_
                    if kind == "sb":
                        j2 = t["g"] % 2
                        if t["first"]:
                            K.op("dve", lambda e: e.memset(lsum[:], 0.0), (), [b_lsum])
                        K.op("pe", lambda e: e.matmul(psum[:, bank, qlo:512], lhsT=negU[:], rhs=spb[j2][:, qlo:512], start=False, stop=False, skip_group_check=True),
                             [b_spb[j2], b_const], (), [PB[bank]])
                        if not t["first"]:
                            K.op("pe", lambda e: e.matmul(psum[:, bank, qlo:512], lhsT=negones[:], rhs=lsum[:, qlo:512], start=False, stop=True, skip_group_check=True),
                                 [b_lsum, b_const], (), [PB[bank]])
                    K.op("act", lambda e: e.activation(out=pT[ps_][:, qlo:512], in_=psum[:, bank, qlo:512], func=AF.Exp), [PB[bank]], [b_pT[ps_]])
                    if kind == "sb":
                        K.op("dve", lambda e: e.tensor_tensor(out=lsum[:, qlo:512], in0=lsum[:, qlo:512], in1=spb[j2][:, qlo:512], op=ALU.add),
                             [b_spb[j2]], [b_lsum])

                def st3(t):
                    kind, s, c, J, r, h = t["kind"], t["s"], t["c"], t["J"], t["r"], t["h"]
                    VW = 65 if kind == "fox" else 64
                    OB = 6 + (t["cg"] % 2)
                    ob = psum[:, OB, 0:4 * 65].rearrange("p (a b) -> p a b", a=4)
                    ps_ = pslot.pop(t["g"])
                    for sub in range(max(0, r), 4):
                        st_ = t["cg"] not in first_pv
                        first_pv[t["cg"]] = True
                        K.op("pe", lambda e, sub=sub, st_=st_: e.matmul(ob[:, sub, 0:VW], lhsT=pT[ps_][:, sub * P:(sub + 1) * P], rhs=Vt[s][:, J, 0:VW],
                                                                        start=st_, stop=False, skip_group_check=True),
                             [b_pT[ps_], b_V[s]], (), [PB[OB]])
                    if t["last"]:
                        if kind == "fox":
                            K.op("dve", lambda e: e.reciprocal(out=rden[:, 0:4], in_=ob[:, :, 64]), [PB[OB]], [b_rden])
                            for sub in range(4):
                                K.op("dve", lambda e, sub=sub: e.scalar_tensor_tensor(out=ystg[s][:, 4 * c + sub, :], in0=ob[:, sub, 0:64], scalar=rden[:, sub:sub + 1],
                                                                                      in1=Gt[s][:, 4 * c + sub, :], op0=ALU.mult, op1=ALU.mult),
                                     [PB[OB], b_rden, b_G[s]], (), [b_ys[s]])
                        else:
                            for sub in range(4):
                                K.op("dve", lambda e, sub=sub: e.tensor_tensor(out=ystg[s][:, 4 * c + sub, :], in0=ob[:, sub, 0:64], in1=Gt[s][:, 4 * c + sub, :], op=ALU.mult),
                                     [PB[OB], b_G[s]], (), [b_ys[s]])
                    if t["last_head"]:
                        ydst = YF if kind == "fox" else YS
                        K.dma("sp", ydst.rearrange("(j p) (h e) -> p j h e", p=P, h=8)[:, :, h, :], ystg[s][:, :, :], b_ys[s], [b_ys[s]], (),
                              [dB["YF" if kind == "fox" else "YS"]])
                        load_head_idx(t["hi"] + 2)

                load_head_idx(0)
                load_head_idx(1)
                nst = len(steps)
                for k in range(nst + 2):
                    if k < nst:
                        st1(steps[k])
                    if 0 <= k - 1 < nst:
                        st2(steps[k - 1])
                    if 0 <= k - 2 < nst:
                        st3(steps[k - 2])
                K.barrier()
            if stop_after == "C1":
                break

            with ExitStack() as cs:
                kd = sbt(cs, "kd", [P, S], BF16); ikd = sbt(cs, "ikd", [P, S], FP16); vd = sbt(cs, "vd", [P, NT, 65], BF16)
                b_kd = Buf("kd"); b_ikd = Buf("ikd"); b_vd = Buf("vd")
                iqt = [sbt(cs, "iqt%d" % i, [P, 8, P], FP16) for i in range(2)]; b_iqt = [Buf("iqt0"), Buf("iqt1")]
                qdt = [sbt(cs, "qdt%d" % i, [P, 8, P], BF16) for i in range(2)]; b_qdt = [Buf("qdt0"), Buf("qdt1")]
                gdt = [sbt(cs, "gdt%d" % i, [P, 512], BF16) for i in range(2)]; b_gdt = [Buf("gdt0"), Buf("gdt1")]
                dg = [sbt(cs, "dg%d" % i, [P, 8, P], FP16) for i in range(2)]; b_dg = [Buf("dg0"), Buf("dg1")]
                ident16 = sbt(cs, "ident16", [P, P], FP16); caus16 = sbt(cs, "caus16", [P, P], FP16); b_c16 = Buf("c16")
                acc = [sbt(cs, "acc%d" % i, [P, S], F32) for i in range(2)]; b_acc = [Buf("acc0"), Buf("acc1")]
                NRT = 4
                rt = [sbt(cs, "rt%d" % i, [P, 512], FP16) for i in range(NRT)]; b_rt = [Buf("rt%d" % i) for i in range(NRT)]
                Mb = sbt(cs, "Mb", [P, S], BF16); b_M = Buf("Mb")
                MT = [sbt(cs, "MT%d" % i, [P, NT, P], BF16) for i in range(2)]; b_MT = [Buf("MT0"), Buf("MT1")]
                pTd = [sbt(cs, "pTd%d" % i, [P, 8, P], BF16) for i in range(3)]; b_pTd = [Buf("pTd%d" % i) for i in range(3)]
                sm = sbt(cs, "smD", [P, 8], F32); b_sm = Buf("smD")
                nW = sbt(cs, "nWD", [P, NIT], F32); hW = sbt(cs, "hWD", [P, NIT], F32); b_W = Buf("WD")
                rden = sbt(cs, "rdenD", [P, 8], F32); b_rden = Buf("rdenD")
                yd = [sbt(cs, "yd%d" % i, [P, 512], BF16) for i in range(2)]; b_yd = [Buf("yd0"), Buf("yd1")]
                K.op("pool", lambda e: e.tensor_copy(out=ident16[:], in_=ident[:]), [b_const], [b_c16])
                K.op("pool", lambda e: e.memset(caus16[:], 0.0), (), [b_c16])
                K.op("pool", lambda e: e.affine_select(out=caus16[:], in_=caus16[:], pattern=[[-1, P]], compare_op=ALU.is_ge,
                                                       fill=NEG, base=0, channel_multiplier=1), (), [b_c16])
                K.op("pool", lambda e: e.memset(kd[64:128, :], 0.0), (), [b_kd])
                K.op("pool", lambda e: e.memset(ikd[64:128, :], 0.0), (), [b_ikd])
                for s_ in range(2):
                    K.op("pool", lambda e, s_=s_: e.memset(iqt[s_][64:128, :, :], 0.0), (), [b_iqt[s_]])
                    K.op("pool", lambda e, s_=s_: e.memset(qdt[s_][64:128, :, :], 0.0), (), [b_qdt[s_]])
                K.dma("sp", kd[0:64, :], KD[:, :], b_kd, [dB["KD"]], [b_kd])
                K.dma("sp", ikd[0:64, :], IK[:, :], b_ikd, [dB["IK"]], [b_ikd])
                K.dma("sp", vd[:], VD.rearrange("(j p) e -> p j e", p=P), b_vd, [dB["VD"]], [b_vd])
                rctr = [0]
                pctr = [0]

                def build_dg(i):
                    s = i % 2
                    for h in range(8):
                        K.op("dve", lambda e, h=h: e.tensor_scalar(out=dg[s][:, h, :], in0=ident16[:], scalar1=wsg[:, i, h:h + 1], scalar2=None, op0=ALU.mult),
                             [b_c16, b_wab], (), [b_dg[s]])

                def stageA(i):
                    s = i % 2
                    K.dma("sp", iqt[s][0:64, :, :], IQ[:, :, i * P:(i + 1) * P], b_iqt[s], [dB["IQ"]], [b_iqt[s]])
                    if i == 0:
                        build_dg(0)
                    if i + 1 < NT:
                        build_dg(i + 1)
                    yield
                    L = (i + 1) * P
                    nkc = (L + 511) // 512
                    for kc in range(nkc):
                        w = min(512, L - kc * 512)
                        ksl = slice(kc * 512, kc * 512 + w)
                        SCB = 2
                        def qk(h):
                            K.op("pe", lambda e: e.matmul(psum[:, h % 2, 0:w], lhsT=iqt[s][:, h, :], rhs=ikd[:, ksl], start=True, stop=True,
                                                          skip_group_check=True), [b_iqt[s], b_ikd], [PB[h % 2]])
                        qk(0)
                        for h in range(8):
                            bank = h % 2
                            if h + 1 < 8:
                                qk(h + 1)
                            ri = rctr[0] % NRT
                            rctr[0] += 1
                            if h % 2 == 0:
                                K.op("act", lambda e, h=h, bank=bank, ri=ri: e.activation(out=rt[ri][:, 0:w], in_=psum[:, bank, 0:w], func=AF.Relu, scale=wab[:, i, h:h + 1]),
                                     [PB[bank], b_wab], [b_rt[ri]])
                            else:
                                K.op("dve", lambda e, h=h, bank=bank, ri=ri: e.tensor_scalar(out=rt[ri][:, 0:w], in0=psum[:, bank, 0:w], scalar1=wab[:, i, h:h + 1], scalar2=0.0,
                                                                                             op0=ALU.mult, op1=ALU.max), [PB[bank], b_wab], [b_rt[ri]])
                            if h == 0:
                                K.op("pe", lambda e, ri=ri: e.matmul(psum[:, SCB, 0:w], lhsT=dg[s][:, 0, :], rhs=rt[ri][:, 0:w], start=True, stop=False, skip_group_check=True),
                                     [b_dg[s], b_rt[ri]], [PB[SCB]])
                            else:
                                K.op("pe", lambda e, h=h, ri=ri: e.matmul(psum[:, SCB, 0:w], lhsT=dg[s][:, h, :], rhs=rt[ri][:, 0:w], start=False, stop=False, skip_group_check=True),
                                     [b_dg[s], b_rt[ri]], (), [PB[SCB]])
                            if h % 2 == 1:
                                yield
                        if kc == nkc - 1:
                            d0 = i * P - kc * 512
                            K.op("pe", lambda e: e.matmul(psum[:, SCB, d0:d0 + P], lhsT=ident16[:], rhs=caus16[:], start=False, stop=True, skip_group_check=True),
                                 [b_c16], (), [PB[SCB]])
                        K.op("act", lambda e: e.activation(out=acc[s][:, ksl], in_=psum[:, SCB, 0:w], func=AF.Identity), [PB[SCB]], (), [b_acc[s]])
                        yield

                def stageB(i):
                    s = i % 2
                    L = (i + 1) * P
                    K.dma("sp", qdt[s][0:64, :, :], QD[:, :, i * P:(i + 1) * P], b_qdt[s], [dB["QD"]], [b_qdt[s]])
                    K.dma("sp", gdt[s][:], GD[i * P:(i + 1) * P, :], b_gdt[s], [dB["GD"]], [b_gdt[s]])
                    if debug and "SC" in dbg:
                        K.dma("sp", dbg["SC"][i * P:(i + 1) * P, 0:L], acc[s][:, 0:L], s_misc, [b_acc[s]], ())
                    if i >= 2:
                        K.op("dve", lambda e: e.tensor_reduce(out=sm[:, 0:1], in_=acc[s][:, 0:i * P], axis=mybir.AxisListType.X, op=ALU.min), [b_acc[s]], [b_sm])
                        K.op("dve", lambda e: e.tensor_reduce(out=sm[:, 1:2], in_=acc[s][:, 0:L], axis=mybir.AxisListType.X, op=ALU.max), [b_acc[s]], [b_sm])
                        yield
                        K.op("dve", lambda e: e.tensor_tensor(out=sm[:, 2:3], in0=sm[:, 1:2], in1=sm[:, 0:1], op=ALU.subtract), [b_sm], [b_sm])
                        use_act = (i % 2 == 0)
                        if use_act:
                            K.op("dve", lambda e: e.tensor_scalar(out=nW[:], in0=pow2[:], scalar1=sm[:, 2:3], scalar2=-1.0, op0=ALU.mult, op1=ALU.mult), [b_sm, b_const], [b_W])
                            K.op("dve", lambda e: e.tensor_scalar(out=hW[:], in0=pow2[:], scalar1=sm[:, 2:3], scalar2=0.5, op0=ALU.mult, op1=ALU.mult), [b_sm, b_const], (), [b_W])
                            K.op("dve", lambda e: e.scalar_tensor_tensor(out=sm[:, 4:5], in0=sm[:, 0:1], scalar=-1.0, in1=nW[:, 0:1], op0=ALU.mult, op1=ALU.add), [b_W], [b_sm])
                            for k in range(NIT):
                                K.op("act", lambda e: e.activation(out=Mb[:, 0:L], in_=acc[s][:, 0:L], func=AF.Sign, bias=sm[:, 4:5], scale=1.0, accum_out=sm[:, 5:6]),
                                     [b_acc[s], b_sm], [b_M, b_sm])
                                K.op("dve", lambda e, k=k: e.scalar_tensor_tensor(out=sm[:, 6:7], in0=sm[:, 5:6], scalar=float(511 - L), in1=nW[:, k:k + 1], op0=ALU.is_ge, op1=ALU.mult),
                                     [b_W], [b_sm])
                                K.op("dve", lambda e, k=k: e.scalar_tensor_tensor(out=sm[:, 4:5], in0=sm[:, 6:7], scalar=hW[:, k:k + 1], in1=sm[:, 4:5], op0=ALU.add, op1=ALU.add),
                                     [b_W], [b_sm])
                                yield
                            K.op("dve", lambda e: e.scalar_tensor_tensor(out=sm[:, 3:4], in0=sm[:, 4:5], scalar=-1.0, in1=hW[:, NIT - 1:NIT], op0=ALU.mult, op1=ALU.subtract),
                                 [b_W], [b_sm])
                        else:
                            K.op("dve", lambda e: e.tensor_scalar(out=nW[:], in0=pow2[:], scalar1=sm[:, 2:3], scalar2=None, op0=ALU.mult), [b_sm, b_const], [b_W])
                            K.op("dve", lambda e: e.tensor_scalar(out=hW[:], in0=pow2[:], scalar1=sm[:, 2:3], scalar2=-0.5, op0=ALU.mult, op1=ALU.mult), [b_sm, b_const], (), [b_W])
                            K.op("dve", lambda e: e.tensor_tensor(out=sm[:, 4:5], in0=sm[:, 0:1], in1=nW[:, 0:1], op=ALU.add), [b_W], [b_sm])
                            for k in range(NIT):
                                K.op("dve", lambda e: e.tensor_scalar(out=Mb[:, 0:L], in0=acc[s][:, 0:L], scalar1=sm[:, 4:5], scalar2=None, op0=ALU.is_ge, op1=ALU.add,
                                                                      accum_out=sm[:, 5:6]), [b_acc[s], b_sm], [b_M, b_sm])
                                K.op("dve", lambda e, k=k: e.scalar_tensor_tensor(out=sm[:, 6:7], in0=sm[:, 5:6], scalar=255.5, in1=nW[:, k:k + 1], op0=ALU.is_ge, op1=ALU.mult),
                                     [b_W], [b_sm])
                                K.op("dve", lambda e, k=k: e.scalar_tensor_tensor(out=sm[:, 4:5], in0=sm[:, 6:7], scalar=hW[:, k:k + 1], in1=sm[:, 4:5], op0=ALU.add, op1=ALU.add),
                                     [b_W], [b_sm])
                                yield
                            K.op("dve", lambda e: e.tensor_tensor(out=sm[:, 3:4], in0=sm[:, 4:5], in1=hW[:, NIT - 1:NIT], op=ALU.add), [b_W], [b_sm])
                    else:
                        K.op("dve", lambda e: e.memset(sm[:, 3:4], -20000.0), (), [b_sm])
                    if debug and "TH" in dbg:
                        K.dma("sp", dbg["TH"][i * P:(i + 1) * P, :], sm[:, 3:4], s_misc, [b_sm], ())
                    K.op("dve", lambda e: e.tensor_scalar(out=Mb[:, 0:L], in0=acc[s][:, 0:L], scalar1=sm[:, 3:4], scalar2=None, op0=ALU.is_ge), [b_acc[s], b_sm], [b_M])
                    yield
                    for g in range((i + 8) // 8):
                        nj = min(8, i + 1 - g * 8)
                        bank = 3
                        pbf = psum[:, bank, :].bitcast(BF16)
                        for jj in range(nj):
                            J = g * 8 + jj
                            K.op("pe", lambda e, jj=jj, J=J, pbf=pbf: e.transpose(out=pbf[:, jj * P:(jj + 1) * P], in_=Mb[:, J * P:(J + 1) * P], identity=ident[:]),
                                 [b_M, b_const], [PB[bank]])
                        K.op("act", lambda e, g=g, nj=nj, pbf=pbf: e.activation(out=MT[s][:, g * 8:g * 8 + nj, :], in_=pbf[:, 0:nj * P].rearrange("p (a b) -> p a b", a=nj),
                                                                                func=AF.Copy), [PB[bank]], (), [b_MT[s]])
                        yield

                def stageC(i):
                    s = i % 2
                    oA = psum[:, 6, 0:260].rearrange("p (a b) -> p a b", a=4)
                    oB = psum[:, 7, 0:260].rearrange("p (a b) -> p a b", a=4)
                    firstA = [True, True]
                    pend = None
                    for J in range(i + 1):
                        for half in range(2):
                            K.op("pe", lambda e, half=half: e.matmul(psum[:, 4 + half, :].rearrange("p (a b) -> p a b", a=4), lhsT=kd[:, J * P:(J + 1) * P],
                                                                     rhs=qdt[s][:, half * 4:(half + 1) * 4, :], start=True, stop=True, skip_group_check=True),
                                 [b_kd, b_qdt[s]], [PB[4 + half]])
                        ps_ = pctr[0] % 3
                        pctr[0] += 1
                        K.op("act", lambda e, ps_=ps_: e.activation(out=pTd[ps_][:].rearrange("p a b -> p (a b)"), in_=psum[:, 4:6, :].rearrange("p a b -> p (a b)"), func=AF.Exp),
                             [PB[4], PB[5]], [b_pTd[ps_]])
                        K.op("dve", lambda e, ps_=ps_: e.tensor_tensor(out=pTd[ps_][:], in0=pTd[ps_][:], in1=MT[s][:, J:J + 1, :].to_broadcast([P, 8, P]), op=ALU.mult),
                             [b_MT[s]], [b_pTd[ps_]])
                        if pend is not None:
                            Jp, pp = pend
                            for h in range(8):
                                o_ = oA if h < 4 else oB
                                hb = 0 if h < 4 else 1
                                st_ = firstA[hb]
                                firstA[hb] = False
                                K.op("pe", lambda e, h=h, o_=o_, st_=st_, Jp=Jp, pp=pp: e.matmul(o_[:, h % 4, :], lhsT=pTd[pp][:, h, :], rhs=vd[:, Jp, :], start=st_, stop=False,
                                                                                                 skip_group_check=True), [b_pTd[pp], b_vd], (), [PB[6 + hb]])
                        pend = (J, ps_)
                        yield
                    Jp, pp = pend
                    for h in range(8):
                        o_ = oA if h < 4 else oB
                        hb = 0 if h < 4 else 1
                        st_ = firstA[hb]
                        firstA[hb] = False
                        K.op("pe", lambda e, h=h, o_=o_, st_=st_: e.matmul(o_[:, h % 4, :], lhsT=pTd[pp][:, h, :], rhs=vd[:, Jp, :], start=st_, stop=False,
                                                                           skip_group_check=True), [b_pTd[pp], b_vd], (), [PB[6 + hb]])
                    K.op("dve", lambda e: e.reciprocal(out=rden[:, 0:4], in_=oA[:, :, 64]), [PB[6]], [b_rden])
                    K.op("dve", lambda e: e.reciprocal(out=rden[:, 4:8], in_=oB[:, :, 64]), [PB[7]], [b_rden])
                    for h in range(8):
                        o_ = oA if h < 4 else oB
                        K.op("dve", lambda e, h=h, o_=o_: e.scalar_tensor_tensor(out=yd[s][:, h * 64:(h + 1) * 64], in0=o_[:, h % 4, 0:64], scalar=rden[:, h:h + 1],
                                                                                 in1=gdt[s][:, h * 64:(h + 1) * 64], op0=ALU.mult, op1=ALU.mult),
                             [PB[6 + (h // 4)], b_rden, b_gdt[s]], (), [b_yd[s]])
                    K.dma("sp", YD[i * P:(i + 1) * P, :], yd[s][:], b_yd[s], [b_yd[s]], (), [dB["YD"]])
                    yield

                for n in range(NT + 2):
                    gens = []
                    if n < NT:
                        gens.append(stageA(n))
                    if 0 <= n - 1 < NT and DSA_STAGES >= 2:
                        gens.append(stageB(n - 1))
                    if 0 <= n - 2 < NT and DSA_STAGES >= 3:
                        gens.append(stageC(n - 2))
                    while gens:
                        for g in list(gens):
                            try:
                                next(g)
                            except StopIteration:
                                gens.remove(g)
                K.barrier()
            if stop_after == "C2":
                break

            with ExitStack() as cs:
                wbr = [sbt(cs, "wbr%d" % b, [P, 4, D], BF16) for b in range(3)]
                wo = sbt(cs, "wo", [P, 8, D], BF16)
                b_wD = Buf("wD")
                gfin = sbt(cs, "gfin", [P, D], F32)
                yt = [sbt(cs, "yt%d" % i, [P, 3, 512], BF16) for i in range(2)]; b_yt = [Buf("yt0"), Buf("yt1")]
                mg = [sbt(cs, "mg%d" % i, [P, 3 * D], BF16) for i in range(2)]; b_mg = [Buf("mg0"), Buf("mg1")]
                xt = [sbt(cs, "xtD%d" % i, [P, D], F32) for i in range(2)]; b_xt = [Buf("xtD0"), Buf("xtD1")]
                yT = sbt(cs, "yT", [P, 12, P], BF16); b_yT = Buf("yT")
                mx = [sbt(cs, "mx%d" % b, [P, D], F32) for b in range(3)]; b_mx = [Buf("mx%d" % b) for b in range(3)]
                mxb = sbt(cs, "mxb", [P, D], BF16); b_mxb = Buf("mxb")
                mT = sbt(cs, "mT", [P, 8, P], BF16); b_mT = Buf("mT")
                xo = [sbt(cs, "xo%d" % i, [P, D], F32) for i in range(2)]; b_xo = [Buf("xo0"), Buf("xo1")]
                junk = sbt(cs, "junkE", [P, D], F32); b_junk = Buf("junkE")
                st = sbt(cs, "statD", [P, 4], F32); b_st = Buf("statD")
                b_wDs = [Buf("wD%d" % i) for i in range(5)]
                for b in range(3):
                    K.dma("pool", wbr[b][:], w_brs[b][l, :, :].rearrange("(kc p) n -> p kc n", p=P), b_wDs[b], (), (), [b_wD])
                K.dma("pool", wo[:], w_out[l, :, :].rearrange("(kc p) n -> p kc n", p=P), b_wDs[3], (), (), [b_wD])
                if last:
                    K.dma("sp", gfin[:], g_final[0:1, :].to_broadcast([P, D]), b_wDs[4], (), (), [b_wD])
                ysrc = [YF, YD, YS]
                ybuf = [dB["YF"], dB["YD"], dB["YS"]]
                b_xdst = Buf("xdst", loose=True)

                mxb2 = [mxb, sbt(cs, "mxb_b", [P, D], BF16)]; b_mxb2 = [b_mxb, Buf("mxb_b")]

                def e_load(i):
                    s = i % 2
                    for b in range(3):
                        K.dma("sp", yt[s][:, b, :], ysrc[b][i * P:(i + 1) * P, :], b_yt[s], [ybuf[b]], (), [b_yt[s]])
                    K.dma("sp", mg[s][:], MG[i * P:(i + 1) * P, :], b_mg[s], [dB["MG"]], [b_mg[s]])

                def x_load(i):
                    s = i % 2
                    K.dma("sp", xt[s][:], x_cur[i * P:(i + 1) * P, :], b_xt[s], [b_xsrc], [b_xt[s]])

                def d1(i):
                    s = i % 2
                    for g in range(2):
                        bank = g
                        pbf = psum[:, bank, :].bitcast(BF16)
                        for jj in range(6):
                            q = g * 6 + jj
                            b, kc = q // 4, q % 4
                            K.op("pe", lambda e, jj=jj, b=b, kc=kc, pbf=pbf: e.transpose(out=pbf[:, jj * P:(jj + 1) * P], in_=yt[s][:, b, kc * P:(kc + 1) * P], identity=ident[:]),
                                 [b_yt[s], b_const], [PB[bank]])
                        K.op("act", lambda e, g=g, pbf=pbf: e.activation(out=yT[:, g * 6:(g + 1) * 6, :], in_=pbf[:, 0:6 * P].rearrange("p (a b) -> p a b", a=6), func=AF.Copy),
                             [PB[bank]], (), [b_yT])
                    for b in range(3):
                        for half in range(2):
                            bank = 2 + half
                            for kc in range(4):
                                K.op("pe", lambda e, b=b, half=half, kc=kc, bank=bank: e.matmul(psum[:, bank, :], lhsT=yT[:, b * 4 + kc, :], rhs=wbr[b][:, kc, half * 512:(half + 1) * 512],
                                                                                                start=(kc == 0), stop=(kc == 3), skip_group_check=True),
                                     [b_yT, b_wD], [PB[bank]])
                            K.op("dve", lambda e, b=b, half=half, bank=bank: e.tensor_tensor(out=mx[b][:, half * 512:(half + 1) * 512], in0=psum[:, bank, :],
                                                                                             in1=mg[s][:, b * D + half * 512: b * D + (half + 1) * 512], op=ALU.mult),
                                 [PB[bank], b_mg[s]], (), [b_mx[b]])
                    K.op("pool", lambda e: e.tensor_tensor(out=mx[0][:], in0=mx[0][:], in1=mx[1][:], op=ALU.add), [b_mx[1]], [b_mx[0]])
                    K.op("pool", lambda e: e.tensor_tensor(out=mxb2[s][:], in0=mx[0][:], in1=mx[2][:], op=ALU.add), [b_mx[0], b_mx[2]], [b_mxb2[s]])

                def d2(i):
                    s = i % 2
                    bank = 4
                    pbf = psum[:, bank, :].bitcast(BF16)
                    for kc in range(8):
                        K.op("pe", lambda e, kc=kc, pbf=pbf: e.transpose(out=pbf[:, kc * P:(kc + 1) * P], in_=mxb2[s][:, kc * P:(kc + 1) * P], identity=ident[:]),
                             [b_mxb2[s], b_const], [PB[bank]])
                    K.op("act", lambda e, pbf=pbf: e.activation(out=mT[:], in_=pbf.rearrange("p (a b) -> p a b", a=8), func=AF.Copy), [PB[bank]], [b_mT])
                    for half in range(2):
                        bank = 5 + half
                        for kc in range(8):
                            K.op("pe", lambda e, half=half, kc=kc, bank=bank: e.matmul(psum[:, bank, :], lhsT=mT[:, kc, :], rhs=wo[:, kc, half * 512:(half + 1) * 512],
                                                                                       start=(kc == 0), stop=(kc == 7), skip_group_check=True), [b_mT, b_wD], [PB[bank]])
                        hs_ = slice(half * 512, (half + 1) * 512)
                        K.op("dve", lambda e, bank=bank, hs_=hs_: e.tensor_tensor(out=xo[s][:, hs_], in0=psum[:, bank, :], in1=gate_rep[:, hs_], op=ALU.mult),
                             [PB[bank], b_mod], (), [b_xo[s]])
                    K.op("pool", lambda e: e.tensor_tensor(out=xo[s][:], in0=xo[s][:], in1=xt[s][:], op=ALU.add), [b_xt[s]], [b_xo[s]])
                    if last:
                        K.op("act", lambda e: e.activation(out=junk[:], in_=xo[s][:], func=AF.Square, accum_out=st[:, 0:1]), [b_xo[s]], [b_junk, b_st])
                        K.op("dve", lambda e: e.tensor_scalar(out=st[:, 1:2], in0=st[:, 0:1], scalar1=float(1.0 / D), scalar2=EPS, op0=ALU.mult, op1=ALU.add), [b_st], [b_st])
                        K.op("act", lambda e: e.activation(out=st[:, 2:3], in_=st[:, 1:2], func=AF.Sqrt), [b_st], [b_st])
                        K.op("dve", lambda e: e.reciprocal(out=st[:, 3:4], in_=st[:, 2:3]), [b_st], [b_st])
                        K.op("dve", lambda e: e.scalar_tensor_tensor(out=xo[s][:], in0=xo[s][:], scalar=st[:, 3:4], in1=gfin[:], op0=ALU.mult, op1=ALU.mult),
                             [b_st, b_wD], [b_xo[s]])
                    K.dma("sp", x_dst[i * P:(i + 1) * P, :], xo[s][:], b_xo[s], [b_xo[s]], (), [b_xdst])

                e_load(0)
                for i in range(NT + 1):
                    if i < NT:
                        if i + 1 < NT:
                            e_load(i + 1)
                        x_load(i)
                        d1(i)
                    if i >= 1:
                        d2(i - 1)
                K.barrier()
            b_xsrc = b_xdst
            x_cur = x_dst
        K.barrier(engines=("sp",))
        print("semaphores used:", K.nsem, "instr counts:", {n: E.cnt for n, E in K.E.items()})
    return nc


def _rot_cols(w, nheads):
    n = w.shape[-1]
    idx = np.arange(n).reshape(nheads, 2, 32)[:, ::-1, :].reshape(-1)
    return w[..., idx]


def prep_shared(inputs):
    w_in = np.ascontiguousarray(inputs["w_in"], dtype=np.float32)
    w_rot = np.concatenate([
        _rot_cols(w_in[:, :, OFF["dq"]:OFF["dq"] + 512], 8),
        _rot_cols(w_in[:, :, OFF["diq"]:OFF["diq"] + 512], 8),
        _rot_cols(w_in[:, :, OFF["dik"]:OFF["dik"] + 64], 1)], axis=-1)
    w_kv = np.ascontiguousarray(inputs["w_kv_up"], dtype=np.float32)
    half = 32
    j = np.arange(P)
    invf = (np.float32(10000.0) ** (-(np.arange(half, dtype=np.float32)) / np.float32(half))).astype(np.float32)
    sgn = np.where((j % 64) < 32, -1.0, 1.0).astype(np.float32)
    invf_t = np.stack([invf[j % 32], sgn * invf[j % 32]], axis=1).astype(np.float32)
    return {
        "invf": np.ascontiguousarray(invf_t),
        "w_ada": np.ascontiguousarray(inputs["w_ada"], dtype=np.float32),
        "b_ada": np.ascontiguousarray(inputs["b_ada"], dtype=np.float32),
        "g_norm": np.ascontiguousarray(inputs["g_norm"], dtype=np.float32),
        "w_in": w_in,
        "w_rot": np.ascontiguousarray(w_rot),
        "b_fgt": np.ascontiguousarray(inputs["b_fgt"], dtype=np.float32).reshape(DEPTH, 8, 1),
        "g_kv": np.ascontiguousarray(inputs["g_kv"], dtype=np.float32).reshape(DEPTH, P, 1),
        "w_kv": w_kv,
        "w_kvr": np.ascontiguousarray(_rot_cols(w_kv[:, :, 0:64], 1)),
        "w_br_fox": np.ascontiguousarray(inputs["w_br_fox"], dtype=np.float32),
        "w_br_dsa": np.ascontiguousarray(inputs["w_br_dsa"], dtype=np.float32),
        "w_br_sb": np.ascontiguousarray(inputs["w_br_sb"], dtype=np.float32),
        "w_out": np.ascontiguousarray(inputs["w_out"], dtype=np.float32),
        "g_final": np.ascontiguousarray(inputs["g_final"], dtype=np.float32).reshape(1, D),
    }


def prep_core(inputs, b):
    c = np.asarray(inputs["c"][b], dtype=np.float32)
    crep = np.ascontiguousarray(np.broadcast_to(c.reshape(8, P).T[:, :, None], (P, 8, P)))
    return {
        "x": np.ascontiguousarray(inputs["x"][b], dtype=np.float32),
        "crep": crep,
        "pos": np.ascontiguousarray(inputs["positions"][b], dtype=np.int32).reshape(1, S),
    }


def kernel(**inputs):
    shared = prep_shared(inputs)
    nb = inputs["x"].shape[0]
    in_maps = []
    for b in range(nb):
        m = dict(shared)
        m.update(prep_core(inputs, b))
        in_maps.append(m)
    nc = build_program()
    res = run_bass_kernel_spmd(nc, in_maps, core_ids=list(range(nb)))
    return np.stack([np.asarray(r["out"], dtype=np.float32) for r in res.results], axis=0)
```
